# Optimizing a Trainium2 kernel written in Bass

```python
import math
import jax, jax.numpy as jnp
from jax import lax
import numpy as np

D_MODEL = 1024
BATCH = 8
SEQ = 4096
DEPTH = 4
DEC_BATCH = 8
DEC_SEQ = 64
PAST_LEN = 2048

CHUNK = 64
Q_BLOCK = 128
RMS_EPS = 1e-6
N_MIXERS = 3
N_POOL_LAYERS = (DEPTH + 2) // 3
N_DN_LAYERS = (DEPTH + 1) // 3
N_SB_LAYERS = DEPTH // 3
POOL_WINDOWS = (2, 4, 8, 16)
POOL_GROUPS = len(POOL_WINDOWS)
POOL_GC = D_MODEL // POOL_GROUPS
POOL_BUF = max(POOL_WINDOWS) - 1
DN_HEADS = 8
DN_DK = D_MODEL // DN_HEADS
DN_DV = D_MODEL // DN_HEADS
DN_KW = DN_HEADS * DN_DK
DN_VW = DN_HEADS * DN_DV
DN_QKV = 2 * DN_KW + DN_VW
DN_CONV = 4
SB_HEADS = 16
SB_DH = D_MODEL // SB_HEADS
D_FF = ((8 * D_MODEL // 3 + 127) // 128) * 128
FFN_CONV = 3

kernel_name = "hybrid_pool_gdn_stickbreak_stream_step"


def _rmsnorm(x, g):
    xf = x.astype(jnp.float32)
    y = xf * lax.rsqrt(jnp.mean(xf * xf, axis=-1, keepdims=True) + RMS_EPS)
    return (y * g.astype(jnp.float32)).astype(x.dtype)


def _l2norm(x):
    xf = x.astype(jnp.float32)
    return xf * lax.rsqrt(jnp.sum(xf * xf, axis=-1, keepdims=True) + RMS_EPS)


def _causal_dwconv(x, buf, w):
    width = w.shape[0]
    L = x.shape[1]
    xp = jnp.concatenate([buf.astype(x.dtype), x], axis=1)
    y = xp[:, 0:L] * w[0]
    for i in range(1, width):
        y = y + xp[:, i:i + L] * w[i]
    return y, xp[:, xp.shape[1] - (width - 1):]


def _pool_mixer(h, buf, w, scale, start):
    B, L, D = h.shape
    ext = jnp.concatenate([buf.astype(h.dtype), h], axis=1).astype(jnp.float32)
    cs = jnp.concatenate([jnp.zeros((B, 1, D), jnp.float32), jnp.cumsum(ext, axis=1)], axis=1)
    n_avail = (start + jnp.arange(L) + 1).astype(jnp.float32)
    hi = cs[:, POOL_BUF + 1:]
    means = []
    for g, win in enumerate(POOL_WINDOWS):
        sl = slice(g * POOL_GC, (g + 1) * POOL_GC)
        lo = cs[:, POOL_BUF + 1 - win:POOL_BUF + 1 - win + L, sl]
        cnt = jnp.minimum(n_avail, float(win))
        means.append((hi[..., sl] - lo) / cnt[None, :, None])
    d = jnp.concatenate(means, axis=-1) - ext[:, POOL_BUF:]
    y = jnp.einsum('blgc,gce->blge', d.reshape(B, L, POOL_GROUPS, POOL_GC), w.astype(jnp.float32))
    y = y.reshape(B, L, D) * scale.astype(jnp.float32)
    return y.astype(h.dtype), ext[:, ext.shape[1] - POOL_BUF:].astype(buf.dtype)


def _gated_delta_chunked(q, k, v, beta, g, s0, chunk):
    B, L, H, DK = q.shape
    DV = v.shape[-1]
    n = L // chunk

    def blk(t):
        t = t.reshape((B, n, chunk, H) + t.shape[3:])
        return jnp.moveaxis(jnp.swapaxes(t, 2, 3), 1, 0)

    q, k, v, beta, g = blk(q), blk(k), blk(v), blk(beta), blk(g)
    gc = jnp.cumsum(g, axis=-1)
    idx = jnp.arange(chunk)
    lower_incl = idx[:, None] >= idx[None, :]
    lower_strict = idx[:, None] > idx[None, :]
    diff = gc[..., :, None] - gc[..., None, :]
    decay_incl = jnp.exp(jnp.where(lower_incl, diff, -jnp.inf))
    decay_strict = jnp.where(lower_strict, decay_incl, 0.0)
    kb = k * beta[..., None]
    m = jnp.einsum('nbhid,nbhjd->nbhij', kb, k) * decay_strict
    eye = jnp.eye(chunk, dtype=jnp.float32)
    rhs = jnp.concatenate([v * beta[..., None], kb * jnp.exp(gc)[..., None]], axis=-1)
    sol = lax.linalg.triangular_solve(eye + m, rhs, left_side=True, lower=True, unit_diagonal=True)
    u, w = sol[..., :DV], sol[..., DV:]
    attn = jnp.einsum('nbhid,nbhjd->nbhij', q, k) * decay_incl
    qg = q * jnp.exp(gc)[..., None]
    kg = k * jnp.exp(gc[..., -1:] - gc)[..., None]
    glast = jnp.exp(gc[..., -1])

    def step(s, xs):
        u_c, w_c, qg_c, kg_c, attn_c, gl_c = xs
        v_new = u_c - jnp.einsum('bhck,bhkv->bhcv', w_c, s)
        o_c = jnp.einsum('bhck,bhkv->bhcv', qg_c, s) + jnp.einsum('bhij,bhjv->bhiv', attn_c, v_new)
        s = s * gl_c[..., None, None] + jnp.einsum('bhck,bhcv->bhkv', kg_c, v_new)
        return s, o_c

    s_fin, o = lax.scan(step, s0, (u, w, qg, kg, attn, glast))
    o = jnp.swapaxes(jnp.moveaxis(o, 0, 1), 2, 3).reshape(B, L, H, DV)
    return o, s_fin


def _gated_delta_mixer(h, conv_buf, s0, w_in, conv_w, a_log, dt_bias, norm_w, w_out, chunk):
    B, L, _ = h.shape
    proj = h @ w_in
    qkv, z, a, b = jnp.split(proj, [DN_QKV, DN_QKV + DN_VW, DN_QKV + DN_VW + DN_HEADS], axis=-1)
    qkv, new_conv = _causal_dwconv(qkv, conv_buf, conv_w)
    qkv = jax.nn.silu(qkv)
    q, k, v = jnp.split(qkv, [DN_KW, 2 * DN_KW], axis=-1)
    q = _l2norm(q.reshape(B, L, DN_HEADS, DN_DK)) * (DN_DK ** -0.5)
    k = _l2norm(k.reshape(B, L, DN_HEADS, DN_DK))
    v = v.reshape(B, L, DN_HEADS, DN_DV).astype(jnp.float32)
    beta = jax.nn.sigmoid(b.astype(jnp.float32))
    g = -jnp.exp(a_log.astype(jnp.float32)) * jax.nn.softplus(a.astype(jnp.float32) + dt_bias.astype(jnp.float32))
    o, s_new = _gated_delta_chunked(q, k, v, beta, g, s0.astype(jnp.float32), chunk)
    o = _rmsnorm(o, norm_w) * jax.nn.silu(z.reshape(B, L, DN_HEADS, DN_DV).astype(jnp.float32))
    y = o.reshape(B, L, DN_VW).astype(h.dtype) @ w_out
    return y, new_conv, s_new.astype(s0.dtype)


def _sb_attend(q, k, v, q_pos, k_pos):
    z = jnp.einsum('bqhd,bshd->bhqs', q, k).astype(jnp.float32) * (SB_DH ** -0.5)
    mask = k_pos[None, :] < q_pos[:, None]
    log_1m = jnp.where(mask, jax.nn.log_sigmoid(-z), 0.0)
    rest = lax.cumsum(log_1m, axis=3, reverse=True) - log_1m
    a = jnp.where(mask, jnp.exp(jax.nn.log_sigmoid(z) + rest), 0.0)
    return jnp.einsum('bhqs,bshd->bqhd', a, v.astype(jnp.float32))


def _sb_mixer(h, k_past, v_past, w_qkv, w_out, start):
    B, L, _ = h.shape
    q, k, v = jnp.split(h @ w_qkv, 3, axis=-1)
    q = q.reshape(B, L, SB_HEADS, SB_DH)
    k = k.reshape(B, L, SB_HEADS, SB_DH)
    v = v.reshape(B, L, SB_HEADS, SB_DH)
    if k_past is None:
        nq = L // Q_BLOCK
        qb = jnp.moveaxis(q.reshape(B, nq, Q_BLOCK, SB_HEADS, SB_DH), 1, 0)
        pb = jnp.arange(L).reshape(nq, Q_BLOCK)
        k_pos = jnp.arange(L)
        o = lax.map(lambda qp: _sb_attend(qp[0], k, v, qp[1], k_pos), (qb, pb))
        o = jnp.moveaxis(o, 0, 1).reshape(B, L, SB_HEADS, SB_DH)
    else:
        k_all = jnp.concatenate([k_past.astype(k.dtype), k], axis=1)
        v_all = jnp.concatenate([v_past.astype(v.dtype), v], axis=1)
        o = _sb_attend(q, k_all, v_all, start + jnp.arange(L), jnp.arange(k_all.shape[1]))
    y = o.reshape(B, L, SB_HEADS * SB_DH).astype(h.dtype) @ w_out
    return y, k, v


def _conv_ffn(h, buf, w_up, conv_w, conv_b, w_down):
    val, gate = jnp.split(h @ w_up, 2, axis=-1)
    gate, new_buf = _causal_dwconv(gate, buf, conv_w)
    return (jax.nn.silu(gate + conv_b) * val) @ w_down, new_buf


def _trunk(x, pool_bufs, dn_conv_bufs, dn_states, sb_k_past, sb_v_past, ffn_bufs, start, dn_chunk, p):
    n_pool, n_dnc, n_dn, n_k, n_v, n_ffn = [], [], [], [], [], []
    for i in range(DEPTH):
        kind, j = i % N_MIXERS, i // N_MIXERS
        h = _rmsnorm(x, p['mix_norm'][i])
        if kind == 0:
            y, buf = _pool_mixer(h, pool_bufs[j], p['pool_w'][j], p['pool_scale'][j], start)
            n_pool.append(buf)
        elif kind == 1:
            y, cbuf, s = _gated_delta_mixer(h, dn_conv_bufs[j], dn_states[j], p['dn_w_in'][j], p['dn_conv_w'][j],
                                            p['dn_a_log'][j], p['dn_dt_bias'][j], p['dn_norm'][j],
                                            p['dn_w_out'][j], dn_chunk)
            n_dnc.append(cbuf)
            n_dn.append(s)
        else:
            kp = None if sb_k_past is None else sb_k_past[j]
            vp = None if sb_v_past is None else sb_v_past[j]
            y, kn, vn = _sb_mixer(h, kp, vp, p['sb_w_qkv'][j], p['sb_w_out'][j], start)
            n_k.append(kn)
            n_v.append(vn)
        x = x + y
        h = _rmsnorm(x, p['ffn_norm'][i])
        y, fbuf = _conv_ffn(h, ffn_bufs[i], p['ffn_w_up'][i], p['ffn_conv_w'][i], p['ffn_conv_b'][i],
                            p['ffn_w_down'][i])
        n_ffn.append(fbuf)
        x = x + y
    y = _rmsnorm(x, p['final_norm'])
    return (y, jnp.stack(n_pool), jnp.stack(n_dnc), jnp.stack(n_dn), jnp.stack(n_k), jnp.stack(n_v),
            jnp.stack(n_ffn))


def setup_inputs(seed: int = 0) -> dict:
    key = jax.random.key(seed)
    ks = list(jax.random.split(key, 32))
    f32 = jnp.float32

    def nrm(k, shape, fan_in):
        return jax.random.normal(k, shape, f32) * (fan_in ** -0.5)

    def gain(k, shape):
        return 1.0 + 0.02 * jax.random.normal(k, shape, f32)

    dt = jnp.exp(jax.random.uniform(ks[20], (N_DN_LAYERS, DN_HEADS), f32, math.log(1e-3), math.log(1e-1)))
    return {
        "x_prompt": jax.random.normal(ks[0], (BATCH, SEQ, D_MODEL), f32),
        "x_sample": jax.random.normal(ks[1], (DEC_BATCH, DEC_SEQ, D_MODEL), f32),
        "state_pool": jax.random.normal(ks[2], (N_POOL_LAYERS, DEC_BATCH, POOL_BUF, D_MODEL), f32),
        "state_dn_conv": jax.random.normal(ks[3], (N_DN_LAYERS, DEC_BATCH, DN_CONV - 1, DN_QKV), f32),
        "state_dn": 0.1 * jax.random.normal(ks[4], (N_DN_LAYERS, DEC_BATCH, DN_HEADS, DN_DK, DN_DV), f32),
        "cache_sb_k": jax.random.normal(ks[5], (N_SB_LAYERS, DEC_BATCH, PAST_LEN, SB_HEADS, SB_DH), f32),
        "cache_sb_v": jax.random.normal(ks[6], (N_SB_LAYERS, DEC_BATCH, PAST_LEN, SB_HEADS, SB_DH), f32),
        "state_ffn_conv": jax.random.normal(ks[7], (DEPTH, DEC_BATCH, FFN_CONV - 1, D_FF), f32),
        "mix_norm": gain(ks[8], (DEPTH, D_MODEL)),
        "ffn_norm": gain(ks[9], (DEPTH, D_MODEL)),
        "final_norm": gain(ks[10], (D_MODEL,)),
        "pool_w": nrm(ks[11], (N_POOL_LAYERS, POOL_GROUPS, POOL_GC, POOL_GC), POOL_GC),
        "pool_scale": gain(ks[12], (N_POOL_LAYERS, D_MODEL)),
        "dn_w_in": nrm(ks[13], (N_DN_LAYERS, D_MODEL, DN_QKV + DN_VW + 2 * DN_HEADS), D_MODEL),
        "dn_conv_w": nrm(ks[14], (N_DN_LAYERS, DN_CONV, DN_QKV), DN_CONV),
        "dn_a_log": jnp.log(jax.random.uniform(ks[15], (N_DN_LAYERS, DN_HEADS), f32, 1.0, 16.0)),
        "dn_dt_bias": jnp.log(jnp.expm1(dt)),
        "dn_norm": gain(ks[16], (N_DN_LAYERS, DN_DV)),
        "dn_w_out": nrm(ks[17], (N_DN_LAYERS, DN_VW, D_MODEL), DN_VW),
        "sb_w_qkv": nrm(ks[18], (N_SB_LAYERS, D_MODEL, 3 * SB_HEADS * SB_DH), D_MODEL),
        "sb_w_out": nrm(ks[19], (N_SB_LAYERS, SB_HEADS * SB_DH, D_MODEL), SB_HEADS * SB_DH),
        "ffn_w_up": nrm(ks[21], (DEPTH, D_MODEL, 2 * D_FF), D_MODEL),
        "ffn_conv_w": nrm(ks[22], (DEPTH, FFN_CONV, D_FF), FFN_CONV),
        "ffn_conv_b": 0.01 * jax.random.normal(ks[23], (DEPTH, D_FF), f32),
        "ffn_w_down": nrm(ks[24], (DEPTH, D_FF, D_MODEL), D_FF),
    }


def reference(x_prompt, x_sample, state_pool, state_dn_conv, state_dn, cache_sb_k, cache_sb_v, state_ffn_conv,
              mix_norm, ffn_norm, final_norm, pool_w, pool_scale, dn_w_in, dn_conv_w, dn_a_log, dn_dt_bias,
              dn_norm, dn_w_out, sb_w_qkv, sb_w_out, ffn_w_up, ffn_conv_w, ffn_conv_b, ffn_w_down):
    p = dict(mix_norm=mix_norm, ffn_norm=ffn_norm, final_norm=final_norm, pool_w=pool_w, pool_scale=pool_scale,
             dn_w_in=dn_w_in, dn_conv_w=dn_conv_w, dn_a_log=dn_a_log, dn_dt_bias=dn_dt_bias, dn_norm=dn_norm,
             dn_w_out=dn_w_out, sb_w_qkv=sb_w_qkv, sb_w_out=sb_w_out, ffn_w_up=ffn_w_up, ffn_conv_w=ffn_conv_w,
             ffn_conv_b=ffn_conv_b, ffn_w_down=ffn_w_down)
    bp = x_prompt.shape[0]
    dtp = x_prompt.dtype
    zero_pool = jnp.zeros((N_POOL_LAYERS, bp, POOL_BUF, D_MODEL), dtp)
    zero_dnc = jnp.zeros((N_DN_LAYERS, bp, DN_CONV - 1, DN_QKV), dtp)
    zero_dn = jnp.zeros((N_DN_LAYERS, bp, DN_HEADS, DN_DK, DN_DV), state_dn.dtype)
    zero_ffn = jnp.zeros((DEPTH, bp, FFN_CONV - 1, D_FF), dtp)
    y_prompt, pool_p, dnc_p, dn_p, k_p, v_p, ffn_p = _trunk(
        x_prompt, zero_pool, zero_dnc, zero_dn, None, None, zero_ffn, 0, CHUNK, p)
    y_sample, pool_s, dnc_s, dn_s, k_s, v_s, ffn_s = _trunk(
        x_sample, state_pool, state_dn_conv, state_dn, cache_sb_k, cache_sb_v, state_ffn_conv,
        PAST_LEN, x_sample.shape[1], p)
    return (y_prompt, y_sample, pool_p, pool_s, dnc_p, dnc_s, dn_p, dn_s, k_p, k_s, v_p, v_s, ffn_p, ffn_s)
```

```python
import bisect
import contextlib
import numpy as np
import concourse.bass as bass
import concourse.mybir as mybir
from concourse.bass_utils import run_bass_kernel_spmd

F32 = mybir.dt.float32
BF16 = mybir.dt.bfloat16
AF = mybir.ActivationFunctionType
ALU = mybir.AluOpType

D = 1024
SEQ = 4096
DSEQ = 64
PAST = 2048
DFF = 2816
NFC = 22
EPS = 1e-6
SLOT = 4096
NSLOT = 4
SAME_ENGINE_SYNC = True


class Sched:
    ENGS = ['pe', 'act', 'dve', 'pool', 'sp']

    def __init__(self):
        self.streams = {e: [] for e in self.ENGS}
        self.cnt = {e: 0 for e in self.ENGS}
        self.lane_cnt = {}
        self.seen = {e: {} for e in self.ENGS}
        self.snaps = {}
        self.res = {}
        self.nops = 0
        self.phase = 'setup'
        self.tags = {e: [] for e in self.ENGS}

    def _snap_put(self, key, val, d):
        vals, dicts = self.snaps.setdefault(key, ([], []))
        if dicts and dicts[-1] is d:
            return
        vals.append(val)
        dicts.append(d)

    def _snap_get(self, key, val):
        if key not in self.snaps:
            return None
        vals, dicts = self.snaps[key]
        i = bisect.bisect_right(vals, val) - 1
        return dicts[i] if i >= 0 else None

    def _wait(self, eng, key, val):
        seen = self.seen[eng]
        if seen.get(key, 0) >= val:
            return
        if key == ('eng', eng) and (eng == 'pe' or not SAME_ENGINE_SYNC):
            return
        self.streams[eng].append(('wait', key, val))
        new = dict(seen)
        new[key] = val
        sn = self._snap_get(key, val)
        if sn:
            for k, v in sn.items():
                if new.get(k, 0) < v:
                    new[k] = v
        self.seen[eng] = new

    def emit(self, eng, fn, reads=(), writes=(), lane=None, same_gen=False):
        deps = {}
        for r in reads:
            ent = self.res.get(r)
            if ent and ent[0]:
                k, v = ent[0]
                if deps.get(k, 0) < v:
                    deps[k] = v
            if ent and isinstance(r, tuple) and r[0] == 'ps':
                for k, v in ent[1].items():
                    if k != ('eng', eng) and deps.get(k, 0) < v:
                        deps[k] = v
        for w in writes:
            ent = self.res.get(w)
            if ent:
                if ent[0]:
                    k, v = ent[0]
                    if deps.get(k, 0) < v:
                        deps[k] = v
                for k, v in ent[1].items():
                    if deps.get(k, 0) < v:
                        deps[k] = v
        for k, v in deps.items():
            self._wait(eng, k, v)
        if lane is not None:
            key = ('lane', lane)
            cur = self.lane_cnt.get(lane, 0)
            if cur and not same_gen:
                self._wait(eng, key, cur)
            cur += 16
            self.lane_cnt[lane] = cur
            ref = (key, cur)
            self.streams[eng].append(('dma', fn, lane))
        else:
            key = ('eng', eng)
            self.cnt[eng] += 1
            ref = (key, self.cnt[eng])
            self.streams[eng].append(('op', fn))
            self.tags[eng].append(self.phase)
        self._snap_put(key, ref[1], self.seen[eng])
        for r in reads:
            ent = self.res.setdefault(r, [None, {}])
            if ent[1].get(key, 0) < ref[1]:
                ent[1][key] = ref[1]
        for w in writes:
            self.res[w] = [ref, {}]
        self.nops += 1
        return ref

    def barrier(self):
        for e in self.ENGS:
            for f in self.ENGS:
                if self.cnt[f]:
                    self._wait(e, ('eng', f), self.cnt[f])
            for l, c in self.lane_cnt.items():
                self._wait(e, ('lane', l), c)

    def finish(self):
        for l, c in self.lane_cnt.items():
            self._wait('sp', ('lane', l), c)
        for f in self.ENGS:
            if self.cnt[f]:
                self._wait('sp', ('eng', f), self.cnt[f])

    def replay(self, nc):
        with contextlib.ExitStack() as es:
            sems = {}
            for e in self.ENGS:
                if self.cnt[e]:
                    sems[('eng', e)] = es.enter_context(nc.semaphore("s_" + e))
            for i, l in enumerate(self.lane_cnt):
                sems[('lane', l)] = es.enter_context(nc.semaphore("l_%d" % i))
            block = es.enter_context(nc.Block())

            def run(engname):
                def body(eng):
                    for it in self.streams[engname]:
                        if it[0] == 'wait':
                            eng.wait_ge(sems[it[1]], it[2])
                        elif it[0] == 'op':
                            it[1](eng).then_inc(sems[('eng', engname)], 1)
                        else:
                            it[1](eng).then_inc(sems[('lane', it[2])], 16)
                return body
            block.tensor(run('pe'))
            block.scalar(run('act'))
            block.vector(run('dve'))
            block.gpsimd(run('pool'))
            block.sync(run('sp'))


def _unit(W, col0, ncol=128):
    K = W.shape[0]
    return np.ascontiguousarray(
        W[:, col0:col0 + ncol].reshape(K // 128, 128, ncol).transpose(1, 0, 2)).reshape(128, -1)


def chunk_plan():
    plan = []

    def ffn(l):
        for cc in range(NFC // 2):
            plan.append([('up', l, 2 * cc, 0), ('up', l, 2 * cc, 1), ('up', l, 2 * cc + 1, 0), ('up', l, 2 * cc + 1, 1)])
        for j in range(8):
            plan.append([('down', l, j)])
    plan.append([('pool', 0)])
    ffn(0)
    plan.append([('dn_ab',)])
    for h in range(8):
        plan.append([('dn_u', h * 128), ('dn_u', 1024 + h * 128), ('dn_u', 2048 + h * 128), ('dn_u', 3072 + h * 128)])
    for n in range(2):
        plan.append([('dn_out', 4 * n + i) for i in range(4)])
    ffn(1)
    for n in range(4):
        plan.append([('sb_u', (4 * n + i) * 128) for i in range(4)])
    plan.append([('sb_v', 0)])
    plan.append([('sb_v', 1)])
    for n in range(2):
        plan.append([('sb_out', 4 * n + i) for i in range(4)])
    ffn(2)
    plan.append([('pool', 1)])
    ffn(3)
    return plan


def piece_size(p):
    k = p[0]
    if k == 'pool':
        return 2048
    if k == 'down':
        return DFF
    if k == 'dn_ab':
        return 128
    if k == 'sb_v':
        return 4096
    return 1024


def chunk_offsets():
    plan = chunk_plan()
    offs = []
    o = 0
    for ch in plan:
        used = sum(piece_size(p) for p in ch)
        offs.append((o, used))
        o += used
    tot = ((o + 1023) // 1024) * 1024
    return plan, offs, tot


def piece_data(p, w):
    k = p[0]
    if k == 'pool':
        a = w['pool_w'][p[1]].reshape(4, 2, 128, 2, 128)
        return np.ascontiguousarray(a.transpose(2, 0, 3, 1, 4)).reshape(128, 2048)
    if k == 'up':
        return _unit(w['ffn_w_up'][p[1]], p[3] * DFF + p[2] * 128)
    if k == 'down':
        return _unit(w['ffn_w_down'][p[1]], p[2] * 128)
    if k == 'dn_ab':
        return _unit(w['dn_w_in'][0], 4096, 16)
    if k == 'dn_u':
        return _unit(w['dn_w_in'][0], p[1])
    if k == 'dn_out':
        return _unit(w['dn_w_out'][0], p[1] * 128)
    if k == 'sb_u':
        return _unit(w['sb_w_qkv'][0], p[1])
    if k == 'sb_v':
        return _unit(w['sb_w_qkv'][0], 2048 + p[1] * 512, 512)
    if k == 'sb_out':
        return _unit(w['sb_w_out'][0], p[1] * 128)
    raise KeyError(k)


def pack_weights(w):
    plan, offs, tot = chunk_offsets()
    out = np.zeros((128, tot), np.float32)
    for ch, (o, used) in zip(plan, offs):
        for p in ch:
            n = piece_size(p)
            out[:, o:o + n] = piece_data(p, w)
            o += n
    return out


CV = {}
_o = 0
for _name, _n in [('mixn', 32), ('ffnn', 32), ('finn', 8), ('pscale', 16), ('fcw', 4 * NFC * 3), ('fcb', 4 * NFC),
                  ('dcw', 96), ('dnorm', 1), ('alog', 8), ('dtb', 8)]:
    CV[_name] = _o
    _o += _n
NCV = _o


def pack_cvec(w):
    cv = np.zeros((128, NCV), np.float32)

    def fm(v):
        return np.asarray(v, np.float32).reshape(-1, 128).T
    for l in range(4):
        cv[:, CV['mixn'] + 8 * l: CV['mixn'] + 8 * l + 8] = fm(w['mix_norm'][l])
        cv[:, CV['ffnn'] + 8 * l: CV['ffnn'] + 8 * l + 8] = fm(w['ffn_norm'][l])
        a = np.asarray(w['ffn_conv_w'][l], np.float32)
        a = a.reshape(3, NFC, 128).transpose(2, 1, 0)
        cv[:, CV['fcw'] + l * NFC * 3: CV['fcw'] + (l + 1) * NFC * 3] = a.reshape(128, NFC * 3)
        cv[:, CV['fcb'] + l * NFC: CV['fcb'] + (l + 1) * NFC] = fm(w['ffn_conv_b'][l])
    cv[:, CV['finn']:CV['finn'] + 8] = fm(w['final_norm'])
    for j in range(2):
        cv[:, CV['pscale'] + 8 * j: CV['pscale'] + 8 * j + 8] = fm(w['pool_scale'][j])
    a = np.asarray(w['dn_conv_w'][0], np.float32).reshape(4, 24, 128).transpose(2, 1, 0)
    cv[:, CV['dcw']:CV['dcw'] + 96] = a.reshape(128, 96)
    cv[:, CV['dnorm']] = np.asarray(w['dn_norm'][0], np.float32)
    cv[:, CV['alog']:CV['alog'] + 8] = np.asarray(w['dn_a_log'][0], np.float32)[None, :]
    cv[:, CV['dtb']:CV['dtb'] + 8] = np.asarray(w['dn_dt_bias'][0], np.float32)[None, :]
    return cv


def build_program(n_tiles=8, do_sample=True, layers=None):
    nc = bass.Bass("TRN2", target_bir_lowering=False)
    S = Sched()
    plan, offs, WTOT = chunk_offsets()
    NCH = len(plan)

    def din(name, shape):
        return nc.dram_tensor(name, shape, F32, kind="ExternalInput").ap()

    def dout(name, shape):
        return nc.dram_tensor(name, shape, F32, kind="ExternalOutput").ap()

    xTp = din("xTp", [D, SEQ])
    xTs = din("xTs", [D, DSEQ])
    wpack = din("wpack", [128, WTOT])
    cvec_d = din("cvec", [128, NCV])
    st_pool = din("st_pool", [2, 128, 8, 15])
    st_dnc = din("st_dnc", [128, 24, 3])
    st_dn = din("st_dn", [128, 8, 128])
    st_ffn = din("st_ffn", [4, 128, NFC, 2])
    cacheKT = din("cacheKT", [D, PAST])
    cacheV = din("cacheV", [8, 128, 16, 128])

    yT = {'p': dout("yTp", [D, SEQ]), 's': dout("yTs", [D, DSEQ])}
    o_pool = {'p': dout("o_poolp", [2, 128, 8, 15]), 's': dout("o_pools", [2, 128, 8, 15])}
    o_dnc = {'p': dout("o_dncp", [128, 24, 3]), 's': dout("o_dncs", [128, 24, 3])}
    o_dn = {'p': dout("o_dnp", [128, 8, 128]), 's': dout("o_dns", [128, 8, 128])}
    kTo = {'p': dout("kTp", [D, SEQ]), 's': dout("kTs", [D, DSEQ])}
    vo = {'p': dout("vp", [SEQ, D]), 's': dout("vs", [DSEQ, D])}
    o_ffn = {'p': dout("o_ffnp", [4, 128, NFC, 2]), 's': dout("o_ffns", [4, 128, NFC, 2])}

    wbf = nc.dram_tensor("wbf", [128, WTOT], BF16).ap()
    KTs = nc.dram_tensor("KTs", [8, 128, SEQ], BF16).ap()
    Vs = nc.dram_tensor("Vs", [8, 128, 32, 128], BF16).ap()

    def sb(name, shape, dt):
        return nc.alloc_sbuf_tensor(name, shape, dt).ap()
    xT = sb("xT", [128, 8, 512], F32)
    hT = sb("hT", [128, 8, 512], BF16)
    wring = sb("wring", [128, NSLOT, SLOT], BF16)
    cv = sb("cv", [128, NCV], F32)
    ident = sb("ident", [128, 128], F32)
    ones_f = sb("ones_f", [128, 128], F32)
    minclT = sb("minclT", [128, 128], F32)
    nminclT = sb("nminclT", [128, 128], F32)
    mstrict = sb("mstrict", [128, 128], F32)
    blockones = sb("blockones", [128, 128], F32)
    ones_b = sb("ones_b", [128, 128], BF16)
    nones_b = sb("nones_b", [128, 128], BF16)
    negUI = sb("negUI", [128, 128], BF16)
    mask_lt = sb("mask_lt", [128, 128], BF16)
    invc = sb("invc", [128, 4, 15], F32)
    nexpA = sb("nexpA", [128, 8], F32)
    poolhist = sb("poolhist", [128, 2, 8, 15], F32)
    ghist = sb("ghist", [128, 4, NFC, 2], F32)
    chist = sb("chist", [128, 24, 3], F32)
    Sdn = sb("Sdn", [128, 8, 128], F32)
    Sdb = sb("Sdb", [128, 8, 128], BF16)
    nsq = sb("nsq", [128, 2, 512], BF16)
    nrt = sb("nrt", [128, 512], F32)
    nrstd = sb("nrstd", [128, 512], F32)
    ARENA = 30 * 1024
    arena = sb("arena", [128, ARENA], F32)
    psb = [nc.alloc_psum_tensor("ps%d" % i, [128, 512], F32).ap() for i in range(8)]

    def MM(out, lhsT, rhs, start=True, stop=True, rd=(), wr=()):
        S.emit('pe', lambda e: e.matmul(out, lhsT=lhsT, rhs=rhs, start=start, stop=stop), rd, wr)
        if lhsT.dtype == F32:
            S.tags['pe'].append(S.phase)

    def TR(out, in_, idn, rd=(), wr=()):
        S.emit('pe', lambda e: e.transpose(out, in_, idn), rd, wr)

    def ACT(out, in_, func, rd=(), wr=(), bias=None, scale=None):
        kw = {}
        if bias is not None:
            kw['bias'] = bias
        if scale is not None:
            kw['scale'] = scale
        S.emit('act', lambda e: e.activation(out=out, in_=in_, func=func, **kw), rd, wr)

    def TT(eng, out, in0, in1, op, rd=(), wr=()):
        S.emit(eng, lambda e: e.tensor_tensor(out=out, in0=in0, in1=in1, op=op), rd, wr)

    def TS(eng, out, in0, s1, s2, op0, op1=None, rd=(), wr=()):
        if op1 is None:
            S.emit(eng, lambda e: e.tensor_scalar(out=out, in0=in0, scalar1=s1, scalar2=None, op0=op0), rd, wr)
        else:
            S.emit(eng, lambda e: e.tensor_scalar(out=out, in0=in0, scalar1=s1, scalar2=s2, op0=op0, op1=op1), rd, wr)

    def STT(out, in0, sc, in1, op0, op1, rd=(), wr=()):
        S.emit('dve', lambda e: e.scalar_tensor_tensor(out=out, in0=in0, scalar=sc, in1=in1, op0=op0, op1=op1), rd, wr)

    def CP(eng, out, in_, rd=(), wr=()):
        if eng == 'act':
            S.emit('act', lambda e: e.copy(out=out, in_=in_), rd, wr)
        else:
            S.emit(eng, lambda e: e.tensor_copy(out=out, in_=in_), rd, wr)

    def MS(eng, ap, val, wr=()):
        S.emit(eng, lambda e: e.memset(ap, val), (), wr)

    def RCP(out, in_, rd=(), wr=()):
        S.emit('dve', lambda e: e.reciprocal(out=out, in_=in_), rd, wr)

    def DMA(eng, out, in_, rd=(), wr=(), lane=None, same_gen=False):
        S.emit(eng, lambda e: e.dma_start(out=out, in_=in_), rd, wr, lane=lane, same_gen=same_gen)

    def ASEL(out, in_, pattern, cmp, fill, base, cm, rd=(), wr=()):
        S.emit('pool', lambda e: e.affine_select(out=out, in_=in_, pattern=pattern, compare_op=cmp, fill=fill,
                                                 base=base, channel_multiplier=cm), rd, wr)

    class Arena:
        def __init__(self):
            self.off = 0

        def reset(self):
            self.off = 0

        def alloc(self, shape, dt):
            n = int(np.prod(shape))
            n4 = n if dt == F32 else (n + 1) // 2
            n4 = (n4 + 7) // 8 * 8
            assert self.off + n4 <= ARENA, ("arena overflow", self.off, n4)
            v = arena[:, self.off:self.off + n4]
            self.off += n4
            if dt != F32:
                v = v.bitcast(dt)
            v = v[:, 0:n]
            if len(shape) == 2:
                return v.rearrange("p (a b) -> p a b", a=shape[0])
            if len(shape) == 3:
                return v.rearrange("p (a b c) -> p a b c", a=shape[0], b=shape[1])
            return v
    AR = Arena()

    class Rot:
        cnt = [0]

        def __init__(self, n, shape, dt):
            Rot.cnt[0] += 1
            self.id = Rot.cnt[0]
            self.bufs = [AR.alloc(shape, dt) for _ in range(n)]
            self.i = 0

        def next(self):
            k = self.i % len(self.bufs)
            self.i += 1
            return self.bufs[k], ('rot', self.id, k)

    class PSRot:
        def __init__(self, banks):
            self.banks = banks
            self.i = 0

        def next(self):
            b = self.banks[self.i % len(self.banks)]
            self.i += 1
            return psb[b], ('ps', b)
    PSA = PSRot([0, 1])
    PSB = PSRot([2, 3, 4, 5])

    class PSQ:
        def __init__(self):
            self.i = 0
            self.banks = [6, 7]

        def next(self):
            b = self.banks[self.i % len(self.banks)]
            self.i += 1
            return psb[b][:, 0:128], ('ps', b)
    PQ = PSQ()

    passes = n_tiles + (1 if do_sample else 0)
    CASTW = 32 * 1024
    ncast = (WTOT + CASTW - 1) // CASTW

    class WStream:
        def __init__(self):
            self.seq = [i % NCH for i in range(passes * NCH)]
            self.nl = 0
            self.na = 0

        def _load(self):
            if self.nl >= len(self.seq):
                return
            ci = self.seq[self.nl]
            s = self.nl % NSLOT
            off, used = offs[ci]
            rd = [('wbf', k) for k in range(off // CASTW, (off + used - 1) // CASTW + 1)]
            DMA('sp', wring[:, s, 0:used], wbf[:, off:off + used], rd=rd, wr=[('w', s)], lane=('w', s))
            self.nl += 1

        def prime(self):
            for _ in range(NSLOT):
                self._load()

        def acquire(self, expect):
            ci = self.seq[self.na]
            assert expect is None or plan[ci][0][0] == expect, (plan[ci], expect)
            s = self.na % NSLOT
            self.na += 1
            return wring[:, s, :], ('w', s)

        def release(self):
            self._load()
    W = WStream()

    DMA('sp', cv, cvec_d, wr=['cv'], lane='cv')
    for k in range(ncast):
        c0 = k * CASTW
        c1 = min(WTOT, c0 + CASTW)
        DMA('pool', wbf[:, c0:c1].rearrange("p (a b) -> p a b", b=1024),
            wpack[:, c0:c1].rearrange("p (a b) -> p a b", b=1024), wr=[('wbf', k)], lane='cast')
    MS('dve', ident, 0.0, ['ident'])
    ASEL(ident, ident, [[-1, 128]], ALU.not_equal, 1.0, 0, 1, ['ident'], ['ident'])
    MS('dve', ones_f, 1.0, ['ones_f'])
    MS('dve', ones_b, 1.0, ['ones_b'])
    MS('dve', nones_b, -1.0, ['nones_b'])
    MS('dve', mask_lt, 1.0, ['mask_lt'])
    ASEL(mask_lt, mask_lt, [[1, 128]], ALU.is_gt, 0.0, 0, -1, ['mask_lt'], ['mask_lt'])
    MS('dve', negUI, -1.0, ['negUI'])
    ASEL(negUI, negUI, [[-1, 128]], ALU.is_ge, 0.0, 0, 1, ['negUI'], ['negUI'])
    MS('dve', minclT, 1.0, ['minclT'])
    ASEL(minclT, minclT, [[1, 128]], ALU.is_ge, 0.0, 0, -1, ['minclT'], ['minclT'])
    MS('pool', minclT[0:64, 64:128], 0.0, ['minclT'])
    MS('dve', nminclT, -1.0, ['nminclT'])
    ASEL(nminclT, nminclT, [[1, 128]], ALU.is_ge, 0.0, 0, -1, ['nminclT'], ['nminclT'])
    MS('pool', nminclT[0:64, 64:128], 0.0, ['nminclT'])
    MS('dve', mstrict, 1.0, ['mstrict'])
    ASEL(mstrict, mstrict, [[-1, 128]], ALU.is_gt, 0.0, 0, 1, ['mstrict'], ['mstrict'])
    MS('pool', mstrict[64:128, 0:64], 0.0, ['mstrict'])
    MS('dve', blockones, 1.0, ['blockones'])
    MS('pool', blockones[0:64, 64:128], 0.0, ['blockones'])
    MS('pool', blockones[64:128, 0:64], 0.0, ['blockones'])
    for g in range(4):
        w_ = 2 ** (g + 1)
        MS('dve', invc[:, g, :], 1.0 / w_, ['invc'])
        for t in range(w_ - 1):
            MS('dve', invc[:, g, t:t + 1], 1.0 / (t + 1), ['invc'])
    ACT(nexpA, cv[:, CV['alog']:CV['alog'] + 8], AF.Exp, ['cv'], ['nexpA'])
    TS('dve', nexpA, nexpA, -1.0, None, ALU.mult, None, ['nexpA'], ['nexpA'])
    W.prime()
    S.barrier()

    def run_rr(gens):
        gens = list(gens)
        while gens:
            for g in list(gens):
                try:
                    next(g)
                except StopIteration:
                    gens.remove(g)

    def cvcol(name, idx):
        o = CV[name] + idx
        return cv[:, o:o + 1]

    def rmsnorm(T, gname, gbase, out_fn, out_keys, out_dt_is_f32=False):
        ps, pk = PSA.next()
        for c in range(8):
            b = c % 2
            ACT(nsq[:, b, 0:T], xT[:, c, 0:T], AF.Square, [('xT', c)], [('nsq', b)])
            MM(ps[:, 0:T], ones_b, nsq[:, b, 0:T], c == 0, c == 7, [('nsq', b)], [pk])
        ACT(nrt[:, 0:T], ps[:, 0:T], AF.Ln, [pk], ['nrt'], bias=EPS, scale=1.0 / D)
        ACT(nrstd[:, 0:T], nrt[:, 0:T], AF.Exp, ['nrt'], ['nrstd'], scale=-0.5)
        for c in range(8):
            STT(out_fn(c), xT[:, c, 0:T], cvcol(gname, gbase + c), nrstd[:, 0:T], ALU.mult, ALU.mult,
                [('xT', c), 'nrstd'], [out_keys(c)])

    def norm_to_hT(T, gname, gbase):
        rmsnorm(T, gname, gbase, lambda c: hT[:, c, 0:T], lambda c: ('hT', c))

    HT_ALL = [('hT', c) for c in range(8)]

    def pool_layer(T, j, l, first):
        S.phase = 'pool%d' % l
        S.barrier()
        AR.reset()
        L = 15 + T
        ext = AR.alloc([8, L], F32)
        dT = AR.alloc([8, T], BF16)
        tmps = [[AR.alloc([2, L], F32) for _ in range(2)] for _ in range(4)]
        fix = AR.alloc([8, 15], F32)
        CP('pool', ext[:, :, 0:15], poolhist[:, j, :, :], [('phist', j)], ['ext_h'])
        rmsnorm(T, 'mixn', 8 * l, lambda c: ext[:, c, 15:L], lambda c: ('ext', c))
        CP('pool', poolhist[:, j, :, :], ext[:, :, T:L], ['ext_h'] + [('ext', c) for c in range(8)], [('phist', j)])
        slot, sk = W.acquire('pool')
        for g in range(4):
            eng = 'dve' if g % 2 == 0 else 'pool'
            cs = [2 * g, 2 * g + 1]
            ekeys = ['ext_h', ('ext', cs[0]), ('ext', cs[1])]
            src = ext[:, 2 * g:2 * g + 2, :]
            lo = 0
            rkeys = ekeys
            for st in range(g + 1):
                sh = 2 ** st
                dst = tmps[g][st % 2]
                nlo = lo + sh
                TT(eng, dst[:, :, nlo:L], src[:, :, nlo:L], src[:, :, lo:L - sh], ALU.add, rkeys, [('ptmp', g, st % 2)])
                src = dst
                lo = nlo
                rkeys = [('ptmp', g, st % 2)]
            wdw = 2 ** (g + 1)
            for ci, c in enumerate(cs):
                STT(dT[:, c, :], src[:, ci, 15:L], 1.0 / wdw, ext[:, c, 15:L], ALU.mult, ALU.subtract,
                    rkeys + ekeys, [('dT', c)])
                if first:
                    TT('dve', fix[:, c, :], src[:, ci, 15:30], invc[:, g, :], ALU.mult, rkeys, [('fix', c)])
                    TT('dve', dT[:, c, 0:15], fix[:, c, :], ext[:, c, 15:30], ALU.subtract,
                       [('fix', c)] + ekeys, [('dT', c)])
        for g in range(4):
            for ec in range(2):
                ps, pk = PSA.next()
                for kc in range(2):
                    o = ((g * 2 + ec) * 2 + kc) * 128
                    MM(ps[:, 0:T], slot[:, o:o + 128], dT[:, 2 * g + kc, :], kc == 0, kc == 1, [sk, ('dT', 2 * g + kc)], [pk])
                c = 2 * g + ec
                STT(xT[:, c, 0:T], ps[:, 0:T], cvcol('pscale', 8 * j + c), xT[:, c, 0:T], ALU.mult, ALU.add,
                    [pk, ('xT', c)], [('xT', c)])
        W.release()

    def ffn_layer(T, l):
        S.phase = 'ffn%d' % l
        S.barrier()
        AR.reset()
        PSB.banks = [2, 3, 4, 5, 6, 7]
        actT = AR.alloc([NFC, T], BF16)
        gext = Rot(3, [T + 2], F32)
        tb = Rot(3, [T], F32)
        sb_ = Rot(2, [T], F32)
        norm_to_hT(T, 'ffnn', 8 * l)
        for cc in range(NFC // 2):
            slot, sk = W.acquire('up')
            for ci in range(2):
                c = 2 * cc + ci
                psv, kv = PSB.next()
                psg, kg = PSB.next()
                for kc in range(8):
                    o = (ci * 2) * 1024 + kc * 128
                    MM(psv[:, 0:T], slot[:, o:o + 128], hT[:, kc, 0:T], kc == 0, kc == 7, [sk, ('hT', kc)], [kv])
                for kc in range(8):
                    o = (ci * 2 + 1) * 1024 + kc * 128
                    MM(psg[:, 0:T], slot[:, o:o + 128], hT[:, kc, 0:T], kc == 0, kc == 7, [sk, ('hT', kc)], [kg])
                ge, gk = gext.next()
                CP('pool', ge[:, 0:2], ghist[:, l, c, :], [('ghist', l)], [gk])
                CP('act', ge[:, 2:T + 2], psg[:, 0:T], [kg], [gk])
                CP('pool', ghist[:, l, c, :], ge[:, T:T + 2], [gk], [('ghist', l)])
                t, tk = tb.next()
                wb = CV['fcw'] + (l * NFC + c) * 3
                TS('dve', t, ge[:, 0:T], cv[:, wb:wb + 1], None, ALU.mult, None, [gk], [tk])
                STT(t, ge[:, 1:T + 1], cv[:, wb + 1:wb + 2], t, ALU.mult, ALU.add, [gk, tk], [tk])
                STT(t, ge[:, 2:T + 2], cv[:, wb + 2:wb + 3], t, ALU.mult, ALU.add, [gk, tk], [tk])
                s_, sk2 = sb_.next()
                ACT(s_, t, AF.Silu, [tk], [sk2], bias=cvcol('fcb', l * NFC + c))
                TT('dve', actT[:, c, :], s_, psv[:, 0:T], ALU.mult, [sk2, kv], [('actT', c)])
            W.release()
        for j in range(8):
            slot, sk = W.acquire('down')
            ps, pk = PSA.next()
            for c in range(NFC):
                MM(ps[:, 0:T], slot[:, c * 128:(c + 1) * 128], actT[:, c, :], c == 0, c == NFC - 1,
                   [sk, ('actT', c)], [pk])
            TT('dve', xT[:, j, 0:T], ps[:, 0:T], xT[:, j, 0:T], ALU.add, [pk, ('xT', j)], [('xT', j)])
            W.release()

    def dn_layer(T):
        S.phase = 'dn'
        S.barrier()
        AR.reset()
        NB = (T + 127) // 128
        BS = min(T, 128)
        NCHK = T // 64
        PSB.banks = [2, 3]
        PQ.banks = [4, 5, 6, 7]
        norm_to_hT(T, 'mixn', 8)
        ogT = AR.alloc([8, T], BF16)
        ab = AR.alloc([NB, 16], F32)
        gx = AR.alloc([NB, 8], F32)
        g_tm = AR.alloc([NB, 8], F32)
        nbeta = AR.alloc([NB, 8], F32)
        beta = AR.alloc([NB, 8], F32)
        gc_tm = AR.alloc([NB, 8], F32)
        ngc_tm = AR.alloc([NB, 8], F32)
        gl_tm = AR.alloc([NB, 8], F32)
        kgs = AR.alloc([NB, 8], F32)
        kbs = AR.alloc([NB, 8], F32)
        slot, sk = W.acquire('dn_ab')
        for nb in range(NB):
            ps, pk = PQ.next()
            for kc in range(8):
                MM(ps[0:BS, 0:16], hT[:, kc, nb * 128:nb * 128 + BS], slot[:, kc * 16:(kc + 1) * 16], kc == 0, kc == 7,
                   [sk, ('hT', kc)], [pk])
            CP('dve', ab[0:BS, nb, :], ps[0:BS, 0:16], [pk], ['ab'])
        W.release()
        A = slice(0, BS)
        for nb in range(NB):
            TT('dve', gx[A, nb, :], ab[A, nb, 0:8], cv[A, CV['dtb']:CV['dtb'] + 8], ALU.add, ['ab'], ['gx'])
        ACT(gx[A], gx[A], AF.Exp, ['gx'], ['gx'])
        ACT(gx[A], gx[A], AF.Ln, ['gx'], ['gx'], bias=1.0)
        for nb in range(NB):
            TT('dve', g_tm[A, nb, :], gx[A, nb, :], nexpA[A], ALU.mult, ['gx'], ['g_tm'])
        ACT(beta[A], ab[A, :, 8:16], AF.Exp, ['ab'], ['beta'], scale=-1.0)
        TS('dve', beta[A], beta[A], 1.0, None, ALU.add, None, ['beta'], ['beta'])
        RCP(beta[A], beta[A], ['beta'], ['beta'])
        TS('dve', nbeta[A], beta[A], -1.0, None, ALU.mult, None, ['beta'], ['nbeta'])
        for nb in range(NB):
            ps, pk = PQ.next()
            MM(ps[0:BS, 0:8], minclT[0:BS, 0:BS], g_tm[0:BS, nb, :], True, True, ['g_tm'], [pk])
            CP('dve', gc_tm[0:BS, nb, :], ps[0:BS, 0:8], [pk], ['gc_tm'])
            ps2, pk2 = PQ.next()
            MM(ps2[0:BS, 0:8], blockones[0:BS, 0:BS], g_tm[0:BS, nb, :], True, True, ['g_tm'], [pk2])
            CP('dve', gl_tm[0:BS, nb, :], ps2[0:BS, 0:8], [pk2], ['gl_tm'])
        TS('dve', ngc_tm[A], gc_tm[A], -1.0, None, ALU.mult, None, ['gc_tm'], ['ngc_tm'])
        TT('dve', kgs[A], gl_tm[A], gc_tm[A], ALU.subtract, ['gl_tm', 'gc_tm'], ['kgs'])
        ACT(kgs[A], kgs[A], AF.Exp, ['kgs'], ['kgs'])
        ACT(kbs[A], gc_tm[A], AF.Exp, ['gc_tm'], ['kbs'])
        TT('dve', kbs[A], kbs[A], beta[A], ALU.mult, ['kbs', 'beta'], ['kbs'])

        cext = Rot(2, [T + 3], F32)
        tbuf = Rot(2, [T], F32)
        qkvs = [AR.alloc([T], F32) for _ in range(3)]
        zs = AR.alloc([T], F32)
        sqb = Rot(2, [T], BF16)
        rr = Rot(2, [T], F32)
        kbg = AR.alloc([NB, 128], BF16)
        kgm = AR.alloc([NB, 128], BF16)
        vb = AR.alloc([NB, 128], BF16)
        u_ = AR.alloc([NB, 128], F32)
        vnew = AR.alloc([NB, 128], BF16)
        wT = AR.alloc([T], BF16)
        qgT = AR.alloc([T], BF16)
        attnT = AR.alloc([NB, 128], BF16)
        attnFs = [Rot(2, [128], F32) for _ in range(NB)]
        qnb = AR.alloc([T], BF16)
        knb = AR.alloc([T], BF16)
        Ybs = [Rot(2, [128], BF16) for _ in range(NB)]
        EG = AR.alloc([T], F32)
        m128s = [Rot(12, [128], F32) for _ in range(NB)]
        mbs = [Rot(12, [128], BF16) for _ in range(NB)]
        og = AR.alloc([T], F32)
        kn2 = AR.alloc([T], BF16)

        for h in range(8):
            slot, sk = W.acquire('dn_u')
            for idx in range(3):
                ps, pk = PSB.next()
                for kc in range(8):
                    o = idx * 1024 + kc * 128
                    MM(ps[:, 0:T], slot[:, o:o + 128], hT[:, kc, 0:T], kc == 0, kc == 7, [sk, ('hT', kc)], [pk])
                ce, ck = cext.next()
                ch = idx * 8 + h
                CP('pool', ce[:, 0:3], chist[:, ch, :], ['chist'], [ck])
                CP('act', ce[:, 3:T + 3], ps[:, 0:T], [pk], [ck])
                CP('pool', chist[:, ch, :], ce[:, T:T + 3], [ck], ['chist'])
                t, tk = tbuf.next()
                wb = CV['dcw'] + ch * 4
                TS('dve', t, ce[:, 0:T], cv[:, wb:wb + 1], None, ALU.mult, None, [ck], [tk])
                for tap in range(1, 4):
                    STT(t, ce[:, tap:T + tap], cv[:, wb + tap:wb + tap + 1], t, ALU.mult, ALU.add, [ck, tk], [tk])
                ACT(qkvs[idx], t, AF.Silu, [tk], [('qkv', idx)])
            ps, pk = PSB.next()
            for kc in range(8):
                o = 3 * 1024 + kc * 128
                MM(ps[:, 0:T], slot[:, o:o + 128], hT[:, kc, 0:T], kc == 0, kc == 7, [sk, ('hT', kc)], [pk])
            ACT(zs, ps[:, 0:T], AF.Silu, [pk], ['zs'])
            W.release()
            for idx in range(2):
                sq, sqk = sqb.next()
                ACT(sq, qkvs[idx], AF.Square, [('qkv', idx)], [sqk])
                ps, pk = PSA.next()
                MM(ps[:, 0:T], ones_b, sq, True, True, [sqk], [pk])
                r, rk = rr.next()
                ACT(r, ps[:, 0:T], AF.Ln, [pk], [rk], bias=EPS)
                ACT(r, r, AF.Exp, [rk], [rk], scale=-0.5)
                STT(qkvs[idx], qkvs[idx], (128.0 ** -0.5) if idx == 0 else 1.0, r, ALU.mult, ALU.mult,
                    [('qkv', idx), rk], [('qkv', idx)])
            qn, kn, vs_ = qkvs
            CP('pool', kn2, kn, [('qkv', 1)], ['kn2'])
            CP('act', knb, kn, [('qkv', 1)], ['knb'])
            CP('pool', qnb, qn, [('qkv', 0)], ['qnb'])
            def blk_gen(nb, h=h, kn=kn, qn=qn, vs_=vs_):
                m128 = m128s[nb]
                mb = mbs[nb]
                attnF = attnFs[nb]
                Yb = Ybs[nb]
                cs = slice(nb * 128, nb * 128 + BS)
                R = slice(0, BS)
                bcol = lambda tl: tl[0:BS, nb, h:h + 1]
                ps, pk = PQ.next()
                TR(ps[0:BS, :], kn[:, cs], ident, [('qkv', 1), 'ident'], [pk])
                TS('dve', kbg[R, nb, :], ps[0:BS, :], bcol(kbs), None, ALU.mult, None, [pk, 'kbs'], [('kbg', nb)])
                TS('dve', kgm[R, nb, :], ps[0:BS, :], bcol(kgs), None, ALU.mult, None, [pk, 'kgs'], [('kgm', nb)])
                ps, pk = PQ.next()
                TR(ps[0:BS, :], vs_[:, cs], ident, [('qkv', 2), 'ident'], [pk])
                TS('dve', vb[R, nb, :], ps[0:BS, :], bcol(beta), None, ALU.mult, None, [pk, 'beta'], [('vb', nb)])
                yield
                gb2, gb2k = m128.next()
                TS('dve', gb2[R, :], ones_f[R, :], bcol(g_tm), None, ALU.mult, None, ['g_tm'], [gb2k])
                pn, pnk = PQ.next()
                MM(pn[R, 0:BS], gb2[R, 0:BS], nminclT[R, 0:BS], True, True, [gb2k], [pnk])
                pg, pgk = PQ.next()
                MM(pg[:, 0:BS], gb2[R, :], minclT[R, 0:BS], True, True, [gb2k], [pgk])
                E, Ek = m128.next()
                TS('dve', E[R, 0:BS], pn[R, 0:BS], bcol(gc_tm), 0.0, ALU.add, ALU.min, [pnk, 'gc_tm'], [Ek])
                ACT(E[R, 0:BS], E[R, 0:BS], AF.Exp, [Ek], [Ek])
                ET, ETk = m128.next()
                TS('dve', ET[R, 0:BS], pg[R, 0:BS], bcol(ngc_tm), 0.0, ALU.add, ALU.min, [pgk, 'ngc_tm'], [ETk])
                ACT(ET[R, 0:BS], ET[R, 0:BS], AF.Exp, [ETk], [ETk])
                ACT(EG[:, cs], pg[:, 0:BS], AF.Exp, [pgk], [('EG', nb)])
                yield
                pa, pak = PQ.next()
                MM(pa[R, 0:BS], knb[:, cs], qnb[:, cs], True, True, ['qnb', 'knb'], [pak])
                af, afk = attnF.next()
                TT('dve', af[R, 0:BS], pa[R, 0:BS], ET[R, 0:BS], ALU.mult, [pak, ETk], [afk])
                TT('dve', attnT[R, nb, 0:BS], af[R, 0:BS], minclT[R, 0:BS], ALU.mult, [afk], [('attnT', nb)])
                TT('dve', qgT[:, cs], qn[:, cs], EG[:, cs], ALU.mult, [('qkv', 0), ('EG', nb)], [('qgT', nb)])
                yield
                pgm, pgmk = PQ.next()
                MM(pgm[R, 0:BS], knb[:, cs], kn2[:, cs], True, True, ['knb', 'kn2'], [pgmk])
                Am, Ak = m128.next()
                STT(Am[R, 0:BS], pgm[R, 0:BS], bcol(nbeta), E[R, 0:BS], ALU.mult, ALU.mult, [pgmk, 'nbeta', Ek], [Ak])
                TT('dve', Am[R, 0:BS], Am[R, 0:BS], mstrict[R, 0:BS], ALU.mult, [Ak], [Ak])
                pt, ptk = PQ.next()
                TR(pt[R, 0:BS], Am[R, 0:BS], ident[R, 0:BS], [Ak], [ptk])
                Bm, Bk = m128.next()
                CP('act', Bm[R, 0:BS], pt[R, 0:BS], [ptk], [Bk])
                yield
                QA, QAk = mb.next()
                CP('dve', QA[R, 0:BS], Am[R, 0:BS], [Ak], [QAk])
                QB, QBk = mb.next()
                CP('act', QB[R, 0:BS], Bm[R, 0:BS], [Bk], [QBk])
                Y, Yk = mb.next()
                TT('dve', Y[R, 0:BS], Bm[R, 0:BS], ident[R, 0:BS], ALU.add, [Bk], [Yk])
                for k in range(1, 6):
                    yield
                    if k < 5:
                        p1, p1k = PQ.next()
                        MM(p1[R, 0:BS], QA[R, 0:BS], QB[R, 0:BS], True, True, [QAk, QBk], [p1k])
                        QBn, QBnk = mb.next()
                        CP('act', QBn[R, 0:BS], p1[R, 0:BS], [p1k], [QBnk])
                    p2, p2k = PQ.next()
                    MM(p2[R, 0:BS], QB[R, 0:BS], QA[R, 0:BS], True, True, [QAk, QBk], [p2k])
                    QAn, QAnk = mb.next()
                    CP('act', QAn[R, 0:BS], p2[R, 0:BS], [p2k], [QAnk])
                    p3, p3k = PQ.next()
                    MM(p3[R, 0:BS], QAn[R, 0:BS], Y[R, 0:BS], True, True, [QAnk, Yk], [p3k])
                    Yn, Ynk = mb.next()
                    TT('dve', Yn[R, 0:BS], p3[R, 0:BS], Y[R, 0:BS], ALU.add, [p3k, Yk], [Ynk])
                    Y, Yk = Yn, Ynk
                    QA, QAk = QAn, QAnk
                    if k < 5:
                        QB, QBk = QBn, QBnk
                yield
                WT, WTk = m128.next()
                TT('dve', WT[R, 0:BS], ident[R, 0:BS], Am[R, 0:BS], ALU.subtract, [Ak], [WTk])
                Y0f, Y0k = m128.next()
                CP('dve', Y0f[R, 0:BS], Y[R, 0:BS], [Yk], [Y0k])
                px, pxk = PQ.next()
                TR(px[R, 0:BS], Y0f[R, 0:BS], ident[R, 0:BS], [Y0k], [pxk])
                Xm, Xk = m128.next()
                CP('act', Xm[R, 0:BS], px[R, 0:BS], [pxk], [Xk])
                yield
                pn1, pn1k = PQ.next()
                MM(pn1[R, 0:BS], WT[R, 0:BS], Y0f[R, 0:BS], True, True, [WTk, Y0k], [pn1k])
                Rm, Rmk = m128.next()
                TT('dve', Rm[R, 0:BS], ident[R, 0:BS], pn1[R, 0:BS], ALU.subtract, [pn1k], [Rmk])
                yield
                pn2, pn2k = PQ.next()
                MM(pn2[R, 0:BS], Xm[R, 0:BS], Rm[R, 0:BS], True, True, [Xk, Rmk], [pn2k])
                Y, Yk = m128.next()
                TT('dve', Y[R, 0:BS], pn2[R, 0:BS], Y0f[R, 0:BS], ALU.add, [pn2k, Y0k], [Yk])
                yield
                pu, puk = PQ.next()
                yb, ybk = Yb.next()
                CP('dve', yb[R, 0:BS], Y[R, 0:BS], [Yk], [ybk])
                MM(pu[R, :], yb[R, 0:BS], vb[R, nb, :], True, True, [ybk, ('vb', nb)], [puk])
                CP('act', u_[R, nb, :], pu[R, :], [puk], [('u', nb)])
                pw, pwk = PQ.next()
                MM(pw[:, 0:BS], kbg[R, nb, :], yb[R, 0:BS], True, True, [ybk, ('kbg', nb)], [pwk])
                CP('act', wT[:, cs], pw[:, 0:BS], [pwk], [('wT', nb)])
            run_rr([blk_gen(nb) for nb in range(NB)])
            po, pok = PSB.next()
            Sh = Sdn[:, h, :]
            Sk = ('Sdn', h)
            Sb = Sdb[:, h, :]
            Sbk = ('Sdb', h)
            CP('act', Sb, Sh, [Sk], [Sbk])
            for ci in range(NCHK):
                nb = ci // 2
                r0 = (ci % 2) * 64
                c0 = ci * 64
                RR = slice(r0, r0 + 64)
                p1, p1k = PQ.next()
                MM(p1[RR, :], wT[:, c0:c0 + 64], Sb, True, True, [('wT', nb), Sbk], [p1k])
                TT('dve', vnew[RR, nb, :], u_[RR, nb, :], p1[RR, :], ALU.subtract, [('u', nb), p1k], [('vnew', ci)])
                MM(po[:, c0:c0 + 64], Sb, qgT[:, c0:c0 + 64], True, False, [Sbk, ('qgT', nb)], [pok])
                MM(po[:, c0:c0 + 64], vnew[RR, nb, :], attnT[RR, nb, r0:r0 + 64], False, True,
                   [('vnew', ci), ('attnT', nb)], [pok])
                p2, p2k = PQ.next()
                MM(p2, kgm[RR, nb, :], vnew[RR, nb, :], True, True, [('kgm', nb), ('vnew', ci)], [p2k])
                STT(Sh, Sh, EG[:, c0 + 63:c0 + 64], p2, ALU.mult, ALU.add, [Sk, ('EG', nb), p2k], [Sk])
                if ci < NCHK - 1:
                    CP('act', Sb, Sh, [Sk], [Sbk])
            sq, sqk = sqb.next()
            ACT(sq, po[:, 0:T], AF.Square, [pok], [sqk])
            ps, pk = PSA.next()
            MM(ps[:, 0:T], ones_b, sq, True, True, [sqk], [pk])
            r, rk = rr.next()
            ACT(r, ps[:, 0:T], AF.Ln, [pk], [rk], bias=EPS, scale=1.0 / 128)
            ACT(r, r, AF.Exp, [rk], [rk], scale=-0.5)
            STT(og, po[:, 0:T], cvcol('dnorm', 0), r, ALU.mult, ALU.mult, [pok, rk], ['og'])
            TT('dve', ogT[:, h, :], og, zs, ALU.mult, ['og', 'zs'], [('ogT', h)])
        for n2 in range(2):
            slot, sk = W.acquire('dn_out')
            for i in range(4):
                n = 4 * n2 + i
                ps, pk = PSA.next()
                for hh in range(8):
                    o = i * 1024 + hh * 128
                    MM(ps[:, 0:T], slot[:, o:o + 128], ogT[:, hh, :], hh == 0, hh == 7, [sk, ('ogT', hh)], [pk])
                TT('dve', xT[:, n, 0:T], ps[:, 0:T], xT[:, n, 0:T], ALU.add, [pk, ('xT', n)], [('xT', n)])
            W.release()

    def sb_layer(T, grp, ti):
        S.phase = 'sb'
        S.barrier()
        AR.reset()
        NB = (T + 127) // 128
        BS = min(T, 128)
        t0 = ti * T
        PSB.banks = [2, 3, 4, 5]
        norm_to_hT(T, 'mixn', 16)
        qT = AR.alloc([8, T], BF16)
        kTb = AR.alloc([8, T], BF16)
        oT = AR.alloc([8, T], BF16)
        kst = Rot(2, [T], F32)
        vst = Rot(2, [512], F32)
        vbb = AR.alloc([NB, 1024], BF16)
        if grp == 'p':
            Stot = t0 + T
        else:
            Stot = PAST + T
        NBLK = (Stot + 127) // 128
        KTc = Rot(2, [Stot], BF16)
        Vc = Rot(2, [NBLK, 128], BF16)
        ebuf = Rot(8, [T], F32)
        spbuf = Rot(12, [T], BF16)
        abuf = Rot(8, [T], BF16)
        Sruns = [AR.alloc([T], F32) for _ in range(4)]
        Sbfs = [AR.alloc([T], BF16) for _ in range(4)]
        for n4 in range(4):
            slot, sk = W.acquire('sb_u')
            for i in range(4):
                cidx = 4 * n4 + i
                ps, pk = PSB.next()
                for kc in range(8):
                    o = i * 1024 + kc * 128
                    MM(ps[:, 0:T], slot[:, o:o + 128], hT[:, kc, 0:T], kc == 0, kc == 7, [sk, ('hT', kc)], [pk])
                if cidx < 8:
                    S.emit('act', (lambda o_, i_: (lambda e: e.mul(out=o_, in_=i_, mul=0.125)))(qT[:, cidx, :], ps[:, 0:T]),
                           [pk], [('qT', cidx)])
                else:
                    c = cidx - 8
                    st, stk = kst.next()
                    CP('act', st, ps[:, 0:T], [pk], [stk])
                    DMA('sp', kTo[grp][c * 128:(c + 1) * 128, t0:t0 + T], st, [stk], [], lane=('kst', stk[2]))
                    CP('dve', kTb[:, c, :], st, [stk], [('kTb', c)])
            W.release()
        if grp == 'p':
            DMA('sp', KTs[:, :, t0:t0 + T].rearrange("c p t -> p c t"), kTb, [('kTb', c) for c in range(8)], ['KTs'], lane='kts')
        for half in range(2):
            slot, sk = W.acquire('sb_v')
            for nb in range(NB):
                ps, pk = PSB.next()
                for kc in range(8):
                    MM(ps[0:BS, :], hT[:, kc, nb * 128:nb * 128 + BS], slot[:, kc * 512:(kc + 1) * 512], kc == 0, kc == 7,
                       [sk, ('hT', kc)], [pk])
                st, stk = vst.next()
                CP('act', st[0:BS, :], ps[0:BS, :], [pk], [stk])
                DMA('sp', vo[grp][t0 + nb * 128:t0 + nb * 128 + BS, half * 512:(half + 1) * 512], st[0:BS, :], [stk], [],
                    lane=('vst', stk[2]))
                CP('dve', vbb[0:BS, nb, half * 512:(half + 1) * 512], st[0:BS, :], [stk], [('vbb', nb, half)])
            W.release()
        VBB = [('vbb', nb, half) for nb in range(NB) for half in range(2)]
        if grp == 'p':
            for nb in range(NB):
                DMA('sp', Vs[:, :, t0 // 128 + nb, :].rearrange("c p n -> p c n"),
                    vbb[:, nb, :].rearrange("p (c n) -> p c n", c=8), VBB, ['Vs'], lane='vs', same_gen=(nb > 0))
        PO_BANKS = [0, 0, 1, 1]
        PSB.banks = [2, 3, 4, 5, 6, 7]
        for c0 in range(0, 8, 2):
          gens = []
          for c in (c0, c0 + 1):
            kt, ktk = KTc.next()
            vc, vck = Vc.next()
            bi = ktk[2]
            if grp == 'p':
                DMA('sp', kt[:, 0:Stot], KTs[c, :, 0:Stot], ['KTs'], [ktk], lane=('ktc', bi))
                DMA('sp', vc[:, 0:NBLK, :], Vs[c, :, 0:NBLK, :], ['Vs'], [vck], lane=('vc', bi))
            else:
                DMA('pool', kt[:, 0:PAST].rearrange("p (a b) -> p a b", b=1024), cacheKT[c * 128:(c + 1) * 128, :].rearrange("p (a b) -> p a b", b=1024), [], [ktk], lane=('ktcs', bi))
                CP('dve', kt[:, PAST:PAST + T], kTb[:, c, :], [('kTb', c)], [ktk])
                DMA('pool', vc[:, 0:16, :], cacheV[c], [], [vck], lane=('vcs', bi))
                CP('dve', vc[0:BS, 16, :], vbb[0:BS, 0, c * 128:(c + 1) * 128], VBB, [vck])
            def head_gen(hh, gi, kt=kt, ktk=ktk, vc=vc, vck=vck, c=c):
                P0 = hh * 64
                PR = slice(P0, P0 + 64)
                po, pok = psb[PO_BANKS[gi]], ('ps', PO_BANKS[gi])
                Srun, Sbf = Sruns[gi], Sbfs[gi]
                srk, sbk = ('Srun', gi), ('Sbf', gi)
                MS('pool', Srun, 0.0, [srk])
                MS('pool', Sbf, 0.0, [sbk])
                def stage1(jb):
                    bs = min(128, Stot - jb * 128)
                    if grp == 'p':
                        diag = jb >= t0 // 128
                        qlo = (jb - t0 // 128) * 128 if diag else 0
                    else:
                        diag = jb == NBLK - 1
                        qlo = 0
                    dcols = min(128, T - qlo)
                    QS = slice(qlo, T)
                    B_ = slice(0, bs)
                    pz, pzk = PSB.next()
                    MM(pz[B_, QS], kt[PR, jb * 128:jb * 128 + bs], qT[PR, c, QS], True, True, [ktk, ('qT', c)], [pzk])
                    e_, ek = ebuf.next()
                    ACT(e_[B_, QS], pz[B_, QS], AF.Exp, [pzk], [ek])
                    sp, spk = spbuf.next()
                    ACT(sp[B_, QS], e_[B_, QS], AF.Ln, [ek], [spk], bias=1.0)
                    if diag:
                        TT('dve', sp[B_, qlo:qlo + dcols], sp[B_, qlo:qlo + dcols], mask_lt[B_, 0:dcols], ALU.mult, [spk], [spk])
                    return dict(jb=jb, bs=bs, diag=diag, qlo=qlo, dcols=dcols, QS=QS, B_=B_, sp=sp, spk=spk)

                units = list(range(NBLK - 1, -1, -1))
                ctx = stage1(units[0])
                yield
                for ui in range(len(units)):
                    first = ui == 0
                    nxt = stage1(units[ui + 1]) if ui + 1 < len(units) else None
                    yield
                    jb, bs, diag, qlo, dcols, QS, B_, sp, spk = (ctx[n] for n in ('jb', 'bs', 'diag', 'qlo', 'dcols', 'QS', 'B_', 'sp', 'spk'))
                    pz, pzk = PSB.next()
                    MM(pz[B_, QS], kt[PR, jb * 128:jb * 128 + bs], qT[PR, c, QS], True, False, [ktk, ('qT', c)], [pzk])
                    yield
                    MM(pz[B_, QS], negUI[B_, B_], sp[B_, QS], False, first, [spk], [pzk])
                    if not first:
                        MM(pz[B_, QS], nones_b[:, B_], Sbf[:, QS], False, True, [sbk], [pzk])
                    a_, ak = abuf.next()
                    if diag and qlo > 0:
                        MS('pool', a_[B_, 0:qlo], 0.0, [ak])
                    ACT(a_[B_, QS], pz[B_, QS], AF.Exp, [pzk], [ak])
                    if diag:
                        TT('dve', a_[B_, qlo:qlo + dcols], a_[B_, qlo:qlo + dcols], mask_lt[B_, 0:dcols], ALU.mult, [ak], [ak])
                    if jb > 0:
                        TT('dve', Srun[B_, QS], Srun[B_, QS], sp[B_, QS], ALU.add, [srk, spk], [srk])
                        CP('dve', Sbf[B_, QS], Srun[B_, QS], [srk], [sbk])
                    yield
                    MM(po[PR, 0:T], vc[B_, jb, hh * 64:(hh + 1) * 64], a_[B_, 0:T], first, jb == 0, [vck, ak], [pok])
                    yield
                    ctx = nxt
                CP('act', oT[PR, c, :], po[PR, 0:T], [pok], [('oT', c)])
            gens.append(head_gen(0, len(gens)))
            gens.append(head_gen(1, len(gens)))
          run_rr(gens)
        for n2 in range(2):
            slot, sk = W.acquire('sb_out')
            for i in range(4):
                n = 4 * n2 + i
                ps, pk = PSA.next()
                for cc in range(8):
                    o = i * 1024 + cc * 128
                    MM(ps[:, 0:T], slot[:, o:o + 128], oT[:, cc, :], cc == 0, cc == 7, [sk, ('oT', cc)], [pk])
                TT('dve', xT[:, n, 0:T], ps[:, 0:T], xT[:, n, 0:T], ALU.add, [pk, ('xT', n)], [('xT', n)])
            W.release()

    def final_layer(T, grp, ti):
        S.phase = 'final'
        S.barrier()
        AR.reset()
        t0 = ti * T
        yst = AR.alloc([8, T], F32)
        rmsnorm(T, 'finn', 0, lambda c: yst[:, c, :], lambda c: ('yst', c))
        DMA('sp', yT[grp].rearrange("(c p) t -> p c t", p=128)[:, :, t0:t0 + T], yst, [('yst', c) for c in range(8)], [],
            lane='yst')

    XT_ALL = [('xT', c) for c in range(8)]
    HIST = [('phist', 0), ('phist', 1), 'chist'] + [('Sdn', h) for h in range(8)] + [('ghist', l) for l in range(4)]

    def run_tile(grp, ti, T):
        t0 = ti * T
        src = xTp if grp == 'p' else xTs
        DMA('sp', xT[:, :, 0:T], src.rearrange("(c p) t -> p c t", p=128)[:, :, t0:t0 + T], [], XT_ALL, lane='xin')
        def skip(n):
            for _ in range(n):
                W.acquire(None)
                W.release()
        LY = layers if layers is not None else {'p0', 'f0', 'dn', 'f1', 'sb', 'f2', 'p3', 'f3'}
        fst = grp == 'p' and ti == 0
        pool_layer(T, 0, 0, fst) if 'p0' in LY else skip(1)
        ffn_layer(T, 0) if 'f0' in LY else skip(19)
        dn_layer(T) if 'dn' in LY else skip(11)
        ffn_layer(T, 1) if 'f1' in LY else skip(19)
        sb_layer(T, grp, ti) if 'sb' in LY else skip(8)
        ffn_layer(T, 2) if 'f2' in LY else skip(19)
        pool_layer(T, 1, 3, fst) if 'p3' in LY else skip(1)
        ffn_layer(T, 3) if 'f3' in LY else skip(19)
        final_layer(T, grp, ti)

    def store_states(grp):
        S.barrier()
        DMA('sp', o_pool[grp].rearrange("l p c t -> p l c t"), poolhist, [('phist', 0), ('phist', 1)], [], lane='st0')
        DMA('sp', o_dnc[grp], chist, ['chist'], [], lane='st1')
        DMA('sp', o_dn[grp], Sdn, [('Sdn', h) for h in range(8)], [], lane='st2')
        DMA('sp', o_ffn[grp].rearrange("l p c t -> p l c t"), ghist, [('ghist', l) for l in range(4)], [], lane='st3')

    MS('pool', poolhist, 0.0, [('phist', 0), ('phist', 1)])
    MS('pool', ghist, 0.0, [('ghist', l) for l in range(4)])
    MS('pool', chist, 0.0, ['chist'])
    MS('pool', Sdn, 0.0, [('Sdn', h) for h in range(8)])
    for ti in range(n_tiles):
        run_tile('p', ti, 512)
    store_states('p')
    if do_sample:
        S.barrier()
        DMA('sp', poolhist, st_pool.rearrange("l p c t -> p l c t"), [], [('phist', 0), ('phist', 1)], lane='ld0')
        DMA('sp', chist, st_dnc, [], ['chist'], lane='ld1')
        DMA('sp', Sdn, st_dn, [], [('Sdn', h) for h in range(8)], lane='ld2')
        DMA('sp', ghist, st_ffn.rearrange("l p c t -> p l c t"), [], [('ghist', l) for l in range(4)], lane='ld3')
        run_tile('s', 0, 64)
        store_states('s')
    S.finish()
    S.replay(nc)
    return nc, S


_CACHE = {}
_LAST = None
_DBG = None


def kernel(x_prompt, x_sample, state_pool, state_dn_conv, state_dn, cache_sb_k, cache_sb_v, state_ffn_conv,
           mix_norm, ffn_norm, final_norm, pool_w, pool_scale, dn_w_in, dn_conv_w, dn_a_log, dn_dt_bias,
           dn_norm, dn_w_out, sb_w_qkv, sb_w_out, ffn_w_up, ffn_conv_w, ffn_conv_b, ffn_w_down,
           _n_tiles=8, _do_sample=True, _layers=None):
    f = lambda a: np.asarray(a, np.float32)
    w = dict(mix_norm=f(mix_norm), ffn_norm=f(ffn_norm), final_norm=f(final_norm), pool_w=f(pool_w),
             pool_scale=f(pool_scale), dn_w_in=f(dn_w_in), dn_conv_w=f(dn_conv_w), dn_a_log=f(dn_a_log),
             dn_dt_bias=f(dn_dt_bias), dn_norm=f(dn_norm), dn_w_out=f(dn_w_out), sb_w_qkv=f(sb_w_qkv),
             sb_w_out=f(sb_w_out), ffn_w_up=f(ffn_w_up), ffn_conv_w=f(ffn_conv_w), ffn_conv_b=f(ffn_conv_b),
             ffn_w_down=f(ffn_w_down))
    wpack = pack_weights(w)
    cvec = pack_cvec(w)
    x_prompt = f(x_prompt); x_sample = f(x_sample); state_pool = f(state_pool); state_dn_conv = f(state_dn_conv)
    state_dn = f(state_dn); cache_sb_k = f(cache_sb_k); cache_sb_v = f(cache_sb_v); state_ffn_conv = f(state_ffn_conv)
    B = 8
    in_maps = []
    for b in range(B):
        m = {
            "xTp": np.ascontiguousarray(x_prompt[b].T),
            "xTs": np.ascontiguousarray(x_sample[b].T),
            "wpack": wpack,
            "cvec": cvec,
            "st_pool": np.ascontiguousarray(state_pool[:, b].reshape(2, 15, 8, 128).transpose(0, 3, 2, 1)),
            "st_dnc": np.ascontiguousarray(state_dn_conv[0, b].reshape(3, 24, 128).transpose(2, 1, 0)),
            "st_dn": np.ascontiguousarray(state_dn[0, b].transpose(1, 0, 2)),
            "st_ffn": np.ascontiguousarray(state_ffn_conv[:, b].reshape(4, 2, NFC, 128).transpose(0, 3, 2, 1)),
            "cacheKT": np.ascontiguousarray(cache_sb_k[0, b].reshape(PAST, D).T),
            "cacheV": np.ascontiguousarray(cache_sb_v[0, b].reshape(16, 128, 8, 128).transpose(2, 1, 0, 3)),
        }
        in_maps.append(m)
    key = (_n_tiles, _do_sample, None if _layers is None else tuple(sorted(_layers)))
    if key not in _CACHE:
        _CACHE[key] = build_program(_n_tiles, _do_sample, _layers)[0]
    nc = _CACHE[key]
    res = run_bass_kernel_spmd(nc, in_maps, core_ids=list(range(B)))
    R = res.results
    global _LAST
    _LAST = R

    def st(name, fn):
        return np.stack([fn(np.asarray(R[b][name])) for b in range(B)], axis=0)
    y_p = st("yTp", lambda a: a.T)
    y_s = st("yTs", lambda a: a.T)
    pool_p = st("o_poolp", lambda a: a.transpose(0, 3, 2, 1).reshape(2, 15, D)).transpose(1, 0, 2, 3)
    pool_s = st("o_pools", lambda a: a.transpose(0, 3, 2, 1).reshape(2, 15, D)).transpose(1, 0, 2, 3)
    dnc_p = st("o_dncp", lambda a: a.transpose(2, 1, 0).reshape(3, 3072))[None]
    dnc_s = st("o_dncs", lambda a: a.transpose(2, 1, 0).reshape(3, 3072))[None]
    dn_p = st("o_dnp", lambda a: a.transpose(1, 0, 2))[None]
    dn_s = st("o_dns", lambda a: a.transpose(1, 0, 2))[None]
    k_p = st("kTp", lambda a: a.T.reshape(SEQ, 16, 64))[None]
    k_s = st("kTs", lambda a: a.T.reshape(DSEQ, 16, 64))[None]
    v_p = st("vp", lambda a: a.reshape(SEQ, 16, 64))[None]
    v_s = st("vs", lambda a: a.reshape(DSEQ, 16, 64))[None]
    ffn_p = st("o_ffnp", lambda a: a.transpose(0, 3, 2, 1).reshape(4, 2, DFF)).transpose(1, 0, 2, 3)
    ffn_s = st("o_ffns", lambda a: a.transpose(0, 3, 2, 1).reshape(4, 2, DFF)).transpose(1, 0, 2, 3)
    outs = (y_p, y_s, pool_p, pool_s, dnc_p, dnc_s, dn_p, dn_s, k_p, k_s, v_p, v_s, ffn_p, ffn_s)
    return tuple(np.ascontiguousarray(o, dtype=np.float32) for o in outs)
```

```python
import bisect
import contextlib
import numpy as np
import concourse.bass as bass
import concourse.mybir as mybir
from concourse.bass_utils import run_bass_kernel_spmd

F32 = mybir.dt.float32
BF16 = mybir.dt.bfloat16
AF = mybir.ActivationFunctionType
ALU = mybir.AluOpType

D = 1024
SEQ = 4096
DSEQ = 64
PAST = 2048
DFF = 2816
NFC = 22
EPS = 1e-6
SLOT = 4096
NSLOT = 4
SAME_ENGINE_SYNC = True


class Sched:
    ENGS = ['pe', 'act', 'dve', 'pool', 'sp']

    def __init__(self):
        self.streams = {e: [] for e in self.ENGS}
        self.cnt = {e: 0 for e in self.ENGS}
        self.lane_cnt = {}
        self.seen = {e: {} for e in self.ENGS}
        self.snaps = {}
        self.res = {}
        self.nops = 0
        self.phase = 'setup'
        self.tags = {e: [] for e in self.ENGS}

    def _snap_put(self, key, val, d):
        vals, dicts = self.snaps.setdefault(key, ([], []))
        if dicts and dicts[-1] is d:
            return
        vals.append(val)
        dicts.append(d)

    def _snap_get(self, key, val):
        if key not in self.snaps:
            return None
        vals, dicts = self.snaps[key]
        i = bisect.bisect_right(vals, val) - 1
        return dicts[i] if i >= 0 else None

    def _wait(self, eng, key, val):
        seen = self.seen[eng]
        if seen.get(key, 0) >= val:
            return
        if key == ('eng', eng) and (eng == 'pe' or not SAME_ENGINE_SYNC):
            return
        self.streams[eng].append(('wait', key, val))
        new = dict(seen)
        new[key] = val
        sn = self._snap_get(key, val)
        if sn:
            for k, v in sn.items():
                if new.get(k, 0) < v:
                    new[k] = v
        self.seen[eng] = new

    def emit(self, eng, fn, reads=(), writes=(), lane=None, same_gen=False):
        deps = {}
        for r in reads:
            ent = self.res.get(r)
            if ent and ent[0]:
                k, v = ent[0]
                if deps.get(k, 0) < v:
                    deps[k] = v
            if ent and isinstance(r, tuple) and r[0] == 'ps':
                for k, v in ent[1].items():
                    if k != ('eng', eng) and deps.get(k, 0) < v:
                        deps[k] = v
        for w in writes:
            ent = self.res.get(w)
            if ent:
                if ent[0]:
                    k, v = ent[0]
                    if deps.get(k, 0) < v:
                        deps[k] = v
                for k, v in ent[1].items():
                    if deps.get(k, 0) < v:
                        deps[k] = v
        for k, v in deps.items():
            self._wait(eng, k, v)
        if lane is not None:
            key = ('lane', lane)
            cur = self.lane_cnt.get(lane, 0)
            if cur and not same_gen:
                self._wait(eng, key, cur)
            cur += 16
            self.lane_cnt[lane] = cur
            ref = (key, cur)
            self.streams[eng].append(('dma', fn, lane))
        else:
            key = ('eng', eng)
            self.cnt[eng] += 1
            ref = (key, self.cnt[eng])
            self.streams[eng].append(('op', fn))
            self.tags[eng].append(self.phase)
        self._snap_put(key, ref[1], self.seen[eng])
        for r in reads:
            ent = self.res.setdefault(r, [None, {}])
            if ent[1].get(key, 0) < ref[1]:
                ent[1][key] = ref[1]
        for w in writes:
            self.res[w] = [ref, {}]
        self.nops += 1
        return ref

    def barrier(self):
        for e in self.ENGS:
            for f in self.ENGS:
                if self.cnt[f]:
                    self._wait(e, ('eng', f), self.cnt[f])
            for l, c in self.lane_cnt.items():
                self._wait(e, ('lane', l), c)

    def finish(self):
        for l, c in self.lane_cnt.items():
            self._wait('sp', ('lane', l), c)
        for f in self.ENGS:
            if self.cnt[f]:
                self._wait('sp', ('eng', f), self.cnt[f])

    def replay(self, nc):
        with contextlib.ExitStack() as es:
            sems = {}
            for e in self.ENGS:
                if self.cnt[e]:
                    sems[('eng', e)] = es.enter_context(nc.semaphore("s_" + e))
            for i, l in enumerate(self.lane_cnt):
                sems[('lane', l)] = es.enter_context(nc.semaphore("l_%d" % i))
            block = es.enter_context(nc.Block())

            def run(engname):
                def body(eng):
                    for it in self.streams[engname]:
                        if it[0] == 'wait':
                            eng.wait_ge(sems[it[1]], it[2])
                        elif it[0] == 'op':
                            it[1](eng).then_inc(sems[('eng', engname)], 1)
                        else:
                            it[1](eng).then_inc(sems[('lane', it[2])], 16)
                return body
            block.tensor(run('pe'))
            block.scalar(run('act'))
            block.vector(run('dve'))
            block.gpsimd(run('pool'))
            block.sync(run('sp'))


def _unit(W, col0, ncol=128):
    K = W.shape[0]
    return np.ascontiguousarray(
        W[:, col0:col0 + ncol].reshape(K // 128, 128, ncol).transpose(1, 0, 2)).reshape(128, -1)


def chunk_plan():
    plan = []

    def ffn(l):
        for cc in range(NFC // 2):
            plan.append([('up', l, 2 * cc, 0), ('up', l, 2 * cc, 1), ('up', l, 2 * cc + 1, 0), ('up', l, 2 * cc + 1, 1)])
        for j in range(8):
            plan.append([('down', l, j)])
    plan.append([('pool', 0)])
    ffn(0)
    plan.append([('dn_ab',)])
    for h in range(8):
        plan.append([('dn_u', h * 128), ('dn_u', 1024 + h * 128), ('dn_u', 2048 + h * 128), ('dn_u', 3072 + h * 128)])
    for n in range(2):
        plan.append([('dn_out', 4 * n + i) for i in range(4)])
    ffn(1)
    for n in range(4):
        plan.append([('sb_u', (4 * n + i) * 128) for i in range(4)])
    plan.append([('sb_v', 0)])
    plan.append([('sb_v', 1)])
    for n in range(2):
        plan.append([('sb_out', 4 * n + i) for i in range(4)])
    ffn(2)
    plan.append([('pool', 1)])
    ffn(3)
    return plan


def piece_size(p):
    k = p[0]
    if k == 'pool':
        return 2048
    if k == 'down':
        return DFF
    if k == 'dn_ab':
        return 128
    if k == 'sb_v':
        return 4096
    return 1024


def chunk_offsets():
    plan = chunk_plan()
    offs = []
    o = 0
    for ch in plan:
        used = sum(piece_size(p) for p in ch)
        offs.append((o, used))
        o += used
    tot = ((o + 1023) // 1024) * 1024
    return plan, offs, tot


def piece_data(p, w):
    k = p[0]
    if k == 'pool':
        a = w['pool_w'][p[1]].reshape(4, 2, 128, 2, 128)
        return np.ascontiguousarray(a.transpose(2, 0, 3, 1, 4)).reshape(128, 2048)
    if k == 'up':
        return _unit(w['ffn_w_up'][p[1]], p[3] * DFF + p[2] * 128)
    if k == 'down':
        return _unit(w['ffn_w_down'][p[1]], p[2] * 128)
    if k == 'dn_ab':
        return _unit(w['dn_w_in'][0], 4096, 16)
    if k == 'dn_u':
        return _unit(w['dn_w_in'][0], p[1])
    if k == 'dn_out':
        return _unit(w['dn_w_out'][0], p[1] * 128)
    if k == 'sb_u':
        return _unit(w['sb_w_qkv'][0], p[1])
    if k == 'sb_v':
        return _unit(w['sb_w_qkv'][0], 2048 + p[1] * 512, 512)
    if k == 'sb_out':
        return _unit(w['sb_w_out'][0], p[1] * 128)
    raise KeyError(k)


def pack_weights(w):
    plan, offs, tot = chunk_offsets()
    out = np.zeros((128, tot), np.float32)
    for ch, (o, used) in zip(plan, offs):
        for p in ch:
            n = piece_size(p)
            out[:, o:o + n] = piece_data(p, w)
            o += n
    return out


CV = {}
_o = 0
for _name, _n in [('mixn', 32), ('ffnn', 32), ('finn', 8), ('pscale', 16), ('fcw', 4 * NFC * 3), ('fcb', 4 * NFC),
                  ('dcw', 96), ('dnorm', 1), ('alog', 8), ('dtb', 8)]:
    CV[_name] = _o
    _o += _n
NCV = _o


def pack_cvec(w):
    cv = np.zeros((128, NCV), np.float32)

    def fm(v):
        return np.asarray(v, np.float32).reshape(-1, 128).T
    for l in range(4):
        cv[:, CV['mixn'] + 8 * l: CV['mixn'] + 8 * l + 8] = fm(w['mix_norm'][l])
        cv[:, CV['ffnn'] + 8 * l: CV['ffnn'] + 8 * l + 8] = fm(w['ffn_norm'][l])
        a = np.asarray(w['ffn_conv_w'][l], np.float32)
        a = a.reshape(3, NFC, 128).transpose(2, 1, 0)
        cv[:, CV['fcw'] + l * NFC * 3: CV['fcw'] + (l + 1) * NFC * 3] = a.reshape(128, NFC * 3)
        cv[:, CV['fcb'] + l * NFC: CV['fcb'] + (l + 1) * NFC] = fm(w['ffn_conv_b'][l])
    cv[:, CV['finn']:CV['finn'] + 8] = fm(w['final_norm'])
    for j in range(2):
        cv[:, CV['pscale'] + 8 * j: CV['pscale'] + 8 * j + 8] = fm(w['pool_scale'][j])
    a = np.asarray(w['dn_conv_w'][0], np.float32).reshape(4, 24, 128).transpose(2, 1, 0)
    cv[:, CV['dcw']:CV['dcw'] + 96] = a.reshape(128, 96)
    cv[:, CV['dnorm']] = np.asarray(w['dn_norm'][0], np.float32)
    cv[:, CV['alog']:CV['alog'] + 8] = np.asarray(w['dn_a_log'][0], np.float32)[None, :]
    cv[:, CV['dtb']:CV['dtb'] + 8] = np.asarray(w['dn_dt_bias'][0], np.float32)[None, :]
    return cv


def build_program(n_tiles=8, do_sample=True, layers=None):
    nc = bass.Bass("TRN2", target_bir_lowering=False)
    S = Sched()
    plan, offs, WTOT = chunk_offsets()
    NCH = len(plan)

    def din(name, shape):
        return nc.dram_tensor(name, shape, F32, kind="ExternalInput").ap()

    def dout(name, shape):
        return nc.dram_tensor(name, shape, F32, kind="ExternalOutput").ap()

    xTp = din("xTp", [D, SEQ])
    xTs = din("xTs", [D, DSEQ])
    wpack = din("wpack", [128, WTOT])
    cvec_d = din("cvec", [128, NCV])
    st_pool = din("st_pool", [2, 128, 8, 15])
    st_dnc = din("st_dnc", [128, 24, 3])
    st_dn = din("st_dn", [128, 8, 128])
    st_ffn = din("st_ffn", [4, 128, NFC, 2])
    cacheKT = din("cacheKT", [D, PAST])
    cacheV = din("cacheV", [8, 128, 16, 128])

    yT = {'p': dout("yTp", [D, SEQ]), 's': dout("yTs", [D, DSEQ])}
    o_pool = {'p': dout("o_poolp", [2, 128, 8, 15]), 's': dout("o_pools", [2, 128, 8, 15])}
    o_dnc = {'p': dout("o_dncp", [128, 24, 3]), 's': dout("o_dncs", [128, 24, 3])}
    o_dn = {'p': dout("o_dnp", [128, 8, 128]), 's': dout("o_dns", [128, 8, 128])}
    kTo = {'p': dout("kTp", [D, SEQ]), 's': dout("kTs", [D, DSEQ])}
    vo = {'p': dout("vp", [SEQ, D]), 's': dout("vs", [DSEQ, D])}
    o_ffn = {'p': dout("o_ffnp", [4, 128, NFC, 2]), 's': dout("o_ffns", [4, 128, NFC, 2])}

    wbf = nc.dram_tensor("wbf", [128, WTOT], BF16).ap()
    KTs = nc.dram_tensor("KTs", [8, 128, SEQ], BF16).ap()
    Vs = nc.dram_tensor("Vs", [8, 128, 32, 128], BF16).ap()

    def sb(name, shape, dt):
        return nc.alloc_sbuf_tensor(name, shape, dt).ap()
    xT = sb("xT", [128, 8, 512], F32)
    hT = sb("hT", [128, 8, 512], BF16)
    wring = sb("wring", [128, NSLOT, SLOT], BF16)
    cv = sb("cv", [128, NCV], F32)
    ident = sb("ident", [128, 128], F32)
    ones_f = sb("ones_f", [128, 128], F32)
    minclT = sb("minclT", [128, 128], F32)
    nminclT = sb("nminclT", [128, 128], F32)
    mstrict = sb("mstrict", [128, 128], F32)
    blockones = sb("blockones", [128, 128], F32)
    ones_b = sb("ones_b", [128, 128], BF16)
    nones_b = sb("nones_b", [128, 128], BF16)
    negUI = sb("negUI", [128, 128], BF16)
    mask_lt = sb("mask_lt", [128, 128], BF16)
    invc = sb("invc", [128, 4, 15], F32)
    nexpA = sb("nexpA", [128, 8], F32)
    poolhist = sb("poolhist", [128, 2, 8, 15], F32)
    ghist = sb("ghist", [128, 4, NFC, 2], F32)
    chist = sb("chist", [128, 24, 3], F32)
    Sdn = sb("Sdn", [128, 8, 128], F32)
    Sdb = sb("Sdb", [128, 8, 128], BF16)
    nsq = sb("nsq", [128, 2, 512], BF16)
    nrt = sb("nrt", [128, 512], F32)
    nrstd = sb("nrstd", [128, 512], F32)
    ARENA = 30 * 1024
    arena = sb("arena", [128, ARENA], F32)
    psb = [nc.alloc_psum_tensor("ps%d" % i, [128, 512], F32).ap() for i in range(8)]

    def MM(out, lhsT, rhs, start=True, stop=True, rd=(), wr=()):
        S.emit('pe', lambda e: e.matmul(out, lhsT=lhsT, rhs=rhs, start=start, stop=stop), rd, wr)
        if lhsT.dtype == F32:
            S.tags['pe'].append(S.phase)

    def TR(out, in_, idn, rd=(), wr=()):
        S.emit('pe', lambda e: e.transpose(out, in_, idn), rd, wr)

    def ACT(out, in_, func, rd=(), wr=(), bias=None, scale=None):
        kw = {}
        if bias is not None:
            kw['bias'] = bias
        if scale is not None:
            kw['scale'] = scale
        S.emit('act', lambda e: e.activation(out=out, in_=in_, func=func, **kw), rd, wr)

    def TT(eng, out, in0, in1, op, rd=(), wr=()):
        S.emit(eng, lambda e: e.tensor_tensor(out=out, in0=in0, in1=in1, op=op), rd, wr)

    def TS(eng, out, in0, s1, s2, op0, op1=None, rd=(), wr=()):
        if op1 is None:
            S.emit(eng, lambda e: e.tensor_scalar(out=out, in0=in0, scalar1=s1, scalar2=None, op0=op0), rd, wr)
        else:
            S.emit(eng, lambda e: e.tensor_scalar(out=out, in0=in0, scalar1=s1, scalar2=s2, op0=op0, op1=op1), rd, wr)

    def STT(out, in0, sc, in1, op0, op1, rd=(), wr=()):
        S.emit('dve', lambda e: e.scalar_tensor_tensor(out=out, in0=in0, scalar=sc, in1=in1, op0=op0, op1=op1), rd, wr)

    def CP(eng, out, in_, rd=(), wr=()):
        if eng == 'act':
            S.emit('act', lambda e: e.copy(out=out, in_=in_), rd, wr)
        else:
            S.emit(eng, lambda e: e.tensor_copy(out=out, in_=in_), rd, wr)

    def MS(eng, ap, val, wr=()):
        S.emit(eng, lambda e: e.memset(ap, val), (), wr)

    def RCP(out, in_, rd=(), wr=()):
        S.emit('dve', lambda e: e.reciprocal(out=out, in_=in_), rd, wr)

    def DMA(eng, out, in_, rd=(), wr=(), lane=None, same_gen=False):
        S.emit(eng, lambda e: e.dma_start(out=out, in_=in_), rd, wr, lane=lane, same_gen=same_gen)

    def ASEL(out, in_, pattern, cmp, fill, base, cm, rd=(), wr=()):
        S.emit('pool', lambda e: e.affine_select(out=out, in_=in_, pattern=pattern, compare_op=cmp, fill=fill,
                                                 base=base, channel_multiplier=cm), rd, wr)

    class Arena:
        def __init__(self):
            self.off = 0

        def reset(self):
            self.off = 0

        def alloc(self, shape, dt):
            n = int(np.prod(shape))
            n4 = n if dt == F32 else (n + 1) // 2
            n4 = (n4 + 7) // 8 * 8
            assert self.off + n4 <= ARENA, ("arena overflow", self.off, n4)
            v = arena[:, self.off:self.off + n4]
            self.off += n4
            if dt != F32:
                v = v.bitcast(dt)
            v = v[:, 0:n]
            if len(shape) == 2:
                return v.rearrange("p (a b) -> p a b", a=shape[0])
            if len(shape) == 3:
                return v.rearrange("p (a b c) -> p a b c", a=shape[0], b=shape[1])
            return v
    AR = Arena()

    class Rot:
        cnt = [0]

        def __init__(self, n, shape, dt):
            Rot.cnt[0] += 1
            self.id = Rot.cnt[0]
            self.bufs = [AR.alloc(shape, dt) for _ in range(n)]
            self.i = 0

        def next(self):
            k = self.i % len(self.bufs)
            self.i += 1
            return self.bufs[k], ('rot', self.id, k)

    class PSRot:
        def __init__(self, banks):
            self.banks = banks
            self.i = 0

        def next(self):
            b = self.banks[self.i % len(self.banks)]
            self.i += 1
            return psb[b], ('ps', b)
    PSA = PSRot([0, 1])
    PSB = PSRot([2, 3, 4, 5])

    class PSQ:
        def __init__(self):
            self.i = 0
            self.banks = [6, 7]

        def next(self):
            b = self.banks[self.i % len(self.banks)]
            self.i += 1
            return psb[b][:, 0:128], ('ps', b)
    PQ = PSQ()

    passes = n_tiles + (1 if do_sample else 0)
    CASTW = 32 * 1024
    ncast = (WTOT + CASTW - 1) // CASTW

    class WStream:
        def __init__(self):
            self.seq = [i % NCH for i in range(passes * NCH)]
            self.nl = 0
            self.na = 0

        def _load(self):
            if self.nl >= len(self.seq):
                return
            ci = self.seq[self.nl]
            s = self.nl % NSLOT
            off, used = offs[ci]
            rd = [('wbf', k) for k in range(off // CASTW, (off + used - 1) // CASTW + 1)]
            DMA('sp', wring[:, s, 0:used], wbf[:, off:off + used], rd=rd, wr=[('w', s)], lane=('w', s))
            self.nl += 1

        def prime(self):
            for _ in range(NSLOT):
                self._load()

        def acquire(self, expect):
            ci = self.seq[self.na]
            assert expect is None or plan[ci][0][0] == expect, (plan[ci], expect)
            s = self.na % NSLOT
            self.na += 1
            return wring[:, s, :], ('w', s)

        def release(self):
            self._load()
    W = WStream()

    DMA('sp', cv, cvec_d, wr=['cv'], lane='cv')
    for k in range(ncast):
        c0 = k * CASTW
        c1 = min(WTOT, c0 + CASTW)
        DMA('pool', wbf[:, c0:c1].rearrange("p (a b) -> p a b", b=1024),
            wpack[:, c0:c1].rearrange("p (a b) -> p a b", b=1024), wr=[('wbf', k)], lane='cast')
    MS('dve', ident, 0.0, ['ident'])
    ASEL(ident, ident, [[-1, 128]], ALU.not_equal, 1.0, 0, 1, ['ident'], ['ident'])
    MS('dve', ones_f, 1.0, ['ones_f'])
    MS('dve', ones_b, 1.0, ['ones_b'])
    MS('dve', nones_b, -1.0, ['nones_b'])
    MS('dve', mask_lt, 1.0, ['mask_lt'])
    ASEL(mask_lt, mask_lt, [[1, 128]], ALU.is_gt, 0.0, 0, -1, ['mask_lt'], ['mask_lt'])
    MS('dve', negUI, -1.0, ['negUI'])
    ASEL(negUI, negUI, [[-1, 128]], ALU.is_ge, 0.0, 0, 1, ['negUI'], ['negUI'])
    MS('dve', minclT, 1.0, ['minclT'])
    ASEL(minclT, minclT, [[1, 128]], ALU.is_ge, 0.0, 0, -1, ['minclT'], ['minclT'])
    MS('pool', minclT[0:64, 64:128], 0.0, ['minclT'])
    MS('dve', nminclT, -1.0, ['nminclT'])
    ASEL(nminclT, nminclT, [[1, 128]], ALU.is_ge, 0.0, 0, -1, ['nminclT'], ['nminclT'])
    MS('pool', nminclT[0:64, 64:128], 0.0, ['nminclT'])
    MS('dve', mstrict, 1.0, ['mstrict'])
    ASEL(mstrict, mstrict, [[-1, 128]], ALU.is_gt, 0.0, 0, 1, ['mstrict'], ['mstrict'])
    MS('pool', mstrict[64:128, 0:64], 0.0, ['mstrict'])
    MS('dve', blockones, 1.0, ['blockones'])
    MS('pool', blockones[0:64, 64:128], 0.0, ['blockones'])
    MS('pool', blockones[64:128, 0:64], 0.0, ['blockones'])
    for g in range(4):
        w_ = 2 ** (g + 1)
        MS('dve', invc[:, g, :], 1.0 / w_, ['invc'])
        for t in range(w_ - 1):
            MS('dve', invc[:, g, t:t + 1], 1.0 / (t + 1), ['invc'])
    ACT(nexpA, cv[:, CV['alog']:CV['alog'] + 8], AF.Exp, ['cv'], ['nexpA'])
    TS('dve', nexpA, nexpA, -1.0, None, ALU.mult, None, ['nexpA'], ['nexpA'])
    W.prime()
    S.barrier()

    def rr_gen(gens):
        gens = list(gens)
        while gens:
            for g in list(gens):
                try:
                    next(g)
                except StopIteration:
                    gens.remove(g)
            yield

    def run_rr(gens):
        gens = list(gens)
        while gens:
            for g in list(gens):
                try:
                    next(g)
                except StopIteration:
                    gens.remove(g)

    def cvcol(name, idx):
        o = CV[name] + idx
        return cv[:, o:o + 1]

    def rmsnorm(T, gname, gbase, out_fn, out_keys, out_dt_is_f32=False):
        ps, pk = PSA.next()
        for c in range(8):
            b = c % 2
            ACT(nsq[:, b, 0:T], xT[:, c, 0:T], AF.Square, [('xT', c)], [('nsq', b)])
            MM(ps[:, 0:T], ones_b, nsq[:, b, 0:T], c == 0, c == 7, [('nsq', b)], [pk])
        ACT(nrt[:, 0:T], ps[:, 0:T], AF.Ln, [pk], ['nrt'], bias=EPS, scale=1.0 / D)
        ACT(nrstd[:, 0:T], nrt[:, 0:T], AF.Exp, ['nrt'], ['nrstd'], scale=-0.5)
        for c in range(8):
            STT(out_fn(c), xT[:, c, 0:T], cvcol(gname, gbase + c), nrstd[:, 0:T], ALU.mult, ALU.mult,
                [('xT', c), 'nrstd'], [out_keys(c)])

    def norm_to_hT(T, gname, gbase):
        rmsnorm(T, gname, gbase, lambda c: hT[:, c, 0:T], lambda c: ('hT', c))

    HT_ALL = [('hT', c) for c in range(8)]

    def pool_layer(T, j, l, first):
        S.phase = 'pool%d' % l
        S.barrier()
        AR.reset()
        L = 15 + T
        ext = AR.alloc([8, L], F32)
        dT = AR.alloc([8, T], BF16)
        tmps = [[AR.alloc([2, L], F32) for _ in range(2)] for _ in range(4)]
        fix = AR.alloc([8, 15], F32)
        CP('pool', ext[:, :, 0:15], poolhist[:, j, :, :], [('phist', j)], ['ext_h'])
        rmsnorm(T, 'mixn', 8 * l, lambda c: ext[:, c, 15:L], lambda c: ('ext', c))
        CP('pool', poolhist[:, j, :, :], ext[:, :, T:L], ['ext_h'] + [('ext', c) for c in range(8)], [('phist', j)])
        slot, sk = W.acquire('pool')
        for g in range(4):
            eng = 'dve' if g % 2 == 0 else 'pool'
            cs = [2 * g, 2 * g + 1]
            ekeys = ['ext_h', ('ext', cs[0]), ('ext', cs[1])]
            src = ext[:, 2 * g:2 * g + 2, :]
            lo = 0
            rkeys = ekeys
            for st in range(g + 1):
                sh = 2 ** st
                dst = tmps[g][st % 2]
                nlo = lo + sh
                TT(eng, dst[:, :, nlo:L], src[:, :, nlo:L], src[:, :, lo:L - sh], ALU.add, rkeys, [('ptmp', g, st % 2)])
                src = dst
                lo = nlo
                rkeys = [('ptmp', g, st % 2)]
            wdw = 2 ** (g + 1)
            for ci, c in enumerate(cs):
                STT(dT[:, c, :], src[:, ci, 15:L], 1.0 / wdw, ext[:, c, 15:L], ALU.mult, ALU.subtract,
                    rkeys + ekeys, [('dT', c)])
                if first:
                    TT('dve', fix[:, c, :], src[:, ci, 15:30], invc[:, g, :], ALU.mult, rkeys, [('fix', c)])
                    TT('dve', dT[:, c, 0:15], fix[:, c, :], ext[:, c, 15:30], ALU.subtract,
                       [('fix', c)] + ekeys, [('dT', c)])
        for g in range(4):
            for ec in range(2):
                ps, pk = PSA.next()
                for kc in range(2):
                    o = ((g * 2 + ec) * 2 + kc) * 128
                    MM(ps[:, 0:T], slot[:, o:o + 128], dT[:, 2 * g + kc, :], kc == 0, kc == 1, [sk, ('dT', 2 * g + kc)], [pk])
                c = 2 * g + ec
                STT(xT[:, c, 0:T], ps[:, 0:T], cvcol('pscale', 8 * j + c), xT[:, c, 0:T], ALU.mult, ALU.add,
                    [pk, ('xT', c)], [('xT', c)])
        W.release()

    def ffn_layer(T, l):
        S.phase = 'ffn%d' % l
        S.barrier()
        AR.reset()
        PSB.banks = [2, 3, 4, 5, 6, 7]
        actT = AR.alloc([NFC, T], BF16)
        gext = Rot(3, [T + 2], F32)
        tb = Rot(3, [T], F32)
        sb_ = Rot(2, [T], F32)
        norm_to_hT(T, 'ffnn', 8 * l)
        for cc in range(NFC // 2):
            slot, sk = W.acquire('up')
            for ci in range(2):
                c = 2 * cc + ci
                psv, kv = PSB.next()
                psg, kg = PSB.next()
                for kc in range(8):
                    o = (ci * 2) * 1024 + kc * 128
                    MM(psv[:, 0:T], slot[:, o:o + 128], hT[:, kc, 0:T], kc == 0, kc == 7, [sk, ('hT', kc)], [kv])
                for kc in range(8):
                    o = (ci * 2 + 1) * 1024 + kc * 128
                    MM(psg[:, 0:T], slot[:, o:o + 128], hT[:, kc, 0:T], kc == 0, kc == 7, [sk, ('hT', kc)], [kg])
                ge, gk = gext.next()
                CP('pool', ge[:, 0:2], ghist[:, l, c, :], [('ghist', l)], [gk])
                CP('act', ge[:, 2:T + 2], psg[:, 0:T], [kg], [gk])
                CP('pool', ghist[:, l, c, :], ge[:, T:T + 2], [gk], [('ghist', l)])
                t, tk = tb.next()
                wb = CV['fcw'] + (l * NFC + c) * 3
                TS('dve', t, ge[:, 0:T], cv[:, wb:wb + 1], None, ALU.mult, None, [gk], [tk])
                STT(t, ge[:, 1:T + 1], cv[:, wb + 1:wb + 2], t, ALU.mult, ALU.add, [gk, tk], [tk])
                STT(t, ge[:, 2:T + 2], cv[:, wb + 2:wb + 3], t, ALU.mult, ALU.add, [gk, tk], [tk])
                s_, sk2 = sb_.next()
                ACT(s_, t, AF.Silu, [tk], [sk2], bias=cvcol('fcb', l * NFC + c))
                TT('dve', actT[:, c, :], s_, psv[:, 0:T], ALU.mult, [sk2, kv], [('actT', c)])
            W.release()
        for j in range(8):
            slot, sk = W.acquire('down')
            ps, pk = PSA.next()
            for c in range(NFC):
                MM(ps[:, 0:T], slot[:, c * 128:(c + 1) * 128], actT[:, c, :], c == 0, c == NFC - 1,
                   [sk, ('actT', c)], [pk])
            TT('dve', xT[:, j, 0:T], ps[:, 0:T], xT[:, j, 0:T], ALU.add, [pk, ('xT', j)], [('xT', j)])
            W.release()

    def dn_layer(T):
        S.phase = 'dn'
        S.barrier()
        AR.reset()
        NB = (T + 127) // 128
        BS = min(T, 128)
        NCHK = T // 64
        PSB.banks = [2, 3]
        PQ.banks = [4, 5, 6, 7]
        norm_to_hT(T, 'mixn', 8)
        ogT = AR.alloc([8, T], BF16)
        ab = AR.alloc([NB, 16], F32)
        gx = AR.alloc([NB, 8], F32)
        g_tm = AR.alloc([NB, 8], F32)
        nbeta = AR.alloc([NB, 8], F32)
        beta = AR.alloc([NB, 8], F32)
        gc_tm = AR.alloc([NB, 8], F32)
        ngc_tm = AR.alloc([NB, 8], F32)
        gl_tm = AR.alloc([NB, 8], F32)
        kgs = AR.alloc([NB, 8], F32)
        kbs = AR.alloc([NB, 8], F32)
        slot, sk = W.acquire('dn_ab')
        for nb in range(NB):
            ps, pk = PQ.next()
            for kc in range(8):
                MM(ps[0:BS, 0:16], hT[:, kc, nb * 128:nb * 128 + BS], slot[:, kc * 16:(kc + 1) * 16], kc == 0, kc == 7,
                   [sk, ('hT', kc)], [pk])
            CP('dve', ab[0:BS, nb, :], ps[0:BS, 0:16], [pk], ['ab'])
        W.release()
        A = slice(0, BS)
        for nb in range(NB):
            TT('dve', gx[A, nb, :], ab[A, nb, 0:8], cv[A, CV['dtb']:CV['dtb'] + 8], ALU.add, ['ab'], ['gx'])
        ACT(gx[A], gx[A], AF.Exp, ['gx'], ['gx'])
        ACT(gx[A], gx[A], AF.Ln, ['gx'], ['gx'], bias=1.0)
        for nb in range(NB):
            TT('dve', g_tm[A, nb, :], gx[A, nb, :], nexpA[A], ALU.mult, ['gx'], ['g_tm'])
        ACT(beta[A], ab[A, :, 8:16], AF.Exp, ['ab'], ['beta'], scale=-1.0)
        TS('dve', beta[A], beta[A], 1.0, None, ALU.add, None, ['beta'], ['beta'])
        RCP(beta[A], beta[A], ['beta'], ['beta'])
        TS('dve', nbeta[A], beta[A], -1.0, None, ALU.mult, None, ['beta'], ['nbeta'])
        for nb in range(NB):
            ps, pk = PQ.next()
            MM(ps[0:BS, 0:8], minclT[0:BS, 0:BS], g_tm[0:BS, nb, :], True, True, ['g_tm'], [pk])
            CP('dve', gc_tm[0:BS, nb, :], ps[0:BS, 0:8], [pk], ['gc_tm'])
            ps2, pk2 = PQ.next()
            MM(ps2[0:BS, 0:8], blockones[0:BS, 0:BS], g_tm[0:BS, nb, :], True, True, ['g_tm'], [pk2])
            CP('dve', gl_tm[0:BS, nb, :], ps2[0:BS, 0:8], [pk2], ['gl_tm'])
        TS('dve', ngc_tm[A], gc_tm[A], -1.0, None, ALU.mult, None, ['gc_tm'], ['ngc_tm'])
        TT('dve', kgs[A], gl_tm[A], gc_tm[A], ALU.subtract, ['gl_tm', 'gc_tm'], ['kgs'])
        ACT(kgs[A], kgs[A], AF.Exp, ['kgs'], ['kgs'])
        ACT(kbs[A], gc_tm[A], AF.Exp, ['gc_tm'], ['kbs'])
        TT('dve', kbs[A], kbs[A], beta[A], ALU.mult, ['kbs', 'beta'], ['kbs'])

        cext = Rot(2, [T + 3], F32)
        tbuf = Rot(2, [T], F32)
        qkvs = [AR.alloc([T], F32) for _ in range(3)]
        sqb = Rot(2, [T], BF16)
        rr = Rot(2, [T], F32)
        kbg = AR.alloc([NB, 128], BF16)
        vb = AR.alloc([NB, 128], BF16)
        vnew = AR.alloc([NB, 128], BF16)
        attnFs = [Rot(2, [128], F32) for _ in range(NB)]
        qnb = AR.alloc([T], BF16)
        knb = AR.alloc([T], BF16)
        Ybs = [Rot(2, [128], BF16) for _ in range(NB)]
        m128s = [Rot(12, [128], F32) for _ in range(NB)]
        mbs = [Rot(12, [128], BF16) for _ in range(NB)]
        og = AR.alloc([T], F32)
        PBUF = [(AR.alloc([T], F32), AR.alloc([NB, 128], BF16), AR.alloc([NB, 128], F32), AR.alloc([T], BF16), AR.alloc([T], BF16),
                 AR.alloc([NB, 128], BF16), AR.alloc([T], F32)) for _ in range(2)]
        PSA.banks = [2, 3]
        kn2 = AR.alloc([T], BF16)

        def front(h):
            p = h % 2
            zs, kgm, u_, wT, qgT, attnT, EG = PBUF[p]
            slot, sk = W.acquire('dn_u')
            for idx in range(3):
                ps, pk = PSB.next()
                for kc in range(8):
                    o = idx * 1024 + kc * 128
                    MM(ps[:, 0:T], slot[:, o:o + 128], hT[:, kc, 0:T], kc == 0, kc == 7, [sk, ('hT', kc)], [pk])
                ce, ck = cext.next()
                ch = idx * 8 + h
                CP('pool', ce[:, 0:3], chist[:, ch, :], ['chist'], [ck])
                CP('act', ce[:, 3:T + 3], ps[:, 0:T], [pk], [ck])
                CP('pool', chist[:, ch, :], ce[:, T:T + 3], [ck], ['chist'])
                t, tk = tbuf.next()
                wb = CV['dcw'] + ch * 4
                TS('dve', t, ce[:, 0:T], cv[:, wb:wb + 1], None, ALU.mult, None, [ck], [tk])
                for tap in range(1, 4):
                    STT(t, ce[:, tap:T + tap], cv[:, wb + tap:wb + tap + 1], t, ALU.mult, ALU.add, [ck, tk], [tk])
                ACT(qkvs[idx], t, AF.Silu, [tk], [('qkv', idx)])
                yield
            ps, pk = PSB.next()
            for kc in range(8):
                o = 3 * 1024 + kc * 128
                MM(ps[:, 0:T], slot[:, o:o + 128], hT[:, kc, 0:T], kc == 0, kc == 7, [sk, ('hT', kc)], [pk])
            ACT(zs, ps[:, 0:T], AF.Silu, [pk], [('zs', p)])
            W.release()
            yield
            for idx in range(2):
                sq, sqk = sqb.next()
                ACT(sq, qkvs[idx], AF.Square, [('qkv', idx)], [sqk])
                ps, pk = PSA.next()
                MM(ps[:, 0:T], ones_b, sq, True, True, [sqk], [pk])
                r, rk = rr.next()
                ACT(r, ps[:, 0:T], AF.Ln, [pk], [rk], bias=EPS)
                ACT(r, r, AF.Exp, [rk], [rk], scale=-0.5)
                STT(qkvs[idx], qkvs[idx], (128.0 ** -0.5) if idx == 0 else 1.0, r, ALU.mult, ALU.mult,
                    [('qkv', idx), rk], [('qkv', idx)])
            yield
            qn, kn, vs_ = qkvs
            CP('pool', kn2, kn, [('qkv', 1)], ['kn2'])
            CP('act', knb, kn, [('qkv', 1)], ['knb'])
            CP('pool', qnb, qn, [('qkv', 0)], ['qnb'])
            def blk_gen(nb, h=h, kn=kn, qn=qn, vs_=vs_):
                m128 = m128s[nb]
                mb = mbs[nb]
                attnF = attnFs[nb]
                Yb = Ybs[nb]
                cs = slice(nb * 128, nb * 128 + BS)
                R = slice(0, BS)
                bcol = lambda tl: tl[0:BS, nb, h:h + 1]
                ps, pk = PQ.next()
                TR(ps[0:BS, :], kn[:, cs], ident, [('qkv', 1), 'ident'], [pk])
                TS('dve', kbg[R, nb, :], ps[0:BS, :], bcol(kbs), None, ALU.mult, None, [pk, 'kbs'], [('kbg', nb)])
                TS('dve', kgm[R, nb, :], ps[0:BS, :], bcol(kgs), None, ALU.mult, None, [pk, 'kgs'], [('kgm', p, nb)])
                ps, pk = PQ.next()
                TR(ps[0:BS, :], vs_[:, cs], ident, [('qkv', 2), 'ident'], [pk])
                TS('dve', vb[R, nb, :], ps[0:BS, :], bcol(beta), None, ALU.mult, None, [pk, 'beta'], [('vb', nb)])
                yield
                gb2, gb2k = m128.next()
                TS('dve', gb2[R, :], ones_f[R, :], bcol(g_tm), None, ALU.mult, None, ['g_tm'], [gb2k])
                pn, pnk = PQ.next()
                MM(pn[R, 0:BS], gb2[R, 0:BS], nminclT[R, 0:BS], True, True, [gb2k], [pnk])
                pg, pgk = PQ.next()
                MM(pg[:, 0:BS], gb2[R, :], minclT[R, 0:BS], True, True, [gb2k], [pgk])
                E, Ek = m128.next()
                TS('dve', E[R, 0:BS], pn[R, 0:BS], bcol(gc_tm), 0.0, ALU.add, ALU.min, [pnk, 'gc_tm'], [Ek])
                ACT(E[R, 0:BS], E[R, 0:BS], AF.Exp, [Ek], [Ek])
                ET, ETk = m128.next()
                TS('dve', ET[R, 0:BS], pg[R, 0:BS], bcol(ngc_tm), 0.0, ALU.add, ALU.min, [pgk, 'ngc_tm'], [ETk])
                ACT(ET[R, 0:BS], ET[R, 0:BS], AF.Exp, [ETk], [ETk])
                ACT(EG[:, cs], pg[:, 0:BS], AF.Exp, [pgk], [('EG', p, nb)])
                yield
                pa, pak = PQ.next()
                MM(pa[R, 0:BS], knb[:, cs], qnb[:, cs], True, True, ['qnb', 'knb'], [pak])
                af, afk = attnF.next()
                TT('dve', af[R, 0:BS], pa[R, 0:BS], ET[R, 0:BS], ALU.mult, [pak, ETk], [afk])
                TT('dve', attnT[R, nb, 0:BS], af[R, 0:BS], minclT[R, 0:BS], ALU.mult, [afk], [('attnT', p, nb)])
                TT('dve', qgT[:, cs], qn[:, cs], EG[:, cs], ALU.mult, [('qkv', 0), ('EG', p, nb)], [('qgT', p, nb)])
                yield
                pgm, pgmk = PQ.next()
                MM(pgm[R, 0:BS], knb[:, cs], kn2[:, cs], True, True, ['knb', 'kn2'], [pgmk])
                Am, Ak = m128.next()
                STT(Am[R, 0:BS], pgm[R, 0:BS], bcol(nbeta), E[R, 0:BS], ALU.mult, ALU.mult, [pgmk, 'nbeta', Ek], [Ak])
                TT('dve', Am[R, 0:BS], Am[R, 0:BS], mstrict[R, 0:BS], ALU.mult, [Ak], [Ak])
                pt, ptk = PQ.next()
                TR(pt[R, 0:BS], Am[R, 0:BS], ident[R, 0:BS], [Ak], [ptk])
                Bm, Bk = m128.next()
                CP('act', Bm[R, 0:BS], pt[R, 0:BS], [ptk], [Bk])
                yield
                QA, QAk = mb.next()
                CP('dve', QA[R, 0:BS], Am[R, 0:BS], [Ak], [QAk])
                QB, QBk = mb.next()
                CP('act', QB[R, 0:BS], Bm[R, 0:BS], [Bk], [QBk])
                Y, Yk = mb.next()
                TT('dve', Y[R, 0:BS], Bm[R, 0:BS], ident[R, 0:BS], ALU.add, [Bk], [Yk])
                for k in range(1, 6):
                    yield
                    if k < 5:
                        p1, p1k = PQ.next()
                        MM(p1[R, 0:BS], QA[R, 0:BS], QB[R, 0:BS], True, True, [QAk, QBk], [p1k])
                        QBn, QBnk = mb.next()
                        CP('act', QBn[R, 0:BS], p1[R, 0:BS], [p1k], [QBnk])
                    p2, p2k = PQ.next()
                    MM(p2[R, 0:BS], QB[R, 0:BS], QA[R, 0:BS], True, True, [QAk, QBk], [p2k])
                    QAn, QAnk = mb.next()
                    CP('act', QAn[R, 0:BS], p2[R, 0:BS], [p2k], [QAnk])
                    p3, p3k = PQ.next()
                    MM(p3[R, 0:BS], QAn[R, 0:BS], Y[R, 0:BS], True, True, [QAnk, Yk], [p3k])
                    Yn, Ynk = mb.next()
                    TT('dve', Yn[R, 0:BS], p3[R, 0:BS], Y[R, 0:BS], ALU.add, [p3k, Yk], [Ynk])
                    Y, Yk = Yn, Ynk
                    QA, QAk = QAn, QAnk
                    if k < 5:
                        QB, QBk = QBn, QBnk
                yield
                WT, WTk = m128.next()
                TT('dve', WT[R, 0:BS], ident[R, 0:BS], Am[R, 0:BS], ALU.subtract, [Ak], [WTk])
                Y0f, Y0k = m128.next()
                CP('dve', Y0f[R, 0:BS], Y[R, 0:BS], [Yk], [Y0k])
                px, pxk = PQ.next()
                TR(px[R, 0:BS], Y0f[R, 0:BS], ident[R, 0:BS], [Y0k], [pxk])
                Xm, Xk = m128.next()
                CP('act', Xm[R, 0:BS], px[R, 0:BS], [pxk], [Xk])
                yield
                pn1, pn1k = PQ.next()
                MM(pn1[R, 0:BS], WT[R, 0:BS], Y0f[R, 0:BS], True, True, [WTk, Y0k], [pn1k])
                Rm, Rmk = m128.next()
                TT('dve', Rm[R, 0:BS], ident[R, 0:BS], pn1[R, 0:BS], ALU.subtract, [pn1k], [Rmk])
                yield
                pn2, pn2k = PQ.next()
                MM(pn2[R, 0:BS], Xm[R, 0:BS], Rm[R, 0:BS], True, True, [Xk, Rmk], [pn2k])
                Y, Yk = m128.next()
                TT('dve', Y[R, 0:BS], pn2[R, 0:BS], Y0f[R, 0:BS], ALU.add, [pn2k, Y0k], [Yk])
                yield
                pu, puk = PQ.next()
                yb, ybk = Yb.next()
                CP('dve', yb[R, 0:BS], Y[R, 0:BS], [Yk], [ybk])
                MM(pu[R, :], yb[R, 0:BS], vb[R, nb, :], True, True, [ybk, ('vb', nb)], [puk])
                CP('act', u_[R, nb, :], pu[R, :], [puk], [('u', p, nb)])
                pw, pwk = PQ.next()
                MM(pw[:, 0:BS], kbg[R, nb, :], yb[R, 0:BS], True, True, [ybk, ('kbg', nb)], [pwk])
                CP('act', wT[:, cs], pw[:, 0:BS], [pwk], [('wT', p, nb)])
            yield
            for _ in rr_gen([blk_gen(nb) for nb in range(NB)]):
                yield

        def rec(h):
            p = h % 2
            zs, kgm, u_, wT, qgT, attnT, EG = PBUF[p]
            po, pok = psb[p], ('ps', p)
            Sh = Sdn[:, h, :]
            Sk = ('Sdn', h)
            Sb = Sdb[:, h, :]
            Sbk = ('Sdb', h)
            CP('act', Sb, Sh, [Sk], [Sbk])
            for ci in range(NCHK):
                nb = ci // 2
                r0 = (ci % 2) * 64
                c0 = ci * 64
                RR = slice(r0, r0 + 64)
                p1, p1k = PQ.next()
                MM(p1[RR, :], wT[:, c0:c0 + 64], Sb, True, True, [('wT', p, nb), Sbk], [p1k])
                TT('dve', vnew[RR, nb, :], u_[RR, nb, :], p1[RR, :], ALU.subtract, [('u', p, nb), p1k], [('vnew', ci)])
                yield
                MM(po[:, c0:c0 + 64], Sb, qgT[:, c0:c0 + 64], True, False, [Sbk, ('qgT', p, nb)], [pok])
                MM(po[:, c0:c0 + 64], vnew[RR, nb, :], attnT[RR, nb, r0:r0 + 64], False, True,
                   [('vnew', ci), ('attnT', p, nb)], [pok])
                p2, p2k = PQ.next()
                MM(p2, kgm[RR, nb, :], vnew[RR, nb, :], True, True, [('kgm', p, nb), ('vnew', ci)], [p2k])
                STT(Sh, Sh, EG[:, c0 + 63:c0 + 64], p2, ALU.mult, ALU.add, [Sk, ('EG', p, nb), p2k], [Sk])
                if ci < NCHK - 1:
                    CP('act', Sb, Sh, [Sk], [Sbk])
                yield
            sq, sqk = sqb.next()
            ACT(sq, po[:, 0:T], AF.Square, [pok], [sqk])
            ps, pk = PSA.next()
            MM(ps[:, 0:T], ones_b, sq, True, True, [sqk], [pk])
            r, rk = rr.next()
            ACT(r, ps[:, 0:T], AF.Ln, [pk], [rk], bias=EPS, scale=1.0 / 128)
            ACT(r, r, AF.Exp, [rk], [rk], scale=-0.5)
            STT(og, po[:, 0:T], cvcol('dnorm', 0), r, ALU.mult, ALU.mult, [pok, rk], ['og'])
            TT('dve', ogT[:, h, :], og, zs, ALU.mult, ['og', ('zs', p)], [('ogT', h)])
        def exhaust(g):
            for _ in g:
                pass
        exhaust(front(0))
        for h in range(8):
            run_rr([rec(h)] + ([front(h + 1)] if h + 1 < 8 else []))
        PSA.banks = [0, 1]
        for n2 in range(2):
            slot, sk = W.acquire('dn_out')
            for i in range(4):
                n = 4 * n2 + i
                ps, pk = PSA.next()
                for hh in range(8):
                    o = i * 1024 + hh * 128
                    MM(ps[:, 0:T], slot[:, o:o + 128], ogT[:, hh, :], hh == 0, hh == 7, [sk, ('ogT', hh)], [pk])
                TT('dve', xT[:, n, 0:T], ps[:, 0:T], xT[:, n, 0:T], ALU.add, [pk, ('xT', n)], [('xT', n)])
            W.release()

    def sb_layer(T, grp, ti):
        S.phase = 'sb'
        S.barrier()
        AR.reset()
        NB = (T + 127) // 128
        BS = min(T, 128)
        t0 = ti * T
        PSB.banks = [2, 3, 4, 5]
        norm_to_hT(T, 'mixn', 16)
        qT = AR.alloc([8, T], BF16)
        kTb = AR.alloc([8, T], BF16)
        oT = AR.alloc([8, T], BF16)
        kst = Rot(2, [T], F32)
        vst = Rot(2, [512], F32)
        vbb = AR.alloc([NB, 1024], BF16)
        if grp == 'p':
            Stot = t0 + T
        else:
            Stot = PAST + T
        NBLK = (Stot + 127) // 128
        KTc = Rot(2, [Stot], BF16)
        Vc = Rot(2, [NBLK, 128], BF16)
        ebuf = Rot(8, [T], F32)
        spbuf = Rot(12, [T], BF16)
        abuf = Rot(8, [T], BF16)
        Sruns = [AR.alloc([T], F32) for _ in range(4)]
        Sbfs = [AR.alloc([T], BF16) for _ in range(4)]
        for n4 in range(4):
            slot, sk = W.acquire('sb_u')
            for i in range(4):
                cidx = 4 * n4 + i
                ps, pk = PSB.next()
                for kc in range(8):
                    o = i * 1024 + kc * 128
                    MM(ps[:, 0:T], slot[:, o:o + 128], hT[:, kc, 0:T], kc == 0, kc == 7, [sk, ('hT', kc)], [pk])
                if cidx < 8:
                    S.emit('act', (lambda o_, i_: (lambda e: e.mul(out=o_, in_=i_, mul=0.125)))(qT[:, cidx, :], ps[:, 0:T]),
                           [pk], [('qT', cidx)])
                else:
                    c = cidx - 8
                    st, stk = kst.next()
                    CP('act', st, ps[:, 0:T], [pk], [stk])
                    DMA('sp', kTo[grp][c * 128:(c + 1) * 128, t0:t0 + T], st, [stk], [], lane=('kst', stk[2]))
                    CP('dve', kTb[:, c, :], st, [stk], [('kTb', c)])
            W.release()
        if grp == 'p':
            DMA('sp', KTs[:, :, t0:t0 + T].rearrange("c p t -> p c t"), kTb, [('kTb', c) for c in range(8)], ['KTs'], lane='kts')
        for half in range(2):
            slot, sk = W.acquire('sb_v')
            for nb in range(NB):
                ps, pk = PSB.next()
                for kc in range(8):
                    MM(ps[0:BS, :], hT[:, kc, nb * 128:nb * 128 + BS], slot[:, kc * 512:(kc + 1) * 512], kc == 0, kc == 7,
                       [sk, ('hT', kc)], [pk])
                st, stk = vst.next()
                CP('act', st[0:BS, :], ps[0:BS, :], [pk], [stk])
                DMA('sp', vo[grp][t0 + nb * 128:t0 + nb * 128 + BS, half * 512:(half + 1) * 512], st[0:BS, :], [stk], [],
                    lane=('vst', stk[2]))
                CP('dve', vbb[0:BS, nb, half * 512:(half + 1) * 512], st[0:BS, :], [stk], [('vbb', nb, half)])
            W.release()
        VBB = [('vbb', nb, half) for nb in range(NB) for half in range(2)]
        if grp == 'p':
            for nb in range(NB):
                DMA('sp', Vs[:, :, t0 // 128 + nb, :].rearrange("c p n -> p c n"),
                    vbb[:, nb, :].rearrange("p (c n) -> p c n", c=8), VBB, ['Vs'], lane='vs', same_gen=(nb > 0))
        PO_BANKS = [0, 0, 1, 1]
        PSB.banks = [2, 3, 4, 5, 6, 7]
        for c0 in range(0, 8, 2):
          gens = []
          for c in (c0, c0 + 1):
            kt, ktk = KTc.next()
            vc, vck = Vc.next()
            bi = ktk[2]
            if grp == 'p':
                DMA('sp', kt[:, 0:Stot], KTs[c, :, 0:Stot], ['KTs'], [ktk], lane=('ktc', bi))
                DMA('sp', vc[:, 0:NBLK, :], Vs[c, :, 0:NBLK, :], ['Vs'], [vck], lane=('vc', bi))
            else:
                DMA('pool', kt[:, 0:PAST].rearrange("p (a b) -> p a b", b=1024), cacheKT[c * 128:(c + 1) * 128, :].rearrange("p (a b) -> p a b", b=1024), [], [ktk], lane=('ktcs', bi))
                CP('dve', kt[:, PAST:PAST + T], kTb[:, c, :], [('kTb', c)], [ktk])
                DMA('pool', vc[:, 0:16, :], cacheV[c], [], [vck], lane=('vcs', bi))
                CP('dve', vc[0:BS, 16, :], vbb[0:BS, 0, c * 128:(c + 1) * 128], VBB, [vck])
            def head_gen(hh, gi, kt=kt, ktk=ktk, vc=vc, vck=vck, c=c):
                P0 = hh * 64
                PR = slice(P0, P0 + 64)
                po, pok = psb[PO_BANKS[gi]], ('ps', PO_BANKS[gi])
                Srun, Sbf = Sruns[gi], Sbfs[gi]
                srk, sbk = ('Srun', gi), ('Sbf', gi)
                MS('pool', Srun, 0.0, [srk])
                MS('pool', Sbf, 0.0, [sbk])
                def stage1(jb):
                    bs = min(128, Stot - jb * 128)
                    if grp == 'p':
                        diag = jb >= t0 // 128
                        qlo = (jb - t0 // 128) * 128 if diag else 0
                    else:
                        diag = jb == NBLK - 1
                        qlo = 0
                    dcols = min(128, T - qlo)
                    QS = slice(qlo, T)
                    B_ = slice(0, bs)
                    pz, pzk = PSB.next()
                    MM(pz[B_, QS], kt[PR, jb * 128:jb * 128 + bs], qT[PR, c, QS], True, True, [ktk, ('qT', c)], [pzk])
                    e_, ek = ebuf.next()
                    ACT(e_[B_, QS], pz[B_, QS], AF.Exp, [pzk], [ek])
                    sp, spk = spbuf.next()
                    ACT(sp[B_, QS], e_[B_, QS], AF.Ln, [ek], [spk], bias=1.0)
                    if diag:
                        TT('dve', sp[B_, qlo:qlo + dcols], sp[B_, qlo:qlo + dcols], mask_lt[B_, 0:dcols], ALU.mult, [spk], [spk])
                    return dict(jb=jb, bs=bs, diag=diag, qlo=qlo, dcols=dcols, QS=QS, B_=B_, sp=sp, spk=spk)

                units = list(range(NBLK - 1, -1, -1))
                ctx = stage1(units[0])
                yield
                for ui in range(len(units)):
                    first = ui == 0
                    nxt = stage1(units[ui + 1]) if ui + 1 < len(units) else None
                    yield
                    jb, bs, diag, qlo, dcols, QS, B_, sp, spk = (ctx[n] for n in ('jb', 'bs', 'diag', 'qlo', 'dcols', 'QS', 'B_', 'sp', 'spk'))
                    pz, pzk = PSB.next()
                    MM(pz[B_, QS], kt[PR, jb * 128:jb * 128 + bs], qT[PR, c, QS], True, False, [ktk, ('qT', c)], [pzk])
                    yield
                    MM(pz[B_, QS], negUI[B_, B_], sp[B_, QS], False, first, [spk], [pzk])
                    if not first:
                        MM(pz[B_, QS], nones_b[:, B_], Sbf[:, QS], False, True, [sbk], [pzk])
                    a_, ak = abuf.next()
                    if diag and qlo > 0:
                        MS('pool', a_[B_, 0:qlo], 0.0, [ak])
                    ACT(a_[B_, QS], pz[B_, QS], AF.Exp, [pzk], [ak])
                    if diag:
                        TT('dve', a_[B_, qlo:qlo + dcols], a_[B_, qlo:qlo + dcols], mask_lt[B_, 0:dcols], ALU.mult, [ak], [ak])
                    if jb > 0:
                        TT('dve', Srun[B_, QS], Srun[B_, QS], sp[B_, QS], ALU.add, [srk, spk], [srk])
                        CP('dve', Sbf[B_, QS], Srun[B_, QS], [srk], [sbk])
                    yield
                    MM(po[PR, 0:T], vc[B_, jb, hh * 64:(hh + 1) * 64], a_[B_, 0:T], first, jb == 0, [vck, ak], [pok])
                    yield
                    ctx = nxt
                CP('act', oT[PR, c, :], po[PR, 0:T], [pok], [('oT', c)])
            gens.append(head_gen(0, len(gens)))
            gens.append(head_gen(1, len(gens)))
          run_rr(gens)
        for n2 in range(2):
            slot, sk = W.acquire('sb_out')
            for i in range(4):
                n = 4 * n2 + i
                ps, pk = PSA.next()
                for cc in range(8):
                    o = i * 1024 + cc * 128
                    MM(ps[:, 0:T], slot[:, o:o + 128], oT[:, cc, :], cc == 0, cc == 7, [sk, ('oT', cc)], [pk])
                TT('dve', xT[:, n, 0:T], ps[:, 0:T], xT[:, n, 0:T], ALU.add, [pk, ('xT', n)], [('xT', n)])
            W.release()

    def final_layer(T, grp, ti):
        S.phase = 'final'
        S.barrier()
        AR.reset()
        t0 = ti * T
        yst = AR.alloc([8, T], F32)
        rmsnorm(T, 'finn', 0, lambda c: yst[:, c, :], lambda c: ('yst', c))
        DMA('sp', yT[grp].rearrange("(c p) t -> p c t", p=128)[:, :, t0:t0 + T], yst, [('yst', c) for c in range(8)], [],
            lane='yst')

    XT_ALL = [('xT', c) for c in range(8)]
    HIST = [('phist', 0), ('phist', 1), 'chist'] + [('Sdn', h) for h in range(8)] + [('ghist', l) for l in range(4)]

    def run_tile(grp, ti, T):
        t0 = ti * T
        src = xTp if grp == 'p' else xTs
        DMA('sp', xT[:, :, 0:T], src.rearrange("(c p) t -> p c t", p=128)[:, :, t0:t0 + T], [], XT_ALL, lane='xin')
        def skip(n):
            for _ in range(n):
                W.acquire(None)
                W.release()
        LY = layers if layers is not None else {'p0', 'f0', 'dn', 'f1', 'sb', 'f2', 'p3', 'f3'}
        fst = grp == 'p' and ti == 0
        pool_layer(T, 0, 0, fst) if 'p0' in LY else skip(1)
        ffn_layer(T, 0) if 'f0' in LY else skip(19)
        dn_layer(T) if 'dn' in LY else skip(11)
        ffn_layer(T, 1) if 'f1' in LY else skip(19)
        sb_layer(T, grp, ti) if 'sb' in LY else skip(8)
        ffn_layer(T, 2) if 'f2' in LY else skip(19)
        pool_layer(T, 1, 3, fst) if 'p3' in LY else skip(1)
        ffn_layer(T, 3) if 'f3' in LY else skip(19)
        final_layer(T, grp, ti)

    def store_states(grp):
        S.barrier()
        DMA('sp', o_pool[grp].rearrange("l p c t -> p l c t"), poolhist, [('phist', 0), ('phist', 1)], [], lane='st0')
        DMA('sp', o_dnc[grp], chist, ['chist'], [], lane='st1')
        DMA('sp', o_dn[grp], Sdn, [('Sdn', h) for h in range(8)], [], lane='st2')
        DMA('sp', o_ffn[grp].rearrange("l p c t -> p l c t"), ghist, [('ghist', l) for l in range(4)], [], lane='st3')

    MS('pool', poolhist, 0.0, [('phist', 0), ('phist', 1)])
    MS('pool', ghist, 0.0, [('ghist', l) for l in range(4)])
    MS('pool', chist, 0.0, ['chist'])
    MS('pool', Sdn, 0.0, [('Sdn', h) for h in range(8)])
    for ti in range(n_tiles):
        run_tile('p', ti, 512)
    store_states('p')
    if do_sample:
        S.barrier()
        DMA('sp', poolhist, st_pool.rearrange("l p c t -> p l c t"), [], [('phist', 0), ('phist', 1)], lane='ld0')
        DMA('sp', chist, st_dnc, [], ['chist'], lane='ld1')
        DMA('sp', Sdn, st_dn, [], [('Sdn', h) for h in range(8)], lane='ld2')
        DMA('sp', ghist, st_ffn.rearrange("l p c t -> p l c t"), [], [('ghist', l) for l in range(4)], lane='ld3')
        run_tile('s', 0, 64)
        store_states('s')
    S.finish()
    S.replay(nc)
    return nc, S


_CACHE = {}
_LAST = None
_DBG = None


def kernel(x_prompt, x_sample, state_pool, state_dn_conv, state_dn, cache_sb_k, cache_sb_v, state_ffn_conv,
           mix_norm, ffn_norm, final_norm, pool_w, pool_scale, dn_w_in, dn_conv_w, dn_a_log, dn_dt_bias,
           dn_norm, dn_w_out, sb_w_qkv, sb_w_out, ffn_w_up, ffn_conv_w, ffn_conv_b, ffn_w_down,
           _n_tiles=8, _do_sample=True, _layers=None):
    f = lambda a: np.asarray(a, np.float32)
    w = dict(mix_norm=f(mix_norm), ffn_norm=f(ffn_norm), final_norm=f(final_norm), pool_w=f(pool_w),
             pool_scale=f(pool_scale), dn_w_in=f(dn_w_in), dn_conv_w=f(dn_conv_w), dn_a_log=f(dn_a_log),
             dn_dt_bias=f(dn_dt_bias), dn_norm=f(dn_norm), dn_w_out=f(dn_w_out), sb_w_qkv=f(sb_w_qkv),
             sb_w_out=f(sb_w_out), ffn_w_up=f(ffn_w_up), ffn_conv_w=f(ffn_conv_w), ffn_conv_b=f(ffn_conv_b),
             ffn_w_down=f(ffn_w_down))
    wpack = pack_weights(w)
    cvec = pack_cvec(w)
    x_prompt = f(x_prompt); x_sample = f(x_sample); state_pool = f(state_pool); state_dn_conv = f(state_dn_conv)
    state_dn = f(state_dn); cache_sb_k = f(cache_sb_k); cache_sb_v = f(cache_sb_v); state_ffn_conv = f(state_ffn_conv)
    B = 8
    in_maps = []
    for b in range(B):
        m = {
            "xTp": np.ascontiguousarray(x_prompt[b].T),
            "xTs": np.ascontiguousarray(x_sample[b].T),
            "wpack": wpack,
            "cvec": cvec,
            "st_pool": np.ascontiguousarray(state_pool[:, b].reshape(2, 15, 8, 128).transpose(0, 3, 2, 1)),
            "st_dnc": np.ascontiguousarray(state_dn_conv[0, b].reshape(3, 24, 128).transpose(2, 1, 0)),
            "st_dn": np.ascontiguousarray(state_dn[0, b].transpose(1, 0, 2)),
            "st_ffn": np.ascontiguousarray(state_ffn_conv[:, b].reshape(4, 2, NFC, 128).transpose(0, 3, 2, 1)),
            "cacheKT": np.ascontiguousarray(cache_sb_k[0, b].reshape(PAST, D).T),
            "cacheV": np.ascontiguousarray(cache_sb_v[0, b].reshape(16, 128, 8, 128).transpose(2, 1, 0, 3)),
        }
        in_maps.append(m)
    key = (_n_tiles, _do_sample, None if _layers is None else tuple(sorted(_layers)))
    if key not in _CACHE:
        _CACHE[key] = build_program(_n_tiles, _do_sample, _layers)[0]
    nc = _CACHE[key]
    res = run_bass_kernel_spmd(nc, in_maps, core_ids=list(range(B)))
    R = res.results
    global _LAST
    _LAST = R

    def st(name, fn):
        return np.stack([fn(np.asarray(R[b][name])) for b in range(B)], axis=0)
    y_p = st("yTp", lambda a: a.T)
    y_s = st("yTs", lambda a: a.T)
    pool_p = st("o_poolp", lambda a: a.transpose(0, 3, 2, 1).reshape(2, 15, D)).transpose(1, 0, 2, 3)
    pool_s = st("o_pools", lambda a: a.transpose(0, 3, 2, 1).reshape(2, 15, D)).transpose(1, 0, 2, 3)
    dnc_p = st("o_dncp", lambda a: a.transpose(2, 1, 0).reshape(3, 3072))[None]
    dnc_s = st("o_dncs", lambda a: a.transpose(2, 1, 0).reshape(3, 3072))[None]
    dn_p = st("o_dnp", lambda a: a.transpose(1, 0, 2))[None]
    dn_s = st("o_dns", lambda a: a.transpose(1, 0, 2))[None]
    k_p = st("kTp", lambda a: a.T.reshape(SEQ, 16, 64))[None]
    k_s = st("kTs", lambda a: a.T.reshape(DSEQ, 16, 64))[None]
    v_p = st("vp", lambda a: a.reshape(SEQ, 16, 64))[None]
    v_s = st("vs", lambda a: a.reshape(DSEQ, 16, 64))[None]
    ffn_p = st("o_ffnp", lambda a: a.transpose(0, 3, 2, 1).reshape(4, 2, DFF)).transpose(1, 0, 2, 3)
    ffn_s = st("o_ffns", lambda a: a.transpose(0, 3, 2, 1).reshape(4, 2, DFF)).transpose(1, 0, 2, 3)
    outs = (y_p, y_s, pool_p, pool_s, dnc_p, dnc_s, dn_p, dn_s, k_p, k_s, v_p, v_s, ffn_p, ffn_s)
    return tuple(np.ascontiguousarray(o, dtype=np.float32) for o in outs)
```

```python
import bisect
import contextlib
import numpy as np
import concourse.bass as bass
import concourse.mybir as mybir
from concourse.bass_utils import run_bass_kernel_spmd

F32 = mybir.dt.float32
BF16 = mybir.dt.bfloat16
AF = mybir.ActivationFunctionType
ALU = mybir.AluOpType

D = 1024
SEQ = 4096
DSEQ = 64
PAST = 2048
DFF = 2816
NFC = 22
EPS = 1e-6
SLOT = 4096
NSLOT = 5
SAME_ENGINE_SYNC = True


class Sched:
    ENGS = ['pe', 'act', 'dve', 'pool', 'sp']

    def __init__(self):
        self.streams = {e: [] for e in self.ENGS}
        self.cnt = {e: 0 for e in self.ENGS}
        self.lane_cnt = {}
        self.seen = {e: {} for e in self.ENGS}
        self.snaps = {}
        self.res = {}
        self.nops = 0
        self.phase = 'setup'
        self.tags = {e: [] for e in self.ENGS}

    def _snap_put(self, key, val, d):
        vals, dicts = self.snaps.setdefault(key, ([], []))
        if dicts and dicts[-1] is d:
            return
        vals.append(val)
        dicts.append(d)

    def _snap_get(self, key, val):
        if key not in self.snaps:
            return None
        vals, dicts = self.snaps[key]
        i = bisect.bisect_right(vals, val) - 1
        return dicts[i] if i >= 0 else None

    def _wait(self, eng, key, val):
        seen = self.seen[eng]
        if seen.get(key, 0) >= val:
            return
        if key == ('eng', eng) and (eng == 'pe' or not SAME_ENGINE_SYNC):
            return
        self.streams[eng].append(('wait', key, val))
        new = dict(seen)
        new[key] = val
        sn = self._snap_get(key, val)
        if sn:
            for k, v in sn.items():
                if new.get(k, 0) < v:
                    new[k] = v
        self.seen[eng] = new

    def emit(self, eng, fn, reads=(), writes=(), lane=None, same_gen=False):
        deps = {}
        for r in reads:
            ent = self.res.get(r)
            if ent and ent[0]:
                k, v = ent[0]
                if deps.get(k, 0) < v:
                    deps[k] = v
            if ent and isinstance(r, tuple) and r[0] == 'ps':
                for k, v in ent[1].items():
                    if k != ('eng', eng) and deps.get(k, 0) < v:
                        deps[k] = v
        for w in writes:
            ent = self.res.get(w)
            if ent:
                if ent[0]:
                    k, v = ent[0]
                    if deps.get(k, 0) < v:
                        deps[k] = v
                for k, v in ent[1].items():
                    if deps.get(k, 0) < v:
                        deps[k] = v
        for k, v in deps.items():
            self._wait(eng, k, v)
        if lane is not None:
            key = ('lane', lane)
            cur = self.lane_cnt.get(lane, 0)
            if cur and not same_gen:
                self._wait(eng, key, cur)
            cur += 16
            self.lane_cnt[lane] = cur
            ref = (key, cur)
            self.streams[eng].append(('dma', fn, lane))
        else:
            key = ('eng', eng)
            self.cnt[eng] += 1
            ref = (key, self.cnt[eng])
            self.streams[eng].append(('op', fn))
            self.tags[eng].append(self.phase)
        self._snap_put(key, ref[1], self.seen[eng])
        for r in reads:
            ent = self.res.setdefault(r, [None, {}])
            if ent[1].get(key, 0) < ref[1]:
                ent[1][key] = ref[1]
        for w in writes:
            self.res[w] = [ref, {}]
        self.nops += 1
        return ref

    def barrier(self):
        for e in self.ENGS:
            for f in self.ENGS:
                if self.cnt[f]:
                    self._wait(e, ('eng', f), self.cnt[f])
            for l, c in self.lane_cnt.items():
                self._wait(e, ('lane', l), c)

    def finish(self):
        for l, c in self.lane_cnt.items():
            self._wait('sp', ('lane', l), c)
        for f in self.ENGS:
            if self.cnt[f]:
                self._wait('sp', ('eng', f), self.cnt[f])

    def replay(self, nc):
        with contextlib.ExitStack() as es:
            sems = {}
            for e in self.ENGS:
                if self.cnt[e]:
                    sems[('eng', e)] = es.enter_context(nc.semaphore("s_" + e))
            for i, l in enumerate(self.lane_cnt):
                sems[('lane', l)] = es.enter_context(nc.semaphore("l_%d" % i))
            block = es.enter_context(nc.Block())

            def run(engname):
                def body(eng):
                    for it in self.streams[engname]:
                        if it[0] == 'wait':
                            eng.wait_ge(sems[it[1]], it[2])
                        elif it[0] == 'op':
                            it[1](eng).then_inc(sems[('eng', engname)], 1)
                        else:
                            it[1](eng).then_inc(sems[('lane', it[2])], 16)
                return body
            block.tensor(run('pe'))
            block.scalar(run('act'))
            block.vector(run('dve'))
            block.gpsimd(run('pool'))
            block.sync(run('sp'))


def _unit(W, col0, ncol=128):
    K = W.shape[0]
    return np.ascontiguousarray(
        W[:, col0:col0 + ncol].reshape(K // 128, 128, ncol).transpose(1, 0, 2)).reshape(128, -1)


def chunk_plan():
    plan = []

    def ffn(l):
        for cc in range(NFC // 2):
            plan.append([('up', l, 2 * cc, 0), ('up', l, 2 * cc, 1), ('up', l, 2 * cc + 1, 0), ('up', l, 2 * cc + 1, 1)])
        for j in range(8):
            plan.append([('down', l, j)])
    plan.append([('pool', 0)])
    ffn(0)
    plan.append([('dn_ab',)])
    for h in range(8):
        plan.append([('dn_u', h * 128), ('dn_u', 1024 + h * 128), ('dn_u', 2048 + h * 128), ('dn_u', 3072 + h * 128)])
    for n in range(2):
        plan.append([('dn_out', 4 * n + i) for i in range(4)])
    ffn(1)
    for n in range(4):
        plan.append([('sb_u', (4 * n + i) * 128) for i in range(4)])
    plan.append([('sb_v', 0)])
    plan.append([('sb_v', 1)])
    for n in range(2):
        plan.append([('sb_out', 4 * n + i) for i in range(4)])
    ffn(2)
    plan.append([('pool', 1)])
    ffn(3)
    return plan


def piece_size(p):
    k = p[0]
    if k == 'pool':
        return 2048
    if k == 'down':
        return DFF
    if k == 'dn_ab':
        return 128
    if k == 'sb_v':
        return 4096
    return 1024


def chunk_offsets():
    plan = chunk_plan()
    offs = []
    o = 0
    for ch in plan:
        used = sum(piece_size(p) for p in ch)
        offs.append((o, used))
        o += used
    tot = ((o + 1023) // 1024) * 1024
    return plan, offs, tot


def piece_data(p, w):
    k = p[0]
    if k == 'pool':
        a = w['pool_w'][p[1]].reshape(4, 2, 128, 2, 128)
        return np.ascontiguousarray(a.transpose(2, 0, 3, 1, 4)).reshape(128, 2048)
    if k == 'up':
        return _unit(w['ffn_w_up'][p[1]], p[3] * DFF + p[2] * 128)
    if k == 'down':
        return _unit(w['ffn_w_down'][p[1]], p[2] * 128)
    if k == 'dn_ab':
        return _unit(w['dn_w_in'][0], 4096, 16)
    if k == 'dn_u':
        return _unit(w['dn_w_in'][0], p[1])
    if k == 'dn_out':
        return _unit(w['dn_w_out'][0], p[1] * 128)
    if k == 'sb_u':
        return _unit(w['sb_w_qkv'][0], p[1])
    if k == 'sb_v':
        return _unit(w['sb_w_qkv'][0], 2048 + p[1] * 512, 512)
    if k == 'sb_out':
        return _unit(w['sb_w_out'][0], p[1] * 128)
    raise KeyError(k)


def pack_weights(w):
    plan, offs, tot = chunk_offsets()
    out = np.zeros((128, tot), np.float32)
    for ch, (o, used) in zip(plan, offs):
        for p in ch:
            n = piece_size(p)
            out[:, o:o + n] = piece_data(p, w)
            o += n
    return out


CV = {}
_o = 0
for _name, _n in [('mixn', 32), ('ffnn', 32), ('finn', 8), ('pscale', 16), ('fcw', 4 * NFC * 3), ('fcb', 4 * NFC),
                  ('dcw', 96), ('dnorm', 1), ('alog', 8), ('dtb', 8)]:
    CV[_name] = _o
    _o += _n
NCV = _o


def pack_cvec(w):
    cv = np.zeros((128, NCV), np.float32)

    def fm(v):
        return np.asarray(v, np.float32).reshape(-1, 128).T
    for l in range(4):
        cv[:, CV['mixn'] + 8 * l: CV['mixn'] + 8 * l + 8] = fm(w['mix_norm'][l])
        cv[:, CV['ffnn'] + 8 * l: CV['ffnn'] + 8 * l + 8] = fm(w['ffn_norm'][l])
        a = np.asarray(w['ffn_conv_w'][l], np.float32)
        a = a.reshape(3, NFC, 128).transpose(2, 1, 0)
        cv[:, CV['fcw'] + l * NFC * 3: CV['fcw'] + (l + 1) * NFC * 3] = a.reshape(128, NFC * 3)
        cv[:, CV['fcb'] + l * NFC: CV['fcb'] + (l + 1) * NFC] = fm(w['ffn_conv_b'][l])
    cv[:, CV['finn']:CV['finn'] + 8] = fm(w['final_norm'])
    for j in range(2):
        cv[:, CV['pscale'] + 8 * j: CV['pscale'] + 8 * j + 8] = fm(w['pool_scale'][j])
    a = np.asarray(w['dn_conv_w'][0], np.float32).reshape(4, 24, 128).transpose(2, 1, 0)
    cv[:, CV['dcw']:CV['dcw'] + 96] = a.reshape(128, 96)
    cv[:, CV['dnorm']] = np.asarray(w['dn_norm'][0], np.float32)
    cv[:, CV['alog']:CV['alog'] + 8] = np.asarray(w['dn_a_log'][0], np.float32)[None, :]
    cv[:, CV['dtb']:CV['dtb'] + 8] = np.asarray(w['dn_dt_bias'][0], np.float32)[None, :]
    return cv


def build_program(n_tiles=8, do_sample=True, layers=None):
    nc = bass.Bass("TRN2", target_bir_lowering=False)
    S = Sched()
    plan, offs, WTOT = chunk_offsets()
    NCH = len(plan)

    def din(name, shape):
        return nc.dram_tensor(name, shape, F32, kind="ExternalInput").ap()

    def dout(name, shape):
        return nc.dram_tensor(name, shape, F32, kind="ExternalOutput").ap()

    xTp = din("xTp", [D, SEQ])
    xTs = din("xTs", [D, DSEQ])
    wpack = din("wpack", [128, WTOT])
    cvec_d = din("cvec", [128, NCV])
    st_pool = din("st_pool", [2, 128, 8, 15])
    st_dnc = din("st_dnc", [128, 24, 3])
    st_dn = din("st_dn", [128, 8, 128])
    st_ffn = din("st_ffn", [4, 128, NFC, 2])
    cacheKT = din("cacheKT", [D, PAST])
    cacheV = din("cacheV", [8, 128, 16, 128])

    yT = {'p': dout("yTp", [D, SEQ]), 's': dout("yTs", [D, DSEQ])}
    o_pool = {'p': dout("o_poolp", [2, 128, 8, 15]), 's': dout("o_pools", [2, 128, 8, 15])}
    o_dnc = {'p': dout("o_dncp", [128, 24, 3]), 's': dout("o_dncs", [128, 24, 3])}
    o_dn = {'p': dout("o_dnp", [128, 8, 128]), 's': dout("o_dns", [128, 8, 128])}
    kTo = {'p': dout("kTp", [D, SEQ]), 's': dout("kTs", [D, DSEQ])}
    vo = {'p': dout("vp", [SEQ, D]), 's': dout("vs", [DSEQ, D])}
    o_ffn = {'p': dout("o_ffnp", [4, 128, NFC, 2]), 's': dout("o_ffns", [4, 128, NFC, 2])}

    wbf = nc.dram_tensor("wbf", [128, WTOT], BF16).ap()
    KTs = nc.dram_tensor("KTs", [8, 128, SEQ], BF16).ap()
    Vs = nc.dram_tensor("Vs", [8, 128, 32, 128], BF16).ap()

    def sb(name, shape, dt):
        return nc.alloc_sbuf_tensor(name, shape, dt).ap()
    xT = sb("xT", [128, 8, 512], F32)
    hT = sb("hT", [128, 8, 512], BF16)
    wring = sb("wring", [128, NSLOT, SLOT], BF16)
    cv = sb("cv", [128, NCV], F32)
    ident = sb("ident", [128, 128], F32)
    ones_f = sb("ones_f", [128, 128], F32)
    minclT = sb("minclT", [128, 128], F32)
    nminclT = sb("nminclT", [128, 128], F32)
    mstrict = sb("mstrict", [128, 128], F32)
    blockones = sb("blockones", [128, 128], F32)
    ones_b = sb("ones_b", [128, 128], BF16)
    nones_b = sb("nones_b", [128, 128], BF16)
    negUI = sb("negUI", [128, 128], BF16)
    mask_lt = sb("mask_lt", [128, 128], BF16)
    invc = sb("invc", [128, 4, 15], F32)
    nexpA = sb("nexpA", [128, 8], F32)
    poolhist = sb("poolhist", [128, 2, 8, 15], F32)
    ghist = sb("ghist", [128, 4, NFC, 2], F32)
    chist = sb("chist", [128, 24, 3], F32)
    Sdn = sb("Sdn", [128, 8, 128], F32)
    Sdb = sb("Sdb", [128, 8, 128], BF16)
    nsq = sb("nsq", [128, 2, 512], BF16)
    nrt = sb("nrt", [128, 512], F32)
    nrstd = sb("nrstd", [128, 512], F32)
    ARENA = 30 * 1024
    arena = sb("arena", [128, ARENA], F32)
    psb = [nc.alloc_psum_tensor("ps%d" % i, [128, 512], F32).ap() for i in range(8)]

    def MM(out, lhsT, rhs, start=True, stop=True, rd=(), wr=()):
        S.emit('pe', lambda e: e.matmul(out, lhsT=lhsT, rhs=rhs, start=start, stop=stop), rd, wr)
        if lhsT.dtype == F32:
            S.tags['pe'].append(S.phase)

    def TR(out, in_, idn, rd=(), wr=()):
        S.emit('pe', lambda e: e.transpose(out, in_, idn), rd, wr)

    def ACT(out, in_, func, rd=(), wr=(), bias=None, scale=None):
        kw = {}
        if bias is not None:
            kw['bias'] = bias
        if scale is not None:
            kw['scale'] = scale
        S.emit('act', lambda e: e.activation(out=out, in_=in_, func=func, **kw), rd, wr)

    def TT(eng, out, in0, in1, op, rd=(), wr=()):
        S.emit(eng, lambda e: e.tensor_tensor(out=out, in0=in0, in1=in1, op=op), rd, wr)

    def TS(eng, out, in0, s1, s2, op0, op1=None, rd=(), wr=()):
        if op1 is None:
            S.emit(eng, lambda e: e.tensor_scalar(out=out, in0=in0, scalar1=s1, scalar2=None, op0=op0), rd, wr)
        else:
            S.emit(eng, lambda e: e.tensor_scalar(out=out, in0=in0, scalar1=s1, scalar2=s2, op0=op0, op1=op1), rd, wr)

    def STT(out, in0, sc, in1, op0, op1, rd=(), wr=()):
        S.emit('dve', lambda e: e.scalar_tensor_tensor(out=out, in0=in0, scalar=sc, in1=in1, op0=op0, op1=op1), rd, wr)

    def CP(eng, out, in_, rd=(), wr=()):
        if eng == 'act':
            S.emit('act', lambda e: e.copy(out=out, in_=in_), rd, wr)
        else:
            S.emit(eng, lambda e: e.tensor_copy(out=out, in_=in_), rd, wr)

    def MS(eng, ap, val, wr=()):
        S.emit(eng, lambda e: e.memset(ap, val), (), wr)

    def RCP(out, in_, rd=(), wr=()):
        S.emit('dve', lambda e: e.reciprocal(out=out, in_=in_), rd, wr)

    def DMA(eng, out, in_, rd=(), wr=(), lane=None, same_gen=False):
        S.emit(eng, lambda e: e.dma_start(out=out, in_=in_), rd, wr, lane=lane, same_gen=same_gen)

    def ASEL(out, in_, pattern, cmp, fill, base, cm, rd=(), wr=()):
        S.emit('pool', lambda e: e.affine_select(out=out, in_=in_, pattern=pattern, compare_op=cmp, fill=fill,
                                                 base=base, channel_multiplier=cm), rd, wr)

    class Arena:
        def __init__(self):
            self.off = 0

        def reset(self):
            self.off = 0

        def alloc(self, shape, dt):
            n = int(np.prod(shape))
            n4 = n if dt == F32 else (n + 1) // 2
            n4 = (n4 + 7) // 8 * 8
            assert self.off + n4 <= ARENA, ("arena overflow", self.off, n4)
            v = arena[:, self.off:self.off + n4]
            self.off += n4
            if dt != F32:
                v = v.bitcast(dt)
            v = v[:, 0:n]
            if len(shape) == 2:
                return v.rearrange("p (a b) -> p a b", a=shape[0])
            if len(shape) == 3:
                return v.rearrange("p (a b c) -> p a b c", a=shape[0], b=shape[1])
            return v
    AR = Arena()

    class Rot:
        cnt = [0]

        def __init__(self, n, shape, dt):
            Rot.cnt[0] += 1
            self.id = Rot.cnt[0]
            self.bufs = [AR.alloc(shape, dt) for _ in range(n)]
            self.i = 0

        def next(self):
            k = self.i % len(self.bufs)
            self.i += 1
            return self.bufs[k], ('rot', self.id, k)

    class PSRot:
        def __init__(self, banks):
            self.banks = banks
            self.i = 0

        def next(self):
            b = self.banks[self.i % len(self.banks)]
            self.i += 1
            return psb[b], ('ps', b)
    PSA = PSRot([0, 1])
    PSB = PSRot([2, 3, 4, 5])

    class PSQ:
        def __init__(self):
            self.i = 0
            self.banks = [6, 7]

        def next(self):
            b = self.banks[self.i % len(self.banks)]
            self.i += 1
            return psb[b][:, 0:128], ('ps', b)
    PQ = PSQ()

    passes = n_tiles + (1 if do_sample else 0)
    CASTW = 32 * 1024
    ncast = (WTOT + CASTW - 1) // CASTW

    class WStream:
        def __init__(self):
            self.seq = [i % NCH for i in range(passes * NCH)]
            self.nl = 0
            self.na = 0

        def _load(self):
            if self.nl >= len(self.seq):
                return
            ci = self.seq[self.nl]
            s = self.nl % NSLOT
            off, used = offs[ci]
            rd = [('wbf', k) for k in range(off // CASTW, (off + used - 1) // CASTW + 1)]
            DMA('sp', wring[:, s, 0:used], wbf[:, off:off + used], rd=rd, wr=[('w', s)], lane=('w', s))
            self.nl += 1

        def prime(self):
            for _ in range(NSLOT):
                self._load()

        def acquire(self, expect):
            ci = self.seq[self.na]
            assert expect is None or plan[ci][0][0] == expect, (plan[ci], expect)
            s = self.na % NSLOT
            self.na += 1
            return wring[:, s, :], ('w', s)

        def release(self):
            self._load()
    W = WStream()

    DMA('sp', cv, cvec_d, wr=['cv'], lane='cv')
    for k in range(ncast):
        c0 = k * CASTW
        c1 = min(WTOT, c0 + CASTW)
        DMA('pool', wbf[:, c0:c1].rearrange("p (a b) -> p a b", b=1024),
            wpack[:, c0:c1].rearrange("p (a b) -> p a b", b=1024), wr=[('wbf', k)], lane='cast')
    MS('dve', ident, 0.0, ['ident'])
    ASEL(ident, ident, [[-1, 128]], ALU.not_equal, 1.0, 0, 1, ['ident'], ['ident'])
    MS('dve', ones_f, 1.0, ['ones_f'])
    MS('dve', ones_b, 1.0, ['ones_b'])
    MS('dve', nones_b, -1.0, ['nones_b'])
    MS('dve', mask_lt, 1.0, ['mask_lt'])
    ASEL(mask_lt, mask_lt, [[1, 128]], ALU.is_gt, 0.0, 0, -1, ['mask_lt'], ['mask_lt'])
    MS('dve', negUI, -1.0, ['negUI'])
    ASEL(negUI, negUI, [[-1, 128]], ALU.is_ge, 0.0, 0, 1, ['negUI'], ['negUI'])
    MS('dve', minclT, 1.0, ['minclT'])
    ASEL(minclT, minclT, [[1, 128]], ALU.is_ge, 0.0, 0, -1, ['minclT'], ['minclT'])
    MS('pool', minclT[0:64, 64:128], 0.0, ['minclT'])
    MS('dve', nminclT, -1.0, ['nminclT'])
    ASEL(nminclT, nminclT, [[1, 128]], ALU.is_ge, 0.0, 0, -1, ['nminclT'], ['nminclT'])
    MS('pool', nminclT[0:64, 64:128], 0.0, ['nminclT'])
    MS('dve', mstrict, 1.0, ['mstrict'])
    ASEL(mstrict, mstrict, [[-1, 128]], ALU.is_gt, 0.0, 0, 1, ['mstrict'], ['mstrict'])
    MS('pool', mstrict[64:128, 0:64], 0.0, ['mstrict'])
    MS('dve', blockones, 1.0, ['blockones'])
    MS('pool', blockones[0:64, 64:128], 0.0, ['blockones'])
    MS('pool', blockones[64:128, 0:64], 0.0, ['blockones'])
    for g in range(4):
        w_ = 2 ** (g + 1)
        MS('dve', invc[:, g, :], 1.0 / w_, ['invc'])
        for t in range(w_ - 1):
            MS('dve', invc[:, g, t:t + 1], 1.0 / (t + 1), ['invc'])
    ACT(nexpA, cv[:, CV['alog']:CV['alog'] + 8], AF.Exp, ['cv'], ['nexpA'])
    TS('dve', nexpA, nexpA, -1.0, None, ALU.mult, None, ['nexpA'], ['nexpA'])
    W.prime()
    S.barrier()

    def rr_gen(gens):
        gens = list(gens)
        while gens:
            for g in list(gens):
                try:
                    next(g)
                except StopIteration:
                    gens.remove(g)
            yield

    def run_rr(gens):
        gens = list(gens)
        while gens:
            for g in list(gens):
                try:
                    next(g)
                except StopIteration:
                    gens.remove(g)

    def cvcol(name, idx):
        o = CV[name] + idx
        return cv[:, o:o + 1]

    def rmsnorm(T, gname, gbase, out_fn, out_keys, out_dt_is_f32=False):
        ps, pk = PSA.next()
        for c in range(8):
            b = c % 2
            ACT(nsq[:, b, 0:T], xT[:, c, 0:T], AF.Square, [('xT', c)], [('nsq', b)])
            MM(ps[:, 0:T], ones_b, nsq[:, b, 0:T], c == 0, c == 7, [('nsq', b)], [pk])
        ACT(nrt[:, 0:T], ps[:, 0:T], AF.Ln, [pk], ['nrt'], bias=EPS, scale=1.0 / D)
        ACT(nrstd[:, 0:T], nrt[:, 0:T], AF.Exp, ['nrt'], ['nrstd'], scale=-0.5)
        for c in range(8):
            STT(out_fn(c), xT[:, c, 0:T], cvcol(gname, gbase + c), nrstd[:, 0:T], ALU.mult, ALU.mult,
                [('xT', c), 'nrstd'], [out_keys(c)])

    def norm_to_hT(T, gname, gbase):
        rmsnorm(T, gname, gbase, lambda c: hT[:, c, 0:T], lambda c: ('hT', c))

    HT_ALL = [('hT', c) for c in range(8)]

    def pool_layer(T, j, l, first):
        S.phase = 'pool%d' % l
        S.barrier()
        AR.reset()
        L = 15 + T
        ext = AR.alloc([8, L], F32)
        dT = AR.alloc([8, T], BF16)
        tmps = [[AR.alloc([2, L], F32) for _ in range(2)] for _ in range(4)]
        fix = AR.alloc([8, 15], F32)
        CP('pool', ext[:, :, 0:15], poolhist[:, j, :, :], [('phist', j)], ['ext_h'])
        rmsnorm(T, 'mixn', 8 * l, lambda c: ext[:, c, 15:L], lambda c: ('ext', c))
        CP('pool', poolhist[:, j, :, :], ext[:, :, T:L], ['ext_h'] + [('ext', c) for c in range(8)], [('phist', j)])
        slot, sk = W.acquire('pool')
        for g in range(4):
            eng = 'dve' if g % 2 == 0 else 'pool'
            cs = [2 * g, 2 * g + 1]
            ekeys = ['ext_h', ('ext', cs[0]), ('ext', cs[1])]
            src = ext[:, 2 * g:2 * g + 2, :]
            lo = 0
            rkeys = ekeys
            for st in range(g + 1):
                sh = 2 ** st
                dst = tmps[g][st % 2]
                nlo = lo + sh
                TT(eng, dst[:, :, nlo:L], src[:, :, nlo:L], src[:, :, lo:L - sh], ALU.add, rkeys, [('ptmp', g, st % 2)])
                src = dst
                lo = nlo
                rkeys = [('ptmp', g, st % 2)]
            wdw = 2 ** (g + 1)
            for ci, c in enumerate(cs):
                STT(dT[:, c, :], src[:, ci, 15:L], 1.0 / wdw, ext[:, c, 15:L], ALU.mult, ALU.subtract,
                    rkeys + ekeys, [('dT', c)])
                if first:
                    TT('dve', fix[:, c, :], src[:, ci, 15:30], invc[:, g, :], ALU.mult, rkeys, [('fix', c)])
                    TT('dve', dT[:, c, 0:15], fix[:, c, :], ext[:, c, 15:30], ALU.subtract,
                       [('fix', c)] + ekeys, [('dT', c)])
        for g in range(4):
            for ec in range(2):
                ps, pk = PSA.next()
                for kc in range(2):
                    o = ((g * 2 + ec) * 2 + kc) * 128
                    MM(ps[:, 0:T], slot[:, o:o + 128], dT[:, 2 * g + kc, :], kc == 0, kc == 1, [sk, ('dT', 2 * g + kc)], [pk])
                c = 2 * g + ec
                STT(xT[:, c, 0:T], ps[:, 0:T], cvcol('pscale', 8 * j + c), xT[:, c, 0:T], ALU.mult, ALU.add,
                    [pk, ('xT', c)], [('xT', c)])
        W.release()

    def ffn_layer(T, l):
        S.phase = 'ffn%d' % l
        S.barrier()
        AR.reset()
        PSB.banks = [2, 3, 4, 5, 6, 7]
        actT = AR.alloc([NFC, T], BF16)
        gext = Rot(3, [T + 2], F32)
        tb = Rot(3, [T], F32)
        sb_ = Rot(2, [T], F32)
        norm_to_hT(T, 'ffnn', 8 * l)
        for cc in range(NFC // 2):
            slot, sk = W.acquire('up')
            for ci in range(2):
                c = 2 * cc + ci
                psv, kv = PSB.next()
                psg, kg = PSB.next()
                for kc in range(8):
                    o = (ci * 2) * 1024 + kc * 128
                    MM(psv[:, 0:T], slot[:, o:o + 128], hT[:, kc, 0:T], kc == 0, kc == 7, [sk, ('hT', kc)], [kv])
                for kc in range(8):
                    o = (ci * 2 + 1) * 1024 + kc * 128
                    MM(psg[:, 0:T], slot[:, o:o + 128], hT[:, kc, 0:T], kc == 0, kc == 7, [sk, ('hT', kc)], [kg])
                ge, gk = gext.next()
                CP('pool', ge[:, 0:2], ghist[:, l, c, :], [('ghist', l)], [gk])
                CP('act', ge[:, 2:T + 2], psg[:, 0:T], [kg], [gk])
                CP('pool', ghist[:, l, c, :], ge[:, T:T + 2], [gk], [('ghist', l)])
                t, tk = tb.next()
                wb = CV['fcw'] + (l * NFC + c) * 3
                TS('dve', t, ge[:, 0:T], cv[:, wb:wb + 1], None, ALU.mult, None, [gk], [tk])
                STT(t, ge[:, 1:T + 1], cv[:, wb + 1:wb + 2], t, ALU.mult, ALU.add, [gk, tk], [tk])
                STT(t, ge[:, 2:T + 2], cv[:, wb + 2:wb + 3], t, ALU.mult, ALU.add, [gk, tk], [tk])
                s_, sk2 = sb_.next()
                ACT(s_, t, AF.Silu, [tk], [sk2], bias=cvcol('fcb', l * NFC + c))
                TT('dve', actT[:, c, :], s_, psv[:, 0:T], ALU.mult, [sk2, kv], [('actT', c)])
            W.release()
        for j in range(8):
            slot, sk = W.acquire('down')
            ps, pk = PSA.next()
            for c in range(NFC):
                MM(ps[:, 0:T], slot[:, c * 128:(c + 1) * 128], actT[:, c, :], c == 0, c == NFC - 1,
                   [sk, ('actT', c)], [pk])
            TT('dve', xT[:, j, 0:T], ps[:, 0:T], xT[:, j, 0:T], ALU.add, [pk, ('xT', j)], [('xT', j)])
            W.release()

    def dn_layer(T):
        S.phase = 'dn'
        S.barrier()
        AR.reset()
        NB = (T + 127) // 128
        BS = min(T, 128)
        NCHK = T // 64
        PSB.banks = [2, 3]
        PQ.banks = [4, 5, 6, 7]
        norm_to_hT(T, 'mixn', 8)
        ogT = AR.alloc([8, T], BF16)
        ab = AR.alloc([NB, 16], F32)
        gx = AR.alloc([NB, 8], F32)
        g_tm = AR.alloc([NB, 8], F32)
        nbeta = AR.alloc([NB, 8], F32)
        beta = AR.alloc([NB, 8], F32)
        gc_tm = AR.alloc([NB, 8], F32)
        ngc_tm = AR.alloc([NB, 8], F32)
        gl_tm = AR.alloc([NB, 8], F32)
        kgs = AR.alloc([NB, 8], F32)
        kbs = AR.alloc([NB, 8], F32)
        slot, sk = W.acquire('dn_ab')
        for nb in range(NB):
            ps, pk = PQ.next()
            for kc in range(8):
                MM(ps[0:BS, 0:16], hT[:, kc, nb * 128:nb * 128 + BS], slot[:, kc * 16:(kc + 1) * 16], kc == 0, kc == 7,
                   [sk, ('hT', kc)], [pk])
            CP('dve', ab[0:BS, nb, :], ps[0:BS, 0:16], [pk], ['ab'])
        W.release()
        A = slice(0, BS)
        for nb in range(NB):
            TT('dve', gx[A, nb, :], ab[A, nb, 0:8], cv[A, CV['dtb']:CV['dtb'] + 8], ALU.add, ['ab'], ['gx'])
        ACT(gx[A], gx[A], AF.Exp, ['gx'], ['gx'])
        ACT(gx[A], gx[A], AF.Ln, ['gx'], ['gx'], bias=1.0)
        for nb in range(NB):
            TT('dve', g_tm[A, nb, :], gx[A, nb, :], nexpA[A], ALU.mult, ['gx'], ['g_tm'])
        ACT(beta[A], ab[A, :, 8:16], AF.Exp, ['ab'], ['beta'], scale=-1.0)
        TS('dve', beta[A], beta[A], 1.0, None, ALU.add, None, ['beta'], ['beta'])
        RCP(beta[A], beta[A], ['beta'], ['beta'])
        TS('dve', nbeta[A], beta[A], -1.0, None, ALU.mult, None, ['beta'], ['nbeta'])
        for nb in range(NB):
            ps, pk = PQ.next()
            MM(ps[0:BS, 0:8], minclT[0:BS, 0:BS], g_tm[0:BS, nb, :], True, True, ['g_tm'], [pk])
            CP('dve', gc_tm[0:BS, nb, :], ps[0:BS, 0:8], [pk], ['gc_tm'])
            ps2, pk2 = PQ.next()
            MM(ps2[0:BS, 0:8], blockones[0:BS, 0:BS], g_tm[0:BS, nb, :], True, True, ['g_tm'], [pk2])
            CP('dve', gl_tm[0:BS, nb, :], ps2[0:BS, 0:8], [pk2], ['gl_tm'])
        TS('dve', ngc_tm[A], gc_tm[A], -1.0, None, ALU.mult, None, ['gc_tm'], ['ngc_tm'])
        TT('dve', kgs[A], gl_tm[A], gc_tm[A], ALU.subtract, ['gl_tm', 'gc_tm'], ['kgs'])
        ACT(kgs[A], kgs[A], AF.Exp, ['kgs'], ['kgs'])
        ACT(kbs[A], gc_tm[A], AF.Exp, ['gc_tm'], ['kbs'])
        TT('dve', kbs[A], kbs[A], beta[A], ALU.mult, ['kbs', 'beta'], ['kbs'])

        cext = Rot(2, [T + 3], F32)
        tbuf = Rot(2, [T], F32)
        qkvs = [AR.alloc([T], F32) for _ in range(3)]
        sqb = Rot(2, [T], BF16)
        rr = Rot(2, [T], F32)
        kbg = AR.alloc([NB, 128], BF16)
        vb = AR.alloc([NB, 128], BF16)
        vnew = AR.alloc([NB, 128], BF16)
        attnFs = [Rot(2, [128], F32) for _ in range(NB)]
        qnb = AR.alloc([T], BF16)
        knb = AR.alloc([T], BF16)
        Ybs = [Rot(2, [128], BF16) for _ in range(NB)]
        m128s = [Rot(12, [128], F32) for _ in range(NB)]
        mbs = [Rot(12, [128], BF16) for _ in range(NB)]
        og = AR.alloc([T], F32)
        PBUF = [(AR.alloc([T], F32), AR.alloc([NB, 128], BF16), AR.alloc([NB, 128], F32), AR.alloc([T], BF16), AR.alloc([T], BF16),
                 AR.alloc([NB, 128], BF16), AR.alloc([T], F32)) for _ in range(2)]
        PSA.banks = [2, 3]
        kn2 = AR.alloc([T], BF16)

        def front(h):
            p = h % 2
            zs, kgm, u_, wT, qgT, attnT, EG = PBUF[p]
            slot, sk = W.acquire('dn_u')
            for idx in range(3):
                ps, pk = PSB.next()
                for kc in range(8):
                    o = idx * 1024 + kc * 128
                    MM(ps[:, 0:T], slot[:, o:o + 128], hT[:, kc, 0:T], kc == 0, kc == 7, [sk, ('hT', kc)], [pk])
                ce, ck = cext.next()
                ch = idx * 8 + h
                CP('pool', ce[:, 0:3], chist[:, ch, :], ['chist'], [ck])
                CP('act', ce[:, 3:T + 3], ps[:, 0:T], [pk], [ck])
                CP('pool', chist[:, ch, :], ce[:, T:T + 3], [ck], ['chist'])
                t, tk = tbuf.next()
                wb = CV['dcw'] + ch * 4
                TS('dve', t, ce[:, 0:T], cv[:, wb:wb + 1], None, ALU.mult, None, [ck], [tk])
                for tap in range(1, 4):
                    STT(t, ce[:, tap:T + tap], cv[:, wb + tap:wb + tap + 1], t, ALU.mult, ALU.add, [ck, tk], [tk])
                ACT(qkvs[idx], t, AF.Silu, [tk], [('qkv', idx)])
                yield
            ps, pk = PSB.next()
            for kc in range(8):
                o = 3 * 1024 + kc * 128
                MM(ps[:, 0:T], slot[:, o:o + 128], hT[:, kc, 0:T], kc == 0, kc == 7, [sk, ('hT', kc)], [pk])
            ACT(zs, ps[:, 0:T], AF.Silu, [pk], [('zs', p)])
            W.release()
            yield
            for idx in range(2):
                sq, sqk = sqb.next()
                ACT(sq, qkvs[idx], AF.Square, [('qkv', idx)], [sqk])
                ps, pk = PSA.next()
                MM(ps[:, 0:T], ones_b, sq, True, True, [sqk], [pk])
                r, rk = rr.next()
                ACT(r, ps[:, 0:T], AF.Ln, [pk], [rk], bias=EPS)
                ACT(r, r, AF.Exp, [rk], [rk], scale=-0.5)
                STT(qkvs[idx], qkvs[idx], (128.0 ** -0.5) if idx == 0 else 1.0, r, ALU.mult, ALU.mult,
                    [('qkv', idx), rk], [('qkv', idx)])
            yield
            qn, kn, vs_ = qkvs
            CP('pool', kn2, kn, [('qkv', 1)], ['kn2'])
            CP('act', knb, kn, [('qkv', 1)], ['knb'])
            CP('pool', qnb, qn, [('qkv', 0)], ['qnb'])
            def blk_gen(nb, h=h, kn=kn, qn=qn, vs_=vs_):
                m128 = m128s[nb]
                mb = mbs[nb]
                attnF = attnFs[nb]
                Yb = Ybs[nb]
                cs = slice(nb * 128, nb * 128 + BS)
                R = slice(0, BS)
                bcol = lambda tl: tl[0:BS, nb, h:h + 1]
                ps, pk = PQ.next()
                TR(ps[0:BS, :], kn[:, cs], ident, [('qkv', 1), 'ident'], [pk])
                TS('dve', kbg[R, nb, :], ps[0:BS, :], bcol(kbs), None, ALU.mult, None, [pk, 'kbs'], [('kbg', nb)])
                TS('dve', kgm[R, nb, :], ps[0:BS, :], bcol(kgs), None, ALU.mult, None, [pk, 'kgs'], [('kgm', p, nb)])
                ps, pk = PQ.next()
                TR(ps[0:BS, :], vs_[:, cs], ident, [('qkv', 2), 'ident'], [pk])
                TS('dve', vb[R, nb, :], ps[0:BS, :], bcol(beta), None, ALU.mult, None, [pk, 'beta'], [('vb', nb)])
                yield
                gb2, gb2k = m128.next()
                TS('dve', gb2[R, :], ones_f[R, :], bcol(g_tm), None, ALU.mult, None, ['g_tm'], [gb2k])
                pn, pnk = PQ.next()
                MM(pn[R, 0:BS], gb2[R, 0:BS], nminclT[R, 0:BS], True, True, [gb2k], [pnk])
                pg, pgk = PQ.next()
                MM(pg[:, 0:BS], gb2[R, :], minclT[R, 0:BS], True, True, [gb2k], [pgk])
                E, Ek = m128.next()
                TS('dve', E[R, 0:BS], pn[R, 0:BS], bcol(gc_tm), 0.0, ALU.add, ALU.min, [pnk, 'gc_tm'], [Ek])
                ACT(E[R, 0:BS], E[R, 0:BS], AF.Exp, [Ek], [Ek])
                ET, ETk = m128.next()
                TS('dve', ET[R, 0:BS], pg[R, 0:BS], bcol(ngc_tm), 0.0, ALU.add, ALU.min, [pgk, 'ngc_tm'], [ETk])
                ACT(ET[R, 0:BS], ET[R, 0:BS], AF.Exp, [ETk], [ETk])
                ACT(EG[:, cs], pg[:, 0:BS], AF.Exp, [pgk], [('EG', p, nb)])
                yield
                pa, pak = PQ.next()
                MM(pa[R, 0:BS], knb[:, cs], qnb[:, cs], True, True, ['qnb', 'knb'], [pak])
                af, afk = attnF.next()
                TT('dve', af[R, 0:BS], pa[R, 0:BS], ET[R, 0:BS], ALU.mult, [pak, ETk], [afk])
                TT('dve', attnT[R, nb, 0:BS], af[R, 0:BS], minclT[R, 0:BS], ALU.mult, [afk], [('attnT', p, nb)])
                TT('dve', qgT[:, cs], qn[:, cs], EG[:, cs], ALU.mult, [('qkv', 0), ('EG', p, nb)], [('qgT', p, nb)])
                yield
                pgm, pgmk = PQ.next()
                MM(pgm[R, 0:BS], knb[:, cs], kn2[:, cs], True, True, ['knb', 'kn2'], [pgmk])
                Am, Ak = m128.next()
                STT(Am[R, 0:BS], pgm[R, 0:BS], bcol(nbeta), E[R, 0:BS], ALU.mult, ALU.mult, [pgmk, 'nbeta', Ek], [Ak])
                TT('dve', Am[R, 0:BS], Am[R, 0:BS], mstrict[R, 0:BS], ALU.mult, [Ak], [Ak])
                pt, ptk = PQ.next()
                TR(pt[R, 0:BS], Am[R, 0:BS], ident[R, 0:BS], [Ak], [ptk])
                Bm, Bk = m128.next()
                CP('act', Bm[R, 0:BS], pt[R, 0:BS], [ptk], [Bk])
                yield
                QA, QAk = mb.next()
                CP('dve', QA[R, 0:BS], Am[R, 0:BS], [Ak], [QAk])
                QB, QBk = mb.next()
                CP('act', QB[R, 0:BS], Bm[R, 0:BS], [Bk], [QBk])
                Y, Yk = mb.next()
                TT('dve', Y[R, 0:BS], Bm[R, 0:BS], ident[R, 0:BS], ALU.add, [Bk], [Yk])
                for k in range(1, 6):
                    yield
                    if k < 5:
                        p1, p1k = PQ.next()
                        MM(p1[R, 0:BS], QA[R, 0:BS], QB[R, 0:BS], True, True, [QAk, QBk], [p1k])
                        QBn, QBnk = mb.next()
                        CP('act', QBn[R, 0:BS], p1[R, 0:BS], [p1k], [QBnk])
                    p2, p2k = PQ.next()
                    MM(p2[R, 0:BS], QB[R, 0:BS], QA[R, 0:BS], True, True, [QAk, QBk], [p2k])
                    QAn, QAnk = mb.next()
                    CP('act', QAn[R, 0:BS], p2[R, 0:BS], [p2k], [QAnk])
                    p3, p3k = PQ.next()
                    MM(p3[R, 0:BS], QAn[R, 0:BS], Y[R, 0:BS], True, True, [QAnk, Yk], [p3k])
                    Yn, Ynk = mb.next()
                    TT('dve', Yn[R, 0:BS], p3[R, 0:BS], Y[R, 0:BS], ALU.add, [p3k, Yk], [Ynk])
                    Y, Yk = Yn, Ynk
                    QA, QAk = QAn, QAnk
                    if k < 5:
                        QB, QBk = QBn, QBnk
                yield
                WT, WTk = m128.next()
                TT('dve', WT[R, 0:BS], ident[R, 0:BS], Am[R, 0:BS], ALU.subtract, [Ak], [WTk])
                Y0f, Y0k = m128.next()
                CP('dve', Y0f[R, 0:BS], Y[R, 0:BS], [Yk], [Y0k])
                px, pxk = PQ.next()
                TR(px[R, 0:BS], Y0f[R, 0:BS], ident[R, 0:BS], [Y0k], [pxk])
                Xm, Xk = m128.next()
                CP('act', Xm[R, 0:BS], px[R, 0:BS], [pxk], [Xk])
                yield
                pn1, pn1k = PQ.next()
                MM(pn1[R, 0:BS], WT[R, 0:BS], Y0f[R, 0:BS], True, True, [WTk, Y0k], [pn1k])
                Rm, Rmk = m128.next()
                TT('dve', Rm[R, 0:BS], ident[R, 0:BS], pn1[R, 0:BS], ALU.subtract, [pn1k], [Rmk])
                yield
                pn2, pn2k = PQ.next()
                MM(pn2[R, 0:BS], Xm[R, 0:BS], Rm[R, 0:BS], True, True, [Xk, Rmk], [pn2k])
                Y, Yk = m128.next()
                TT('dve', Y[R, 0:BS], pn2[R, 0:BS], Y0f[R, 0:BS], ALU.add, [pn2k, Y0k], [Yk])
                yield
                pu, puk = PQ.next()
                yb, ybk = Yb.next()
                CP('dve', yb[R, 0:BS], Y[R, 0:BS], [Yk], [ybk])
                MM(pu[R, :], yb[R, 0:BS], vb[R, nb, :], True, True, [ybk, ('vb', nb)], [puk])
                CP('act', u_[R, nb, :], pu[R, :], [puk], [('u', p, nb)])
                pw, pwk = PQ.next()
                MM(pw[:, 0:BS], kbg[R, nb, :], yb[R, 0:BS], True, True, [ybk, ('kbg', nb)], [pwk])
                CP('act', wT[:, cs], pw[:, 0:BS], [pwk], [('wT', p, nb)])
            yield
            for _ in rr_gen([blk_gen(nb) for nb in range(NB)]):
                yield

        def rec(h):
            p = h % 2
            zs, kgm, u_, wT, qgT, attnT, EG = PBUF[p]
            po, pok = psb[p], ('ps', p)
            Sh = Sdn[:, h, :]
            Sk = ('Sdn', h)
            Sb = Sdb[:, h, :]
            Sbk = ('Sdb', h)
            CP('act', Sb, Sh, [Sk], [Sbk])
            for ci in range(NCHK):
                nb = ci // 2
                r0 = (ci % 2) * 64
                c0 = ci * 64
                RR = slice(r0, r0 + 64)
                p1, p1k = PQ.next()
                MM(p1[RR, :], wT[:, c0:c0 + 64], Sb, True, True, [('wT', p, nb), Sbk], [p1k])
                TT('dve', vnew[RR, nb, :], u_[RR, nb, :], p1[RR, :], ALU.subtract, [('u', p, nb), p1k], [('vnew', ci)])
                yield
                MM(po[:, c0:c0 + 64], Sb, qgT[:, c0:c0 + 64], True, False, [Sbk, ('qgT', p, nb)], [pok])
                MM(po[:, c0:c0 + 64], vnew[RR, nb, :], attnT[RR, nb, r0:r0 + 64], False, True,
                   [('vnew', ci), ('attnT', p, nb)], [pok])
                p2, p2k = PQ.next()
                MM(p2, kgm[RR, nb, :], vnew[RR, nb, :], True, True, [('kgm', p, nb), ('vnew', ci)], [p2k])
                STT(Sh, Sh, EG[:, c0 + 63:c0 + 64], p2, ALU.mult, ALU.add, [Sk, ('EG', p, nb), p2k], [Sk])
                if ci < NCHK - 1:
                    CP('act', Sb, Sh, [Sk], [Sbk])
                yield
            sq, sqk = sqb.next()
            ACT(sq, po[:, 0:T], AF.Square, [pok], [sqk])
            ps, pk = PSA.next()
            MM(ps[:, 0:T], ones_b, sq, True, True, [sqk], [pk])
            r, rk = rr.next()
            ACT(r, ps[:, 0:T], AF.Ln, [pk], [rk], bias=EPS, scale=1.0 / 128)
            ACT(r, r, AF.Exp, [rk], [rk], scale=-0.5)
            STT(og, po[:, 0:T], cvcol('dnorm', 0), r, ALU.mult, ALU.mult, [pok, rk], ['og'])
            TT('dve', ogT[:, h, :], og, zs, ALU.mult, ['og', ('zs', p)], [('ogT', h)])
        def exhaust(g):
            for _ in g:
                pass
        exhaust(front(0))
        for h in range(8):
            run_rr([rec(h)] + ([front(h + 1)] if h + 1 < 8 else []))
        PSA.banks = [0, 1]
        for n2 in range(2):
            slot, sk = W.acquire('dn_out')
            for i in range(4):
                n = 4 * n2 + i
                ps, pk = PSA.next()
                for hh in range(8):
                    o = i * 1024 + hh * 128
                    MM(ps[:, 0:T], slot[:, o:o + 128], ogT[:, hh, :], hh == 0, hh == 7, [sk, ('ogT', hh)], [pk])
                TT('dve', xT[:, n, 0:T], ps[:, 0:T], xT[:, n, 0:T], ALU.add, [pk, ('xT', n)], [('xT', n)])
            W.release()

    def sb_layer(T, grp, ti):
        S.phase = 'sb'
        S.barrier()
        AR.reset()
        NB = (T + 127) // 128
        BS = min(T, 128)
        t0 = ti * T
        PSB.banks = [2, 3, 4, 5]
        norm_to_hT(T, 'mixn', 16)
        qT = AR.alloc([8, T], BF16)
        kTb = AR.alloc([8, T], BF16)
        oT = AR.alloc([8, T], BF16)
        kst = Rot(2, [T], F32)
        vst = Rot(2, [512], F32)
        vbb = AR.alloc([NB, 1024], BF16)
        if grp == 'p':
            Stot = t0 + T
        else:
            Stot = PAST + T
        NBLK = (Stot + 127) // 128
        KTc = Rot(2, [Stot], BF16)
        Vc = Rot(2, [NBLK, 128], BF16)
        ebuf = Rot(8, [T], F32)
        spbuf = Rot(12, [T], BF16)
        abuf = Rot(8, [T], BF16)
        Sruns = [AR.alloc([T], F32) for _ in range(4)]
        Sbfs = [AR.alloc([T], BF16) for _ in range(4)]
        for n4 in range(4):
            slot, sk = W.acquire('sb_u')
            for i in range(4):
                cidx = 4 * n4 + i
                ps, pk = PSB.next()
                for kc in range(8):
                    o = i * 1024 + kc * 128
                    MM(ps[:, 0:T], slot[:, o:o + 128], hT[:, kc, 0:T], kc == 0, kc == 7, [sk, ('hT', kc)], [pk])
                if cidx < 8:
                    S.emit('act', (lambda o_, i_: (lambda e: e.mul(out=o_, in_=i_, mul=0.125)))(qT[:, cidx, :], ps[:, 0:T]),
                           [pk], [('qT', cidx)])
                else:
                    c = cidx - 8
                    st, stk = kst.next()
                    CP('act', st, ps[:, 0:T], [pk], [stk])
                    DMA('sp', kTo[grp][c * 128:(c + 1) * 128, t0:t0 + T], st, [stk], [], lane=('kst', stk[2]))
                    CP('dve', kTb[:, c, :], st, [stk], [('kTb', c)])
            W.release()
        if grp == 'p':
            DMA('sp', KTs[:, :, t0:t0 + T].rearrange("c p t -> p c t"), kTb, [('kTb', c) for c in range(8)], ['KTs'], lane='kts')
        for half in range(2):
            slot, sk = W.acquire('sb_v')
            for nb in range(NB):
                ps, pk = PSB.next()
                for kc in range(8):
                    MM(ps[0:BS, :], hT[:, kc, nb * 128:nb * 128 + BS], slot[:, kc * 512:(kc + 1) * 512], kc == 0, kc == 7,
                       [sk, ('hT', kc)], [pk])
                st, stk = vst.next()
                CP('act', st[0:BS, :], ps[0:BS, :], [pk], [stk])
                DMA('sp', vo[grp][t0 + nb * 128:t0 + nb * 128 + BS, half * 512:(half + 1) * 512], st[0:BS, :], [stk], [],
                    lane=('vst', stk[2]))
                CP('dve', vbb[0:BS, nb, half * 512:(half + 1) * 512], st[0:BS, :], [stk], [('vbb', nb, half)])
            W.release()
        VBB = [('vbb', nb, half) for nb in range(NB) for half in range(2)]
        if grp == 'p':
            for nb in range(NB):
                DMA('sp', Vs[:, :, t0 // 128 + nb, :].rearrange("c p n -> p c n"),
                    vbb[:, nb, :].rearrange("p (c n) -> p c n", c=8), VBB, ['Vs'], lane='vs', same_gen=(nb > 0))
        PO_BANKS = [0, 0, 1, 1]
        PSB.banks = [2, 3, 4, 5, 6, 7]
        for c0 in range(0, 8, 2):
          gens = []
          for c in (c0, c0 + 1):
            kt, ktk = KTc.next()
            vc, vck = Vc.next()
            bi = ktk[2]
            if grp == 'p':
                DMA('sp', kt[:, 0:Stot], KTs[c, :, 0:Stot], ['KTs'], [ktk], lane=('ktc', bi))
                DMA('sp', vc[:, 0:NBLK, :], Vs[c, :, 0:NBLK, :], ['Vs'], [vck], lane=('vc', bi))
            else:
                DMA('pool', kt[:, 0:PAST].rearrange("p (a b) -> p a b", b=1024), cacheKT[c * 128:(c + 1) * 128, :].rearrange("p (a b) -> p a b", b=1024), [], [ktk], lane=('ktcs', bi))
                CP('dve', kt[:, PAST:PAST + T], kTb[:, c, :], [('kTb', c)], [ktk])
                DMA('pool', vc[:, 0:16, :], cacheV[c], [], [vck], lane=('vcs', bi))
                CP('dve', vc[0:BS, 16, :], vbb[0:BS, 0, c * 128:(c + 1) * 128], VBB, [vck])
            def head_gen(hh, gi, kt=kt, ktk=ktk, vc=vc, vck=vck, c=c):
                P0 = hh * 64
                PR = slice(P0, P0 + 64)
                po, pok = psb[PO_BANKS[gi]], ('ps', PO_BANKS[gi])
                Srun, Sbf = Sruns[gi], Sbfs[gi]
                srk, sbk = ('Srun', gi), ('Sbf', gi)
                MS('pool', Srun, 0.0, [srk])
                MS('pool', Sbf, 0.0, [sbk])
                def stage1(jb):
                    bs = min(128, Stot - jb * 128)
                    if grp == 'p':
                        diag = jb >= t0 // 128
                        qlo = (jb - t0 // 128) * 128 if diag else 0
                    else:
                        diag = jb == NBLK - 1
                        qlo = 0
                    dcols = min(128, T - qlo)
                    QS = slice(qlo, T)
                    B_ = slice(0, bs)
                    pz, pzk = PSB.next()
                    MM(pz[B_, QS], kt[PR, jb * 128:jb * 128 + bs], qT[PR, c, QS], True, True, [ktk, ('qT', c)], [pzk])
                    e_, ek = ebuf.next()
                    ACT(e_[B_, QS], pz[B_, QS], AF.Exp, [pzk], [ek])
                    sp, spk = spbuf.next()
                    ACT(sp[B_, QS], e_[B_, QS], AF.Ln, [ek], [spk], bias=1.0)
                    if diag:
                        TT('dve', sp[B_, qlo:qlo + dcols], sp[B_, qlo:qlo + dcols], mask_lt[B_, 0:dcols], ALU.mult, [spk], [spk])
                    return dict(jb=jb, bs=bs, diag=diag, qlo=qlo, dcols=dcols, QS=QS, B_=B_, sp=sp, spk=spk)

                units = list(range(NBLK - 1, -1, -1))
                ctx = stage1(units[0])
                yield
                for ui in range(len(units)):
                    first = ui == 0
                    nxt = stage1(units[ui + 1]) if ui + 1 < len(units) else None
                    yield
                    jb, bs, diag, qlo, dcols, QS, B_, sp, spk = (ctx[n] for n in ('jb', 'bs', 'diag', 'qlo', 'dcols', 'QS', 'B_', 'sp', 'spk'))
                    pz, pzk = PSB.next()
                    MM(pz[B_, QS], kt[PR, jb * 128:jb * 128 + bs], qT[PR, c, QS], True, False, [ktk, ('qT', c)], [pzk])
                    yield
                    MM(pz[B_, QS], negUI[B_, B_], sp[B_, QS], False, first, [spk], [pzk])
                    if not first:
                        MM(pz[B_, QS], nones_b[:, B_], Sbf[:, QS], False, True, [sbk], [pzk])
                    a_, ak = abuf.next()
                    if diag and qlo > 0:
                        MS('pool', a_[B_, 0:qlo], 0.0, [ak])
                    ACT(a_[B_, QS], pz[B_, QS], AF.Exp, [pzk], [ak])
                    if diag:
                        TT('dve', a_[B_, qlo:qlo + dcols], a_[B_, qlo:qlo + dcols], mask_lt[B_, 0:dcols], ALU.mult, [ak], [ak])
                    if jb > 0:
                        TT('dve', Srun[B_, QS], Srun[B_, QS], sp[B_, QS], ALU.add, [srk, spk], [srk])
                        CP('dve', Sbf[B_, QS], Srun[B_, QS], [srk], [sbk])
                    yield
                    MM(po[PR, 0:T], vc[B_, jb, hh * 64:(hh + 1) * 64], a_[B_, 0:T], first, jb == 0, [vck, ak], [pok])
                    yield
                    ctx = nxt
                CP('act', oT[PR, c, :], po[PR, 0:T], [pok], [('oT', c)])
            gens.append(head_gen(0, len(gens)))
            gens.append(head_gen(1, len(gens)))
          run_rr(gens)
        for n2 in range(2):
            slot, sk = W.acquire('sb_out')
            for i in range(4):
                n = 4 * n2 + i
                ps, pk = PSA.next()
                for cc in range(8):
                    o = i * 1024 + cc * 128
                    MM(ps[:, 0:T], slot[:, o:o + 128], oT[:, cc, :], cc == 0, cc == 7, [sk, ('oT', cc)], [pk])
                TT('dve', xT[:, n, 0:T], ps[:, 0:T], xT[:, n, 0:T], ALU.add, [pk, ('xT', n)], [('xT', n)])
            W.release()

    def final_layer(T, grp, ti):
        S.phase = 'final'
        S.barrier()
        AR.reset()
        t0 = ti * T
        yst = AR.alloc([8, T], F32)
        rmsnorm(T, 'finn', 0, lambda c: yst[:, c, :], lambda c: ('yst', c))
        DMA('sp', yT[grp].rearrange("(c p) t -> p c t", p=128)[:, :, t0:t0 + T], yst, [('yst', c) for c in range(8)], [],
            lane='yst')

    XT_ALL = [('xT', c) for c in range(8)]
    HIST = [('phist', 0), ('phist', 1), 'chist'] + [('Sdn', h) for h in range(8)] + [('ghist', l) for l in range(4)]

    def run_tile(grp, ti, T):
        t0 = ti * T
        src = xTp if grp == 'p' else xTs
        DMA('sp', xT[:, :, 0:T], src.rearrange("(c p) t -> p c t", p=128)[:, :, t0:t0 + T], [], XT_ALL, lane='xin')
        def skip(n):
            for _ in range(n):
                W.acquire(None)
                W.release()
        LY = layers if layers is not None else {'p0', 'f0', 'dn', 'f1', 'sb', 'f2', 'p3', 'f3'}
        fst = grp == 'p' and ti == 0
        pool_layer(T, 0, 0, fst) if 'p0' in LY else skip(1)
        ffn_layer(T, 0) if 'f0' in LY else skip(19)
        dn_layer(T) if 'dn' in LY else skip(11)
        ffn_layer(T, 1) if 'f1' in LY else skip(19)
        sb_layer(T, grp, ti) if 'sb' in LY else skip(8)
        ffn_layer(T, 2) if 'f2' in LY else skip(19)
        pool_layer(T, 1, 3, fst) if 'p3' in LY else skip(1)
        ffn_layer(T, 3) if 'f3' in LY else skip(19)
        final_layer(T, grp, ti)

    def store_states(grp):
        S.barrier()
        DMA('sp', o_pool[grp].rearrange("l p c t -> p l c t"), poolhist, [('phist', 0), ('phist', 1)], [], lane='st0')
        DMA('sp', o_dnc[grp], chist, ['chist'], [], lane='st1')
        DMA('sp', o_dn[grp], Sdn, [('Sdn', h) for h in range(8)], [], lane='st2')
        DMA('sp', o_ffn[grp].rearrange("l p c t -> p l c t"), ghist, [('ghist', l) for l in range(4)], [], lane='st3')

    MS('pool', poolhist, 0.0, [('phist', 0), ('phist', 1)])
    MS('pool', ghist, 0.0, [('ghist', l) for l in range(4)])
    MS('pool', chist, 0.0, ['chist'])
    MS('pool', Sdn, 0.0, [('Sdn', h) for h in range(8)])
    for ti in range(n_tiles):
        run_tile('p', ti, 512)
    store_states('p')
    if do_sample:
        S.barrier()
        DMA('sp', poolhist, st_pool.rearrange("l p c t -> p l c t"), [], [('phist', 0), ('phist', 1)], lane='ld0')
        DMA('sp', chist, st_dnc, [], ['chist'], lane='ld1')
        DMA('sp', Sdn, st_dn, [], [('Sdn', h) for h in range(8)], lane='ld2')
        DMA('sp', ghist, st_ffn.rearrange("l p c t -> p l c t"), [], [('ghist', l) for l in range(4)], lane='ld3')
        run_tile('s', 0, 64)
        store_states('s')
    S.finish()
    S.replay(nc)
    return nc, S


_CACHE = {}
_LAST = None
_DBG = None


def kernel(x_prompt, x_sample, state_pool, state_dn_conv, state_dn, cache_sb_k, cache_sb_v, state_ffn_conv,
           mix_norm, ffn_norm, final_norm, pool_w, pool_scale, dn_w_in, dn_conv_w, dn_a_log, dn_dt_bias,
           dn_norm, dn_w_out, sb_w_qkv, sb_w_out, ffn_w_up, ffn_conv_w, ffn_conv_b, ffn_w_down,
           _n_tiles=8, _do_sample=True, _layers=None):
    f = lambda a: np.asarray(a, np.float32)
    w = dict(mix_norm=f(mix_norm), ffn_norm=f(ffn_norm), final_norm=f(final_norm), pool_w=f(pool_w),
             pool_scale=f(pool_scale), dn_w_in=f(dn_w_in), dn_conv_w=f(dn_conv_w), dn_a_log=f(dn_a_log),
             dn_dt_bias=f(dn_dt_bias), dn_norm=f(dn_norm), dn_w_out=f(dn_w_out), sb_w_qkv=f(sb_w_qkv),
             sb_w_out=f(sb_w_out), ffn_w_up=f(ffn_w_up), ffn_conv_w=f(ffn_conv_w), ffn_conv_b=f(ffn_conv_b),
             ffn_w_down=f(ffn_w_down))
    wpack = pack_weights(w)
    cvec = pack_cvec(w)
    x_prompt = f(x_prompt); x_sample = f(x_sample); state_pool = f(state_pool); state_dn_conv = f(state_dn_conv)
    state_dn = f(state_dn); cache_sb_k = f(cache_sb_k); cache_sb_v = f(cache_sb_v); state_ffn_conv = f(state_ffn_conv)
    B = 8
    in_maps = []
    for b in range(B):
        m = {
            "xTp": np.ascontiguousarray(x_prompt[b].T),
            "xTs": np.ascontiguousarray(x_sample[b].T),
            "wpack": wpack,
            "cvec": cvec,
            "st_pool": np.ascontiguousarray(state_pool[:, b].reshape(2, 15, 8, 128).transpose(0, 3, 2, 1)),
            "st_dnc": np.ascontiguousarray(state_dn_conv[0, b].reshape(3, 24, 128).transpose(2, 1, 0)),
            "st_dn": np.ascontiguousarray(state_dn[0, b].transpose(1, 0, 2)),
            "st_ffn": np.ascontiguousarray(state_ffn_conv[:, b].reshape(4, 2, NFC, 128).transpose(0, 3, 2, 1)),
            "cacheKT": np.ascontiguousarray(cache_sb_k[0, b].reshape(PAST, D).T),
            "cacheV": np.ascontiguousarray(cache_sb_v[0, b].reshape(16, 128, 8, 128).transpose(2, 1, 0, 3)),
        }
        in_maps.append(m)
    key = (_n_tiles, _do_sample, None if _layers is None else tuple(sorted(_layers)))
    if key not in _CACHE:
        _CACHE[key] = build_program(_n_tiles, _do_sample, _layers)[0]
    nc = _CACHE[key]
    res = run_bass_kernel_spmd(nc, in_maps, core_ids=list(range(B)))
    R = res.results
    global _LAST
    _LAST = R

    def st(name, fn):
        return np.stack([fn(np.asarray(R[b][name])) for b in range(B)], axis=0)
    y_p = st("yTp", lambda a: a.T)
    y_s = st("yTs", lambda a: a.T)
    pool_p = st("o_poolp", lambda a: a.transpose(0, 3, 2, 1).reshape(2, 15, D)).transpose(1, 0, 2, 3)
    pool_s = st("o_pools", lambda a: a.transpose(0, 3, 2, 1).reshape(2, 15, D)).transpose(1, 0, 2, 3)
    dnc_p = st("o_dncp", lambda a: a.transpose(2, 1, 0).reshape(3, 3072))[None]
    dnc_s = st("o_dncs", lambda a: a.transpose(2, 1, 0).reshape(3, 3072))[None]
    dn_p = st("o_dnp", lambda a: a.transpose(1, 0, 2))[None]
    dn_s = st("o_dns", lambda a: a.transpose(1, 0, 2))[None]
    k_p = st("kTp", lambda a: a.T.reshape(SEQ, 16, 64))[None]
    k_s = st("kTs", lambda a: a.T.reshape(DSEQ, 16, 64))[None]
    v_p = st("vp", lambda a: a.reshape(SEQ, 16, 64))[None]
    v_s = st("vs", lambda a: a.reshape(DSEQ, 16, 64))[None]
    ffn_p = st("o_ffnp", lambda a: a.transpose(0, 3, 2, 1).reshape(4, 2, DFF)).transpose(1, 0, 2, 3)
    ffn_s = st("o_ffns", lambda a: a.transpose(0, 3, 2, 1).reshape(4, 2, DFF)).transpose(1, 0, 2, 3)
    outs = (y_p, y_s, pool_p, pool_s, dnc_p, dnc_s, dn_p, dn_s, k_p, k_s, v_p, v_s, ffn_p, ffn_s)
    return tuple(np.ascontiguousarray(o, dtype=np.float32) for o in outs)
```

```python
import bisect
import contextlib
import numpy as np
import concourse.bass as bass
import concourse.mybir as mybir
from concourse.bass_utils import run_bass_kernel_spmd

F32 = mybir.dt.float32
BF16 = mybir.dt.bfloat16
AF = mybir.ActivationFunctionType
ALU = mybir.AluOpType

D = 1024
SEQ = 4096
DSEQ = 64
PAST = 2048
DFF = 2816
NFC = 22
EPS = 1e-6
SLOT = 4096
NSLOT = 5
SAME_ENGINE_SYNC = True


class Sched:
    ENGS = ['pe', 'act', 'dve', 'pool', 'sp']

    def __init__(self):
        self.streams = {e: [] for e in self.ENGS}
        self.cnt = {e: 0 for e in self.ENGS}
        self.lane_cnt = {}
        self.seen = {e: {} for e in self.ENGS}
        self.snaps = {}
        self.res = {}
        self.nops = 0
        self.phase = 'setup'
        self.tags = {e: [] for e in self.ENGS}

    def _snap_put(self, key, val, d):
        vals, dicts = self.snaps.setdefault(key, ([], []))
        if dicts and dicts[-1] is d:
            return
        vals.append(val)
        dicts.append(d)

    def _snap_get(self, key, val):
        if key not in self.snaps:
            return None
        vals, dicts = self.snaps[key]
        i = bisect.bisect_right(vals, val) - 1
        return dicts[i] if i >= 0 else None

    def _wait(self, eng, key, val):
        seen = self.seen[eng]
        if seen.get(key, 0) >= val:
            return
        if key == ('eng', eng) and (eng == 'pe' or not SAME_ENGINE_SYNC):
            return
        self.streams[eng].append(('wait', key, val))
        new = dict(seen)
        new[key] = val
        sn = self._snap_get(key, val)
        if sn:
            for k, v in sn.items():
                if new.get(k, 0) < v:
                    new[k] = v
        self.seen[eng] = new

    def emit(self, eng, fn, reads=(), writes=(), lane=None, same_gen=False):
        deps = {}
        for r in reads:
            ent = self.res.get(r)
            if ent and ent[0]:
                k, v = ent[0]
                if deps.get(k, 0) < v:
                    deps[k] = v
            if ent and isinstance(r, tuple) and r[0] == 'ps':
                for k, v in ent[1].items():
                    if k != ('eng', eng) and deps.get(k, 0) < v:
                        deps[k] = v
        for w in writes:
            ent = self.res.get(w)
            if ent:
                if ent[0]:
                    k, v = ent[0]
                    if deps.get(k, 0) < v:
                        deps[k] = v
                for k, v in ent[1].items():
                    if deps.get(k, 0) < v:
                        deps[k] = v
        for k, v in deps.items():
            self._wait(eng, k, v)
        if lane is not None:
            key = ('lane', lane)
            cur = self.lane_cnt.get(lane, 0)
            if cur and not same_gen:
                self._wait(eng, key, cur)
            cur += 16
            self.lane_cnt[lane] = cur
            ref = (key, cur)
            self.streams[eng].append(('dma', fn, lane))
        else:
            key = ('eng', eng)
            self.cnt[eng] += 1
            ref = (key, self.cnt[eng])
            self.streams[eng].append(('op', fn))
            self.tags[eng].append(self.phase)
        self._snap_put(key, ref[1], self.seen[eng])
        for r in reads:
            ent = self.res.setdefault(r, [None, {}])
            if ent[1].get(key, 0) < ref[1]:
                ent[1][key] = ref[1]
        for w in writes:
            self.res[w] = [ref, {}]
        self.nops += 1
        return ref

    def barrier(self):
        for e in self.ENGS:
            for f in self.ENGS:
                if self.cnt[f]:
                    self._wait(e, ('eng', f), self.cnt[f])
            for l, c in self.lane_cnt.items():
                self._wait(e, ('lane', l), c)

    def finish(self):
        for l, c in self.lane_cnt.items():
            self._wait('sp', ('lane', l), c)
        for f in self.ENGS:
            if self.cnt[f]:
                self._wait('sp', ('eng', f), self.cnt[f])

    def replay(self, nc):
        with contextlib.ExitStack() as es:
            sems = {}
            for e in self.ENGS:
                if self.cnt[e]:
                    sems[('eng', e)] = es.enter_context(nc.semaphore("s_" + e))
            for i, l in enumerate(self.lane_cnt):
                sems[('lane', l)] = es.enter_context(nc.semaphore("l_%d" % i))
            block = es.enter_context(nc.Block())

            def run(engname):
                def body(eng):
                    for it in self.streams[engname]:
                        if it[0] == 'wait':
                            eng.wait_ge(sems[it[1]], it[2])
                        elif it[0] == 'op':
                            it[1](eng).then_inc(sems[('eng', engname)], 1)
                        else:
                            it[1](eng).then_inc(sems[('lane', it[2])], 16)
                return body
            block.tensor(run('pe'))
            block.scalar(run('act'))
            block.vector(run('dve'))
            block.gpsimd(run('pool'))
            block.sync(run('sp'))


def _unit(W, col0, ncol=128):
    K = W.shape[0]
    return np.ascontiguousarray(
        W[:, col0:col0 + ncol].reshape(K // 128, 128, ncol).transpose(1, 0, 2)).reshape(128, -1)


def chunk_plan():
    plan = []

    def ffn(l):
        for cc in range(NFC // 2):
            plan.append([('up', l, 2 * cc, 0), ('up', l, 2 * cc, 1), ('up', l, 2 * cc + 1, 0), ('up', l, 2 * cc + 1, 1)])
        for j in range(8):
            plan.append([('down', l, j)])
    plan.append([('pool', 0)])
    ffn(0)
    plan.append([('dn_ab',)])
    for h in range(8):
        plan.append([('dn_u', h * 128), ('dn_u', 1024 + h * 128), ('dn_u', 2048 + h * 128), ('dn_u', 3072 + h * 128)])
    for n in range(2):
        plan.append([('dn_out', 4 * n + i) for i in range(4)])
    ffn(1)
    for n in range(4):
        plan.append([('sb_u', (4 * n + i) * 128) for i in range(4)])
    plan.append([('sb_v', 0)])
    plan.append([('sb_v', 1)])
    for n in range(2):
        plan.append([('sb_out', 4 * n + i) for i in range(4)])
    ffn(2)
    plan.append([('pool', 1)])
    ffn(3)
    return plan


def piece_size(p):
    k = p[0]
    if k == 'pool':
        return 2048
    if k == 'down':
        return DFF
    if k == 'dn_ab':
        return 128
    if k == 'sb_v':
        return 4096
    return 1024


def chunk_offsets():
    plan = chunk_plan()
    offs = []
    o = 0
    for ch in plan:
        used = sum(piece_size(p) for p in ch)
        offs.append((o, used))
        o += used
    tot = ((o + 1023) // 1024) * 1024
    return plan, offs, tot


def piece_data(p, w):
    k = p[0]
    if k == 'pool':
        a = w['pool_w'][p[1]].reshape(4, 2, 128, 2, 128)
        return np.ascontiguousarray(a.transpose(2, 0, 3, 1, 4)).reshape(128, 2048)
    if k == 'up':
        return _unit(w['ffn_w_up'][p[1]], p[3] * DFF + p[2] * 128)
    if k == 'down':
        return _unit(w['ffn_w_down'][p[1]], p[2] * 128)
    if k == 'dn_ab':
        return _unit(w['dn_w_in'][0], 4096, 16)
    if k == 'dn_u':
        return _unit(w['dn_w_in'][0], p[1])
    if k == 'dn_out':
        return _unit(w['dn_w_out'][0], p[1] * 128)
    if k == 'sb_u':
        return _unit(w['sb_w_qkv'][0], p[1])
    if k == 'sb_v':
        return _unit(w['sb_w_qkv'][0], 2048 + p[1] * 512, 512)
    if k == 'sb_out':
        return _unit(w['sb_w_out'][0], p[1] * 128)
    raise KeyError(k)


def pack_weights(w):
    plan, offs, tot = chunk_offsets()
    out = np.zeros((128, tot), np.float32)
    for ch, (o, used) in zip(plan, offs):
        for p in ch:
            n = piece_size(p)
            out[:, o:o + n] = piece_data(p, w)
            o += n
    return out


CV = {}
_o = 0
for _name, _n in [('mixn', 32), ('ffnn', 32), ('finn', 8), ('pscale', 16), ('fcw', 4 * NFC * 3), ('fcb', 4 * NFC),
                  ('dcw', 96), ('dnorm', 1), ('alog', 8), ('dtb', 8)]:
    CV[_name] = _o
    _o += _n
NCV = _o


def pack_cvec(w):
    cv = np.zeros((128, NCV), np.float32)

    def fm(v):
        return np.asarray(v, np.float32).reshape(-1, 128).T
    for l in range(4):
        cv[:, CV['mixn'] + 8 * l: CV['mixn'] + 8 * l + 8] = fm(w['mix_norm'][l])
        cv[:, CV['ffnn'] + 8 * l: CV['ffnn'] + 8 * l + 8] = fm(w['ffn_norm'][l])
        a = np.asarray(w['ffn_conv_w'][l], np.float32)
        a = a.reshape(3, NFC, 128).transpose(2, 1, 0)
        cv[:, CV['fcw'] + l * NFC * 3: CV['fcw'] + (l + 1) * NFC * 3] = a.reshape(128, NFC * 3)
        cv[:, CV['fcb'] + l * NFC: CV['fcb'] + (l + 1) * NFC] = fm(w['ffn_conv_b'][l])
    cv[:, CV['finn']:CV['finn'] + 8] = fm(w['final_norm'])
    for j in range(2):
        cv[:, CV['pscale'] + 8 * j: CV['pscale'] + 8 * j + 8] = fm(w['pool_scale'][j])
    a = np.asarray(w['dn_conv_w'][0], np.float32).reshape(4, 24, 128).transpose(2, 1, 0)
    cv[:, CV['dcw']:CV['dcw'] + 96] = a.reshape(128, 96)
    cv[:, CV['dnorm']] = np.asarray(w['dn_norm'][0], np.float32)
    cv[:, CV['alog']:CV['alog'] + 8] = np.asarray(w['dn_a_log'][0], np.float32)[None, :]
    cv[:, CV['dtb']:CV['dtb'] + 8] = np.asarray(w['dn_dt_bias'][0], np.float32)[None, :]
    return cv


def build_program(n_tiles=8, do_sample=True, layers=None):
    nc = bass.Bass("TRN2", target_bir_lowering=False)
    S = Sched()
    plan, offs, WTOT = chunk_offsets()
    NCH = len(plan)

    def din(name, shape):
        return nc.dram_tensor(name, shape, F32, kind="ExternalInput").ap()

    def dout(name, shape):
        return nc.dram_tensor(name, shape, F32, kind="ExternalOutput").ap()

    xTp = din("xTp", [D, SEQ])
    xTs = din("xTs", [D, DSEQ])
    wpack = din("wpack", [128, WTOT])
    cvec_d = din("cvec", [128, NCV])
    st_pool = din("st_pool", [2, 128, 8, 15])
    st_dnc = din("st_dnc", [128, 24, 3])
    st_dn = din("st_dn", [128, 8, 128])
    st_ffn = din("st_ffn", [4, 128, NFC, 2])
    cacheKT = din("cacheKT", [D, PAST])
    cacheV = din("cacheV", [8, 128, 16, 128])

    yT = {'p': dout("yTp", [D, SEQ]), 's': dout("yTs", [D, DSEQ])}
    o_pool = {'p': dout("o_poolp", [2, 128, 8, 15]), 's': dout("o_pools", [2, 128, 8, 15])}
    o_dnc = {'p': dout("o_dncp", [128, 24, 3]), 's': dout("o_dncs", [128, 24, 3])}
    o_dn = {'p': dout("o_dnp", [128, 8, 128]), 's': dout("o_dns", [128, 8, 128])}
    kTo = {'p': dout("kTp", [D, SEQ]), 's': dout("kTs", [D, DSEQ])}
    vo = {'p': dout("vp", [SEQ, D]), 's': dout("vs", [DSEQ, D])}
    o_ffn = {'p': dout("o_ffnp", [4, 128, NFC, 2]), 's': dout("o_ffns", [4, 128, NFC, 2])}

    wbf = nc.dram_tensor("wbf", [128, WTOT], BF16).ap()
    KTs = nc.dram_tensor("KTs", [8, 128, SEQ], BF16).ap()
    Vs = nc.dram_tensor("Vs", [8, 128, 32, 128], BF16).ap()

    def sb(name, shape, dt):
        return nc.alloc_sbuf_tensor(name, shape, dt).ap()
    xT = sb("xT", [128, 8, 512], F32)
    hT = sb("hT", [128, 8, 512], BF16)
    wring = sb("wring", [128, NSLOT, SLOT], BF16)
    cv = sb("cv", [128, NCV], F32)
    ident = sb("ident", [128, 128], F32)
    ones_f = sb("ones_f", [128, 128], F32)
    minclT = sb("minclT", [128, 128], F32)
    nminclT = sb("nminclT", [128, 128], F32)
    mstrict = sb("mstrict", [128, 128], F32)
    blockones = sb("blockones", [128, 128], F32)
    ones_b = sb("ones_b", [128, 128], BF16)
    nones_b = sb("nones_b", [128, 128], BF16)
    negUI = sb("negUI", [128, 128], BF16)
    mask_lt = sb("mask_lt", [128, 128], BF16)
    invc = sb("invc", [128, 4, 15], F32)
    nexpA = sb("nexpA", [128, 8], F32)
    poolhist = sb("poolhist", [128, 2, 8, 15], F32)
    ghist = sb("ghist", [128, 4, NFC, 2], F32)
    chist = sb("chist", [128, 24, 3], F32)
    Sdn = sb("Sdn", [128, 8, 128], F32)
    Sdb = sb("Sdb", [128, 8, 128], BF16)
    nsq = sb("nsq", [128, 2, 512], BF16)
    nrt = sb("nrt", [128, 512], F32)
    nrstd = sb("nrstd", [128, 512], F32)
    ARENA = 30 * 1024
    arena = sb("arena", [128, ARENA], F32)
    psb = [nc.alloc_psum_tensor("ps%d" % i, [128, 512], F32).ap() for i in range(8)]

    def MM(out, lhsT, rhs, start=True, stop=True, rd=(), wr=()):
        S.emit('pe', lambda e: e.matmul(out, lhsT=lhsT, rhs=rhs, start=start, stop=stop), rd, wr)
        if lhsT.dtype == F32:
            S.tags['pe'].append(S.phase)

    def TR(out, in_, idn, rd=(), wr=()):
        S.emit('pe', lambda e: e.transpose(out, in_, idn), rd, wr)

    def ACT(out, in_, func, rd=(), wr=(), bias=None, scale=None):
        kw = {}
        if bias is not None:
            kw['bias'] = bias
        if scale is not None:
            kw['scale'] = scale
        S.emit('act', lambda e: e.activation(out=out, in_=in_, func=func, **kw), rd, wr)

    def TT(eng, out, in0, in1, op, rd=(), wr=()):
        S.emit(eng, lambda e: e.tensor_tensor(out=out, in0=in0, in1=in1, op=op), rd, wr)

    def TS(eng, out, in0, s1, s2, op0, op1=None, rd=(), wr=()):
        if op1 is None:
            S.emit(eng, lambda e: e.tensor_scalar(out=out, in0=in0, scalar1=s1, scalar2=None, op0=op0), rd, wr)
        else:
            S.emit(eng, lambda e: e.tensor_scalar(out=out, in0=in0, scalar1=s1, scalar2=s2, op0=op0, op1=op1), rd, wr)

    def STT(out, in0, sc, in1, op0, op1, rd=(), wr=()):
        S.emit('dve', lambda e: e.scalar_tensor_tensor(out=out, in0=in0, scalar=sc, in1=in1, op0=op0, op1=op1), rd, wr)

    def CP(eng, out, in_, rd=(), wr=()):
        if eng == 'act':
            S.emit('act', lambda e: e.copy(out=out, in_=in_), rd, wr)
        else:
            S.emit(eng, lambda e: e.tensor_copy(out=out, in_=in_), rd, wr)

    def MS(eng, ap, val, wr=()):
        S.emit(eng, lambda e: e.memset(ap, val), (), wr)

    def RCP(out, in_, rd=(), wr=()):
        S.emit('dve', lambda e: e.reciprocal(out=out, in_=in_), rd, wr)

    def DMA(eng, out, in_, rd=(), wr=(), lane=None, same_gen=False):
        S.emit(eng, lambda e: e.dma_start(out=out, in_=in_), rd, wr, lane=lane, same_gen=same_gen)

    def ASEL(out, in_, pattern, cmp, fill, base, cm, rd=(), wr=()):
        S.emit('pool', lambda e: e.affine_select(out=out, in_=in_, pattern=pattern, compare_op=cmp, fill=fill,
                                                 base=base, channel_multiplier=cm), rd, wr)

    class Arena:
        def __init__(self):
            self.off = 0

        def reset(self):
            self.off = 0

        def alloc(self, shape, dt):
            n = int(np.prod(shape))
            n4 = n if dt == F32 else (n + 1) // 2
            n4 = (n4 + 7) // 8 * 8
            assert self.off + n4 <= ARENA, ("arena overflow", self.off, n4)
            v = arena[:, self.off:self.off + n4]
            self.off += n4
            if dt != F32:
                v = v.bitcast(dt)
            v = v[:, 0:n]
            if len(shape) == 2:
                return v.rearrange("p (a b) -> p a b", a=shape[0])
            if len(shape) == 3:
                return v.rearrange("p (a b c) -> p a b c", a=shape[0], b=shape[1])
            return v
    AR = Arena()

    class Rot:
        cnt = [0]

        def __init__(self, n, shape, dt):
            Rot.cnt[0] += 1
            self.id = Rot.cnt[0]
            self.bufs = [AR.alloc(shape, dt) for _ in range(n)]
            self.i = 0

        def next(self):
            k = self.i % len(self.bufs)
            self.i += 1
            return self.bufs[k], ('rot', self.id, k)

    class PSRot:
        def __init__(self, banks):
            self.banks = banks
            self.i = 0

        def next(self):
            b = self.banks[self.i % len(self.banks)]
            self.i += 1
            return psb[b], ('ps', b)
    PSA = PSRot([0, 1])
    PSB = PSRot([2, 3, 4, 5])

    class PSQ:
        def __init__(self):
            self.i = 0
            self.banks = [6, 7]

        def next(self):
            b = self.banks[self.i % len(self.banks)]
            self.i += 1
            return psb[b][:, 0:128], ('ps', b)
    PQ = PSQ()

    passes = n_tiles + (1 if do_sample else 0)
    CASTW = 32 * 1024
    ncast = (WTOT + CASTW - 1) // CASTW

    class WStream:
        def __init__(self):
            self.seq = [i % NCH for i in range(passes * NCH)]
            self.nl = 0
            self.na = 0

        def _load(self):
            if self.nl >= len(self.seq):
                return
            ci = self.seq[self.nl]
            s = self.nl % NSLOT
            off, used = offs[ci]
            rd = [('wbf', k) for k in range(off // CASTW, (off + used - 1) // CASTW + 1)]
            DMA('sp', wring[:, s, 0:used], wbf[:, off:off + used], rd=rd, wr=[('w', s)], lane=('w', s))
            self.nl += 1

        def prime(self):
            for _ in range(NSLOT):
                self._load()

        def acquire(self, expect):
            ci = self.seq[self.na]
            assert expect is None or plan[ci][0][0] == expect, (plan[ci], expect)
            s = self.na % NSLOT
            self.na += 1
            return wring[:, s, :], ('w', s)

        def release(self):
            self._load()
    W = WStream()

    DMA('sp', cv, cvec_d, wr=['cv'], lane='cv')
    for k in range(ncast):
        c0 = k * CASTW
        c1 = min(WTOT, c0 + CASTW)
        DMA('pool', wbf[:, c0:c1].rearrange("p (a b) -> p a b", b=1024),
            wpack[:, c0:c1].rearrange("p (a b) -> p a b", b=1024), wr=[('wbf', k)], lane='cast')
    MS('dve', ident, 0.0, ['ident'])
    ASEL(ident, ident, [[-1, 128]], ALU.not_equal, 1.0, 0, 1, ['ident'], ['ident'])
    MS('dve', ones_f, 1.0, ['ones_f'])
    MS('dve', ones_b, 1.0, ['ones_b'])
    MS('dve', nones_b, -1.0, ['nones_b'])
    MS('dve', mask_lt, 1.0, ['mask_lt'])
    ASEL(mask_lt, mask_lt, [[1, 128]], ALU.is_gt, 0.0, 0, -1, ['mask_lt'], ['mask_lt'])
    MS('dve', negUI, -1.0, ['negUI'])
    ASEL(negUI, negUI, [[-1, 128]], ALU.is_ge, 0.0, 0, 1, ['negUI'], ['negUI'])
    MS('dve', minclT, 1.0, ['minclT'])
    ASEL(minclT, minclT, [[1, 128]], ALU.is_ge, 0.0, 0, -1, ['minclT'], ['minclT'])
    MS('pool', minclT[0:64, 64:128], 0.0, ['minclT'])
    MS('dve', nminclT, -1.0, ['nminclT'])
    ASEL(nminclT, nminclT, [[1, 128]], ALU.is_ge, 0.0, 0, -1, ['nminclT'], ['nminclT'])
    MS('pool', nminclT[0:64, 64:128], 0.0, ['nminclT'])
    MS('dve', mstrict, 1.0, ['mstrict'])
    ASEL(mstrict, mstrict, [[-1, 128]], ALU.is_gt, 0.0, 0, 1, ['mstrict'], ['mstrict'])
    MS('pool', mstrict[64:128, 0:64], 0.0, ['mstrict'])
    MS('dve', blockones, 1.0, ['blockones'])
    MS('pool', blockones[0:64, 64:128], 0.0, ['blockones'])
    MS('pool', blockones[64:128, 0:64], 0.0, ['blockones'])
    for g in range(4):
        w_ = 2 ** (g + 1)
        MS('dve', invc[:, g, :], 1.0 / w_, ['invc'])
        for t in range(w_ - 1):
            MS('dve', invc[:, g, t:t + 1], 1.0 / (t + 1), ['invc'])
    ACT(nexpA, cv[:, CV['alog']:CV['alog'] + 8], AF.Exp, ['cv'], ['nexpA'])
    TS('dve', nexpA, nexpA, -1.0, None, ALU.mult, None, ['nexpA'], ['nexpA'])
    W.prime()
    S.barrier()

    def rr_gen(gens):
        gens = list(gens)
        while gens:
            for g in list(gens):
                try:
                    next(g)
                except StopIteration:
                    gens.remove(g)
            yield

    def run_rr(gens):
        gens = list(gens)
        while gens:
            for g in list(gens):
                try:
                    next(g)
                except StopIteration:
                    gens.remove(g)

    def cvcol(name, idx):
        o = CV[name] + idx
        return cv[:, o:o + 1]

    def rmsnorm(T, gname, gbase, out_fn, out_keys, out_dt_is_f32=False):
        ps, pk = PSA.next()
        for c in range(8):
            b = c % 2
            ACT(nsq[:, b, 0:T], xT[:, c, 0:T], AF.Square, [('xT', c)], [('nsq', b)])
            MM(ps[:, 0:T], ones_b, nsq[:, b, 0:T], c == 0, c == 7, [('nsq', b)], [pk])
        ACT(nrt[:, 0:T], ps[:, 0:T], AF.Ln, [pk], ['nrt'], bias=EPS, scale=1.0 / D)
        ACT(nrstd[:, 0:T], nrt[:, 0:T], AF.Exp, ['nrt'], ['nrstd'], scale=-0.5)
        for c in range(8):
            STT(out_fn(c), xT[:, c, 0:T], cvcol(gname, gbase + c), nrstd[:, 0:T], ALU.mult, ALU.mult,
                [('xT', c), 'nrstd'], [out_keys(c)])

    def norm_to_hT(T, gname, gbase):
        rmsnorm(T, gname, gbase, lambda c: hT[:, c, 0:T], lambda c: ('hT', c))

    HT_ALL = [('hT', c) for c in range(8)]

    def pool_layer(T, j, l, first):
        S.phase = 'pool%d' % l
        S.barrier()
        AR.reset()
        L = 15 + T
        ext = AR.alloc([8, L], F32)
        dT = AR.alloc([8, T], BF16)
        tmps = [[AR.alloc([2, L], F32) for _ in range(2)] for _ in range(4)]
        fix = AR.alloc([8, 15], F32)
        CP('pool', ext[:, :, 0:15], poolhist[:, j, :, :], [('phist', j)], ['ext_h'])
        rmsnorm(T, 'mixn', 8 * l, lambda c: ext[:, c, 15:L], lambda c: ('ext', c))
        CP('pool', poolhist[:, j, :, :], ext[:, :, T:L], ['ext_h'] + [('ext', c) for c in range(8)], [('phist', j)])
        slot, sk = W.acquire('pool')
        for g in range(4):
            eng = 'dve' if g % 2 == 0 else 'pool'
            cs = [2 * g, 2 * g + 1]
            ekeys = ['ext_h', ('ext', cs[0]), ('ext', cs[1])]
            src = ext[:, 2 * g:2 * g + 2, :]
            lo = 0
            rkeys = ekeys
            for st in range(g + 1):
                sh = 2 ** st
                dst = tmps[g][st % 2]
                nlo = lo + sh
                TT(eng, dst[:, :, nlo:L], src[:, :, nlo:L], src[:, :, lo:L - sh], ALU.add, rkeys, [('ptmp', g, st % 2)])
                src = dst
                lo = nlo
                rkeys = [('ptmp', g, st % 2)]
            wdw = 2 ** (g + 1)
            for ci, c in enumerate(cs):
                STT(dT[:, c, :], src[:, ci, 15:L], 1.0 / wdw, ext[:, c, 15:L], ALU.mult, ALU.subtract,
                    rkeys + ekeys, [('dT', c)])
                if first:
                    TT('dve', fix[:, c, :], src[:, ci, 15:30], invc[:, g, :], ALU.mult, rkeys, [('fix', c)])
                    TT('dve', dT[:, c, 0:15], fix[:, c, :], ext[:, c, 15:30], ALU.subtract,
                       [('fix', c)] + ekeys, [('dT', c)])
        for g in range(4):
            for ec in range(2):
                ps, pk = PSA.next()
                for kc in range(2):
                    o = ((g * 2 + ec) * 2 + kc) * 128
                    MM(ps[:, 0:T], slot[:, o:o + 128], dT[:, 2 * g + kc, :], kc == 0, kc == 1, [sk, ('dT', 2 * g + kc)], [pk])
                c = 2 * g + ec
                STT(xT[:, c, 0:T], ps[:, 0:T], cvcol('pscale', 8 * j + c), xT[:, c, 0:T], ALU.mult, ALU.add,
                    [pk, ('xT', c)], [('xT', c)])
        W.release()

    def ffn_layer(T, l):
        S.phase = 'ffn%d' % l
        S.barrier()
        AR.reset()
        PSB.banks = [2, 3, 4, 5, 6, 7]
        actT = AR.alloc([NFC, T], BF16)
        gext = Rot(3, [T + 2], F32)
        tb = Rot(3, [T], F32)
        sb_ = Rot(2, [T], F32)
        norm_to_hT(T, 'ffnn', 8 * l)
        pending = None
        for cc in range(NFC // 2):
            slot, sk = W.acquire('up')
            for ci in range(2):
                c = 2 * cc + ci
                psv, kv = PSB.next()
                psg, kg = PSB.next()
                for kc in range(8):
                    o = (ci * 2) * 1024 + kc * 128
                    MM(psv[:, 0:T], slot[:, o:o + 128], hT[:, kc, 0:T], kc == 0, kc == 7, [sk, ('hT', kc)], [kv])
                for kc in range(8):
                    o = (ci * 2 + 1) * 1024 + kc * 128
                    MM(psg[:, 0:T], slot[:, o:o + 128], hT[:, kc, 0:T], kc == 0, kc == 7, [sk, ('hT', kc)], [kg])
                ge, gk = gext.next()
                CP('pool', ge[:, 0:2], ghist[:, l, c, :], [('ghist', l)], [gk])
                CP('act', ge[:, 2:T + 2], psg[:, 0:T], [kg], [gk])
                CP('pool', ghist[:, l, c, :], ge[:, T:T + 2], [gk], [('ghist', l)])
                t, tk = tb.next()
                wb = CV['fcw'] + (l * NFC + c) * 3
                TS('dve', t, ge[:, 0:T], cv[:, wb:wb + 1], None, ALU.mult, None, [gk], [tk])
                STT(t, ge[:, 1:T + 1], cv[:, wb + 1:wb + 2], t, ALU.mult, ALU.add, [gk, tk], [tk])
                STT(t, ge[:, 2:T + 2], cv[:, wb + 2:wb + 3], t, ALU.mult, ALU.add, [gk, tk], [tk])
                def fin(t=t, tk=tk, c=c, psv=psv, kv=kv):
                    s_, sk2 = sb_.next()
                    ACT(s_, t, AF.Silu, [tk], [sk2], bias=cvcol('fcb', l * NFC + c))
                    TT('dve', actT[:, c, :], s_, psv[:, 0:T], ALU.mult, [sk2, kv], [('actT', c)])
                if pending is not None:
                    pending()
                pending = fin
            W.release()
        pending()
        for j in range(8):
            slot, sk = W.acquire('down')
            ps, pk = PSA.next()
            for c in range(NFC):
                MM(ps[:, 0:T], slot[:, c * 128:(c + 1) * 128], actT[:, c, :], c == 0, c == NFC - 1,
                   [sk, ('actT', c)], [pk])
            TT('dve', xT[:, j, 0:T], ps[:, 0:T], xT[:, j, 0:T], ALU.add, [pk, ('xT', j)], [('xT', j)])
            W.release()

    def dn_layer(T):
        S.phase = 'dn'
        S.barrier()
        AR.reset()
        NB = (T + 127) // 128
        BS = min(T, 128)
        NCHK = T // 64
        PSB.banks = [2, 3]
        PQ.banks = [4, 5, 6, 7]
        norm_to_hT(T, 'mixn', 8)
        ogT = AR.alloc([8, T], BF16)
        ab = AR.alloc([NB, 16], F32)
        gx = AR.alloc([NB, 8], F32)
        g_tm = AR.alloc([NB, 8], F32)
        nbeta = AR.alloc([NB, 8], F32)
        beta = AR.alloc([NB, 8], F32)
        gc_tm = AR.alloc([NB, 8], F32)
        ngc_tm = AR.alloc([NB, 8], F32)
        gl_tm = AR.alloc([NB, 8], F32)
        kgs = AR.alloc([NB, 8], F32)
        kbs = AR.alloc([NB, 8], F32)
        slot, sk = W.acquire('dn_ab')
        for nb in range(NB):
            ps, pk = PQ.next()
            for kc in range(8):
                MM(ps[0:BS, 0:16], hT[:, kc, nb * 128:nb * 128 + BS], slot[:, kc * 16:(kc + 1) * 16], kc == 0, kc == 7,
                   [sk, ('hT', kc)], [pk])
            CP('dve', ab[0:BS, nb, :], ps[0:BS, 0:16], [pk], ['ab'])
        W.release()
        A = slice(0, BS)
        for nb in range(NB):
            TT('dve', gx[A, nb, :], ab[A, nb, 0:8], cv[A, CV['dtb']:CV['dtb'] + 8], ALU.add, ['ab'], ['gx'])
        ACT(gx[A], gx[A], AF.Exp, ['gx'], ['gx'])
        ACT(gx[A], gx[A], AF.Ln, ['gx'], ['gx'], bias=1.0)
        for nb in range(NB):
            TT('dve', g_tm[A, nb, :], gx[A, nb, :], nexpA[A], ALU.mult, ['gx'], ['g_tm'])
        ACT(beta[A], ab[A, :, 8:16], AF.Exp, ['ab'], ['beta'], scale=-1.0)
        TS('dve', beta[A], beta[A], 1.0, None, ALU.add, None, ['beta'], ['beta'])
        RCP(beta[A], beta[A], ['beta'], ['beta'])
        TS('dve', nbeta[A], beta[A], -1.0, None, ALU.mult, None, ['beta'], ['nbeta'])
        for nb in range(NB):
            ps, pk = PQ.next()
            MM(ps[0:BS, 0:8], minclT[0:BS, 0:BS], g_tm[0:BS, nb, :], True, True, ['g_tm'], [pk])
            CP('dve', gc_tm[0:BS, nb, :], ps[0:BS, 0:8], [pk], ['gc_tm'])
            ps2, pk2 = PQ.next()
            MM(ps2[0:BS, 0:8], blockones[0:BS, 0:BS], g_tm[0:BS, nb, :], True, True, ['g_tm'], [pk2])
            CP('dve', gl_tm[0:BS, nb, :], ps2[0:BS, 0:8], [pk2], ['gl_tm'])
        TS('dve', ngc_tm[A], gc_tm[A], -1.0, None, ALU.mult, None, ['gc_tm'], ['ngc_tm'])
        TT('dve', kgs[A], gl_tm[A], gc_tm[A], ALU.subtract, ['gl_tm', 'gc_tm'], ['kgs'])
        ACT(kgs[A], kgs[A], AF.Exp, ['kgs'], ['kgs'])
        ACT(kbs[A], gc_tm[A], AF.Exp, ['gc_tm'], ['kbs'])
        TT('dve', kbs[A], kbs[A], beta[A], ALU.mult, ['kbs', 'beta'], ['kbs'])

        cext = Rot(2, [T + 3], F32)
        tbuf = Rot(2, [T], F32)
        qkvs = [AR.alloc([T], F32) for _ in range(3)]
        sqb = Rot(2, [T], BF16)
        rr = Rot(2, [T], F32)
        kbg = AR.alloc([NB, 128], BF16)
        vb = AR.alloc([NB, 128], BF16)
        vnew = AR.alloc([NB, 128], BF16)
        attnFs = [Rot(2, [128], F32) for _ in range(NB)]
        qnb = AR.alloc([T], BF16)
        knb = AR.alloc([T], BF16)
        Ybs = [Rot(2, [128], BF16) for _ in range(NB)]
        m128s = [Rot(12, [128], F32) for _ in range(NB)]
        mbs = [Rot(12, [128], BF16) for _ in range(NB)]
        og = AR.alloc([T], F32)
        PBUF = [(AR.alloc([T], F32), AR.alloc([NB, 128], BF16), AR.alloc([NB, 128], F32), AR.alloc([T], BF16), AR.alloc([T], BF16),
                 AR.alloc([NB, 128], BF16), AR.alloc([T], F32)) for _ in range(2)]
        PSA.banks = [2, 3]
        kn2 = AR.alloc([T], BF16)

        def front(h):
            p = h % 2
            zs, kgm, u_, wT, qgT, attnT, EG = PBUF[p]
            slot, sk = W.acquire('dn_u')
            for idx in range(3):
                ps, pk = PSB.next()
                for kc in range(8):
                    o = idx * 1024 + kc * 128
                    MM(ps[:, 0:T], slot[:, o:o + 128], hT[:, kc, 0:T], kc == 0, kc == 7, [sk, ('hT', kc)], [pk])
                ce, ck = cext.next()
                ch = idx * 8 + h
                CP('pool', ce[:, 0:3], chist[:, ch, :], ['chist'], [ck])
                CP('act', ce[:, 3:T + 3], ps[:, 0:T], [pk], [ck])
                CP('pool', chist[:, ch, :], ce[:, T:T + 3], [ck], ['chist'])
                t, tk = tbuf.next()
                wb = CV['dcw'] + ch * 4
                TS('dve', t, ce[:, 0:T], cv[:, wb:wb + 1], None, ALU.mult, None, [ck], [tk])
                for tap in range(1, 4):
                    STT(t, ce[:, tap:T + tap], cv[:, wb + tap:wb + tap + 1], t, ALU.mult, ALU.add, [ck, tk], [tk])
                ACT(qkvs[idx], t, AF.Silu, [tk], [('qkv', idx)])
                yield
            ps, pk = PSB.next()
            for kc in range(8):
                o = 3 * 1024 + kc * 128
                MM(ps[:, 0:T], slot[:, o:o + 128], hT[:, kc, 0:T], kc == 0, kc == 7, [sk, ('hT', kc)], [pk])
            ACT(zs, ps[:, 0:T], AF.Silu, [pk], [('zs', p)])
            W.release()
            yield
            for idx in range(2):
                sq, sqk = sqb.next()
                ACT(sq, qkvs[idx], AF.Square, [('qkv', idx)], [sqk])
                ps, pk = PSA.next()
                MM(ps[:, 0:T], ones_b, sq, True, True, [sqk], [pk])
                r, rk = rr.next()
                ACT(r, ps[:, 0:T], AF.Ln, [pk], [rk], bias=EPS)
                ACT(r, r, AF.Exp, [rk], [rk], scale=-0.5)
                STT(qkvs[idx], qkvs[idx], (128.0 ** -0.5) if idx == 0 else 1.0, r, ALU.mult, ALU.mult,
                    [('qkv', idx), rk], [('qkv', idx)])
            yield
            qn, kn, vs_ = qkvs
            CP('pool', kn2, kn, [('qkv', 1)], ['kn2'])
            CP('act', knb, kn, [('qkv', 1)], ['knb'])
            CP('pool', qnb, qn, [('qkv', 0)], ['qnb'])
            def blk_gen(nb, h=h, kn=kn, qn=qn, vs_=vs_):
                m128 = m128s[nb]
                mb = mbs[nb]
                attnF = attnFs[nb]
                Yb = Ybs[nb]
                cs = slice(nb * 128, nb * 128 + BS)
                R = slice(0, BS)
                bcol = lambda tl: tl[0:BS, nb, h:h + 1]
                ps, pk = PQ.next()
                TR(ps[0:BS, :], kn[:, cs], ident, [('qkv', 1), 'ident'], [pk])
                TS('dve', kbg[R, nb, :], ps[0:BS, :], bcol(kbs), None, ALU.mult, None, [pk, 'kbs'], [('kbg', nb)])
                TS('dve', kgm[R, nb, :], ps[0:BS, :], bcol(kgs), None, ALU.mult, None, [pk, 'kgs'], [('kgm', p, nb)])
                ps, pk = PQ.next()
                TR(ps[0:BS, :], vs_[:, cs], ident, [('qkv', 2), 'ident'], [pk])
                TS('dve', vb[R, nb, :], ps[0:BS, :], bcol(beta), None, ALU.mult, None, [pk, 'beta'], [('vb', nb)])
                yield
                gb2, gb2k = m128.next()
                TS('dve', gb2[R, :], ones_f[R, :], bcol(g_tm), None, ALU.mult, None, ['g_tm'], [gb2k])
                pn, pnk = PQ.next()
                MM(pn[R, 0:BS], gb2[R, 0:BS], nminclT[R, 0:BS], True, True, [gb2k], [pnk])
                pg, pgk = PQ.next()
                MM(pg[:, 0:BS], gb2[R, :], minclT[R, 0:BS], True, True, [gb2k], [pgk])
                E, Ek = m128.next()
                TS('dve', E[R, 0:BS], pn[R, 0:BS], bcol(gc_tm), 0.0, ALU.add, ALU.min, [pnk, 'gc_tm'], [Ek])
                ACT(E[R, 0:BS], E[R, 0:BS], AF.Exp, [Ek], [Ek])
                ET, ETk = m128.next()
                TS('dve', ET[R, 0:BS], pg[R, 0:BS], bcol(ngc_tm), 0.0, ALU.add, ALU.min, [pgk, 'ngc_tm'], [ETk])
                ACT(ET[R, 0:BS], ET[R, 0:BS], AF.Exp, [ETk], [ETk])
                ACT(EG[:, cs], pg[:, 0:BS], AF.Exp, [pgk], [('EG', p, nb)])
                yield
                pa, pak = PQ.next()
                MM(pa[R, 0:BS], knb[:, cs], qnb[:, cs], True, True, ['qnb', 'knb'], [pak])
                af, afk = attnF.next()
                TT('dve', af[R, 0:BS], pa[R, 0:BS], ET[R, 0:BS], ALU.mult, [pak, ETk], [afk])
                TT('dve', attnT[R, nb, 0:BS], af[R, 0:BS], minclT[R, 0:BS], ALU.mult, [afk], [('attnT', p, nb)])
                TT('dve', qgT[:, cs], qn[:, cs], EG[:, cs], ALU.mult, [('qkv', 0), ('EG', p, nb)], [('qgT', p, nb)])
                yield
                pgm, pgmk = PQ.next()
                MM(pgm[R, 0:BS], knb[:, cs], kn2[:, cs], True, True, ['knb', 'kn2'], [pgmk])
                Am, Ak = m128.next()
                STT(Am[R, 0:BS], pgm[R, 0:BS], bcol(nbeta), E[R, 0:BS], ALU.mult, ALU.mult, [pgmk, 'nbeta', Ek], [Ak])
                TT('dve', Am[R, 0:BS], Am[R, 0:BS], mstrict[R, 0:BS], ALU.mult, [Ak], [Ak])
                pt, ptk = PQ.next()
                TR(pt[R, 0:BS], Am[R, 0:BS], ident[R, 0:BS], [Ak], [ptk])
                Bm, Bk = m128.next()
                CP('act', Bm[R, 0:BS], pt[R, 0:BS], [ptk], [Bk])
                yield
                QA, QAk = mb.next()
                CP('dve', QA[R, 0:BS], Am[R, 0:BS], [Ak], [QAk])
                QB, QBk = mb.next()
                CP('act', QB[R, 0:BS], Bm[R, 0:BS], [Bk], [QBk])
                Y, Yk = mb.next()
                TT('dve', Y[R, 0:BS], Bm[R, 0:BS], ident[R, 0:BS], ALU.add, [Bk], [Yk])
                for k in range(1, 6):
                    yield
                    if k < 5:
                        p1, p1k = PQ.next()
                        MM(p1[R, 0:BS], QA[R, 0:BS], QB[R, 0:BS], True, True, [QAk, QBk], [p1k])
                        QBn, QBnk = mb.next()
                        CP('act', QBn[R, 0:BS], p1[R, 0:BS], [p1k], [QBnk])
                    p2, p2k = PQ.next()
                    MM(p2[R, 0:BS], QB[R, 0:BS], QA[R, 0:BS], True, True, [QAk, QBk], [p2k])
                    QAn, QAnk = mb.next()
                    CP('act', QAn[R, 0:BS], p2[R, 0:BS], [p2k], [QAnk])
                    p3, p3k = PQ.next()
                    MM(p3[R, 0:BS], QAn[R, 0:BS], Y[R, 0:BS], True, True, [QAnk, Yk], [p3k])
                    Yn, Ynk = mb.next()
                    TT('dve', Yn[R, 0:BS], p3[R, 0:BS], Y[R, 0:BS], ALU.add, [p3k, Yk], [Ynk])
                    Y, Yk = Yn, Ynk
                    QA, QAk = QAn, QAnk
                    if k < 5:
                        QB, QBk = QBn, QBnk
                yield
                WT, WTk = m128.next()
                TT('dve', WT[R, 0:BS], ident[R, 0:BS], Am[R, 0:BS], ALU.subtract, [Ak], [WTk])
                Y0f, Y0k = m128.next()
                CP('dve', Y0f[R, 0:BS], Y[R, 0:BS], [Yk], [Y0k])
                px, pxk = PQ.next()
                TR(px[R, 0:BS], Y0f[R, 0:BS], ident[R, 0:BS], [Y0k], [pxk])
                Xm, Xk = m128.next()
                CP('act', Xm[R, 0:BS], px[R, 0:BS], [pxk], [Xk])
                yield
                pn1, pn1k = PQ.next()
                MM(pn1[R, 0:BS], WT[R, 0:BS], Y0f[R, 0:BS], True, True, [WTk, Y0k], [pn1k])
                Rm, Rmk = m128.next()
                TT('dve', Rm[R, 0:BS], ident[R, 0:BS], pn1[R, 0:BS], ALU.subtract, [pn1k], [Rmk])
                yield
                pn2, pn2k = PQ.next()
                MM(pn2[R, 0:BS], Xm[R, 0:BS], Rm[R, 0:BS], True, True, [Xk, Rmk], [pn2k])
                Y, Yk = m128.next()
                TT('dve', Y[R, 0:BS], pn2[R, 0:BS], Y0f[R, 0:BS], ALU.add, [pn2k, Y0k], [Yk])
                yield
                pu, puk = PQ.next()
                yb, ybk = Yb.next()
                CP('dve', yb[R, 0:BS], Y[R, 0:BS], [Yk], [ybk])
                MM(pu[R, :], yb[R, 0:BS], vb[R, nb, :], True, True, [ybk, ('vb', nb)], [puk])
                CP('act', u_[R, nb, :], pu[R, :], [puk], [('u', p, nb)])
                pw, pwk = PQ.next()
                MM(pw[:, 0:BS], kbg[R, nb, :], yb[R, 0:BS], True, True, [ybk, ('kbg', nb)], [pwk])
                CP('act', wT[:, cs], pw[:, 0:BS], [pwk], [('wT', p, nb)])
            yield
            for _ in rr_gen([blk_gen(nb) for nb in range(NB)]):
                yield

        def rec(h):
            p = h % 2
            zs, kgm, u_, wT, qgT, attnT, EG = PBUF[p]
            po, pok = psb[p], ('ps', p)
            Sh = Sdn[:, h, :]
            Sk = ('Sdn', h)
            Sb = Sdb[:, h, :]
            Sbk = ('Sdb', h)
            CP('act', Sb, Sh, [Sk], [Sbk])
            for ci in range(NCHK):
                nb = ci // 2
                r0 = (ci % 2) * 64
                c0 = ci * 64
                RR = slice(r0, r0 + 64)
                p1, p1k = PQ.next()
                MM(p1[RR, :], wT[:, c0:c0 + 64], Sb, True, True, [('wT', p, nb), Sbk], [p1k])
                TT('dve', vnew[RR, nb, :], u_[RR, nb, :], p1[RR, :], ALU.subtract, [('u', p, nb), p1k], [('vnew', ci)])
                yield
                MM(po[:, c0:c0 + 64], Sb, qgT[:, c0:c0 + 64], True, False, [Sbk, ('qgT', p, nb)], [pok])
                MM(po[:, c0:c0 + 64], vnew[RR, nb, :], attnT[RR, nb, r0:r0 + 64], False, True,
                   [('vnew', ci), ('attnT', p, nb)], [pok])
                p2, p2k = PQ.next()
                MM(p2, kgm[RR, nb, :], vnew[RR, nb, :], True, True, [('kgm', p, nb), ('vnew', ci)], [p2k])
                STT(Sh, Sh, EG[:, c0 + 63:c0 + 64], p2, ALU.mult, ALU.add, [Sk, ('EG', p, nb), p2k], [Sk])
                if ci < NCHK - 1:
                    CP('act', Sb, Sh, [Sk], [Sbk])
                yield
            sq, sqk = sqb.next()
            ACT(sq, po[:, 0:T], AF.Square, [pok], [sqk])
            ps, pk = PSA.next()
            MM(ps[:, 0:T], ones_b, sq, True, True, [sqk], [pk])
            r, rk = rr.next()
            ACT(r, ps[:, 0:T], AF.Ln, [pk], [rk], bias=EPS, scale=1.0 / 128)
            ACT(r, r, AF.Exp, [rk], [rk], scale=-0.5)
            STT(og, po[:, 0:T], cvcol('dnorm', 0), r, ALU.mult, ALU.mult, [pok, rk], ['og'])
            TT('dve', ogT[:, h, :], og, zs, ALU.mult, ['og', ('zs', p)], [('ogT', h)])
        def exhaust(g):
            for _ in g:
                pass
        exhaust(front(0))
        for h in range(8):
            run_rr([rec(h)] + ([front(h + 1)] if h + 1 < 8 else []))
        PSA.banks = [0, 1]
        for n2 in range(2):
            slot, sk = W.acquire('dn_out')
            for i in range(4):
                n = 4 * n2 + i
                ps, pk = PSA.next()
                for hh in range(8):
                    o = i * 1024 + hh * 128
                    MM(ps[:, 0:T], slot[:, o:o + 128], ogT[:, hh, :], hh == 0, hh == 7, [sk, ('ogT', hh)], [pk])
                TT('dve', xT[:, n, 0:T], ps[:, 0:T], xT[:, n, 0:T], ALU.add, [pk, ('xT', n)], [('xT', n)])
            W.release()

    def sb_layer(T, grp, ti):
        S.phase = 'sb'
        S.barrier()
        AR.reset()
        NB = (T + 127) // 128
        BS = min(T, 128)
        t0 = ti * T
        PSB.banks = [2, 3, 4, 5]
        norm_to_hT(T, 'mixn', 16)
        qT = AR.alloc([8, T], BF16)
        kTb = AR.alloc([8, T], BF16)
        oT = AR.alloc([8, T], BF16)
        kst = Rot(2, [T], F32)
        vst = Rot(2, [512], F32)
        vbb = AR.alloc([NB, 1024], BF16)
        if grp == 'p':
            Stot = t0 + T
        else:
            Stot = PAST + T
        NBLK = (Stot + 127) // 128
        KTc = Rot(2, [Stot], BF16)
        Vc = Rot(2, [NBLK, 128], BF16)
        ebuf = Rot(8, [T], F32)
        spbuf = Rot(12, [T], BF16)
        abuf = Rot(8, [T], BF16)
        Sruns = [AR.alloc([T], F32) for _ in range(4)]
        Sbfs = [AR.alloc([T], BF16) for _ in range(4)]
        for n4 in range(4):
            slot, sk = W.acquire('sb_u')
            for i in range(4):
                cidx = 4 * n4 + i
                ps, pk = PSB.next()
                for kc in range(8):
                    o = i * 1024 + kc * 128
                    MM(ps[:, 0:T], slot[:, o:o + 128], hT[:, kc, 0:T], kc == 0, kc == 7, [sk, ('hT', kc)], [pk])
                if cidx < 8:
                    S.emit('act', (lambda o_, i_: (lambda e: e.mul(out=o_, in_=i_, mul=0.125)))(qT[:, cidx, :], ps[:, 0:T]),
                           [pk], [('qT', cidx)])
                else:
                    c = cidx - 8
                    st, stk = kst.next()
                    CP('act', st, ps[:, 0:T], [pk], [stk])
                    DMA('sp', kTo[grp][c * 128:(c + 1) * 128, t0:t0 + T], st, [stk], [], lane=('kst', stk[2]))
                    CP('dve', kTb[:, c, :], st, [stk], [('kTb', c)])
            W.release()
        if grp == 'p':
            DMA('sp', KTs[:, :, t0:t0 + T].rearrange("c p t -> p c t"), kTb, [('kTb', c) for c in range(8)], ['KTs'], lane='kts')
        for half in range(2):
            slot, sk = W.acquire('sb_v')
            for nb in range(NB):
                ps, pk = PSB.next()
                for kc in range(8):
                    MM(ps[0:BS, :], hT[:, kc, nb * 128:nb * 128 + BS], slot[:, kc * 512:(kc + 1) * 512], kc == 0, kc == 7,
                       [sk, ('hT', kc)], [pk])
                st, stk = vst.next()
                CP('act', st[0:BS, :], ps[0:BS, :], [pk], [stk])
                DMA('sp', vo[grp][t0 + nb * 128:t0 + nb * 128 + BS, half * 512:(half + 1) * 512], st[0:BS, :], [stk], [],
                    lane=('vst', stk[2]))
                CP('dve', vbb[0:BS, nb, half * 512:(half + 1) * 512], st[0:BS, :], [stk], [('vbb', nb, half)])
            W.release()
        VBB = [('vbb', nb, half) for nb in range(NB) for half in range(2)]
        if grp == 'p':
            for nb in range(NB):
                DMA('sp', Vs[:, :, t0 // 128 + nb, :].rearrange("c p n -> p c n"),
                    vbb[:, nb, :].rearrange("p (c n) -> p c n", c=8), VBB, ['Vs'], lane='vs', same_gen=(nb > 0))
        PO_BANKS = [0, 0, 1, 1]
        PSB.banks = [2, 3, 4, 5, 6, 7]
        for c0 in range(0, 8, 2):
          gens = []
          for c in (c0, c0 + 1):
            kt, ktk = KTc.next()
            vc, vck = Vc.next()
            bi = ktk[2]
            if grp == 'p':
                DMA('sp', kt[:, 0:Stot], KTs[c, :, 0:Stot], ['KTs'], [ktk], lane=('ktc', bi))
                DMA('sp', vc[:, 0:NBLK, :], Vs[c, :, 0:NBLK, :], ['Vs'], [vck], lane=('vc', bi))
            else:
                DMA('pool', kt[:, 0:PAST].rearrange("p (a b) -> p a b", b=1024), cacheKT[c * 128:(c + 1) * 128, :].rearrange("p (a b) -> p a b", b=1024), [], [ktk], lane=('ktcs', bi))
                CP('dve', kt[:, PAST:PAST + T], kTb[:, c, :], [('kTb', c)], [ktk])
                DMA('pool', vc[:, 0:16, :], cacheV[c], [], [vck], lane=('vcs', bi))
                CP('dve', vc[0:BS, 16, :], vbb[0:BS, 0, c * 128:(c + 1) * 128], VBB, [vck])
            def head_gen(hh, gi, kt=kt, ktk=ktk, vc=vc, vck=vck, c=c):
                P0 = hh * 64
                PR = slice(P0, P0 + 64)
                po, pok = psb[PO_BANKS[gi]], ('ps', PO_BANKS[gi])
                Srun, Sbf = Sruns[gi], Sbfs[gi]
                srk, sbk = ('Srun', gi), ('Sbf', gi)
                MS('pool', Srun, 0.0, [srk])
                MS('pool', Sbf, 0.0, [sbk])
                def stage1(jb):
                    bs = min(128, Stot - jb * 128)
                    if grp == 'p':
                        diag = jb >= t0 // 128
                        qlo = (jb - t0 // 128) * 128 if diag else 0
                    else:
                        diag = jb == NBLK - 1
                        qlo = 0
                    dcols = min(128, T - qlo)
                    QS = slice(qlo, T)
                    B_ = slice(0, bs)
                    pz, pzk = PSB.next()
                    MM(pz[B_, QS], kt[PR, jb * 128:jb * 128 + bs], qT[PR, c, QS], True, True, [ktk, ('qT', c)], [pzk])
                    e_, ek = ebuf.next()
                    ACT(e_[B_, QS], pz[B_, QS], AF.Exp, [pzk], [ek])
                    sp, spk = spbuf.next()
                    ACT(sp[B_, QS], e_[B_, QS], AF.Ln, [ek], [spk], bias=1.0)
                    if diag:
                        TT('dve', sp[B_, qlo:qlo + dcols], sp[B_, qlo:qlo + dcols], mask_lt[B_, 0:dcols], ALU.mult, [spk], [spk])
                    return dict(jb=jb, bs=bs, diag=diag, qlo=qlo, dcols=dcols, QS=QS, B_=B_, sp=sp, spk=spk)

                units = list(range(NBLK - 1, -1, -1))
                ctx = stage1(units[0])
                yield
                for ui in range(len(units)):
                    first = ui == 0
                    nxt = stage1(units[ui + 1]) if ui + 1 < len(units) else None
                    yield
                    jb, bs, diag, qlo, dcols, QS, B_, sp, spk = (ctx[n] for n in ('jb', 'bs', 'diag', 'qlo', 'dcols', 'QS', 'B_', 'sp', 'spk'))
                    pz, pzk = PSB.next()
                    MM(pz[B_, QS], kt[PR, jb * 128:jb * 128 + bs], qT[PR, c, QS], True, False, [ktk, ('qT', c)], [pzk])
                    yield
                    MM(pz[B_, QS], negUI[B_, B_], sp[B_, QS], False, first, [spk], [pzk])
                    if not first:
                        MM(pz[B_, QS], nones_b[:, B_], Sbf[:, QS], False, True, [sbk], [pzk])
                    a_, ak = abuf.next()
                    if diag and qlo > 0:
                        MS('pool', a_[B_, 0:qlo], 0.0, [ak])
                    ACT(a_[B_, QS], pz[B_, QS], AF.Exp, [pzk], [ak])
                    if diag:
                        TT('dve', a_[B_, qlo:qlo + dcols], a_[B_, qlo:qlo + dcols], mask_lt[B_, 0:dcols], ALU.mult, [ak], [ak])
                    if jb > 0:
                        TT('dve', Srun[B_, QS], Srun[B_, QS], sp[B_, QS], ALU.add, [srk, spk], [srk])
                        CP('dve', Sbf[B_, QS], Srun[B_, QS], [srk], [sbk])
                    yield
                    MM(po[PR, 0:T], vc[B_, jb, hh * 64:(hh + 1) * 64], a_[B_, 0:T], first, jb == 0, [vck, ak], [pok])
                    yield
                    ctx = nxt
                CP('act', oT[PR, c, :], po[PR, 0:T], [pok], [('oT', c)])
            gens.append(head_gen(0, len(gens)))
            gens.append(head_gen(1, len(gens)))
          run_rr(gens)
        for n2 in range(2):
            slot, sk = W.acquire('sb_out')
            for i in range(4):
                n = 4 * n2 + i
                ps, pk = PSA.next()
                for cc in range(8):
                    o = i * 1024 + cc * 128
                    MM(ps[:, 0:T], slot[:, o:o + 128], oT[:, cc, :], cc == 0, cc == 7, [sk, ('oT', cc)], [pk])
                TT('dve', xT[:, n, 0:T], ps[:, 0:T], xT[:, n, 0:T], ALU.add, [pk, ('xT', n)], [('xT', n)])
            W.release()

    def final_layer(T, grp, ti):
        S.phase = 'final'
        S.barrier()
        AR.reset()
        t0 = ti * T
        yst = AR.alloc([8, T], F32)
        rmsnorm(T, 'finn', 0, lambda c: yst[:, c, :], lambda c: ('yst', c))
        DMA('sp', yT[grp].rearrange("(c p) t -> p c t", p=128)[:, :, t0:t0 + T], yst, [('yst', c) for c in range(8)], [],
            lane='yst')

    XT_ALL = [('xT', c) for c in range(8)]
    HIST = [('phist', 0), ('phist', 1), 'chist'] + [('Sdn', h) for h in range(8)] + [('ghist', l) for l in range(4)]

    def run_tile(grp, ti, T):
        t0 = ti * T
        src = xTp if grp == 'p' else xTs
        DMA('sp', xT[:, :, 0:T], src.rearrange("(c p) t -> p c t", p=128)[:, :, t0:t0 + T], [], XT_ALL, lane='xin')
        def skip(n):
            for _ in range(n):
                W.acquire(None)
                W.release()
        LY = layers if layers is not None else {'p0', 'f0', 'dn', 'f1', 'sb', 'f2', 'p3', 'f3'}
        fst = grp == 'p' and ti == 0
        pool_layer(T, 0, 0, fst) if 'p0' in LY else skip(1)
        ffn_layer(T, 0) if 'f0' in LY else skip(19)
        dn_layer(T) if 'dn' in LY else skip(11)
        ffn_layer(T, 1) if 'f1' in LY else skip(19)
        sb_layer(T, grp, ti) if 'sb' in LY else skip(8)
        ffn_layer(T, 2) if 'f2' in LY else skip(19)
        pool_layer(T, 1, 3, fst) if 'p3' in LY else skip(1)
        ffn_layer(T, 3) if 'f3' in LY else skip(19)
        final_layer(T, grp, ti)

    def store_states(grp):
        S.barrier()
        DMA('sp', o_pool[grp].rearrange("l p c t -> p l c t"), poolhist, [('phist', 0), ('phist', 1)], [], lane='st0')
        DMA('sp', o_dnc[grp], chist, ['chist'], [], lane='st1')
        DMA('sp', o_dn[grp], Sdn, [('Sdn', h) for h in range(8)], [], lane='st2')
        DMA('sp', o_ffn[grp].rearrange("l p c t -> p l c t"), ghist, [('ghist', l) for l in range(4)], [], lane='st3')

    MS('pool', poolhist, 0.0, [('phist', 0), ('phist', 1)])
    MS('pool', ghist, 0.0, [('ghist', l) for l in range(4)])
    MS('pool', chist, 0.0, ['chist'])
    MS('pool', Sdn, 0.0, [('Sdn', h) for h in range(8)])
    for ti in range(n_tiles):
        run_tile('p', ti, 512)
    store_states('p')
    if do_sample:
        S.barrier()
        DMA('sp', poolhist, st_pool.rearrange("l p c t -> p l c t"), [], [('phist', 0), ('phist', 1)], lane='ld0')
        DMA('sp', chist, st_dnc, [], ['chist'], lane='ld1')
        DMA('sp', Sdn, st_dn, [], [('Sdn', h) for h in range(8)], lane='ld2')
        DMA('sp', ghist, st_ffn.rearrange("l p c t -> p l c t"), [], [('ghist', l) for l in range(4)], lane='ld3')
        run_tile('s', 0, 64)
        store_states('s')
    S.finish()
    S.replay(nc)
    return nc, S


_CACHE = {}
_LAST = None
_DBG = None


def kernel(x_prompt, x_sample, state_pool, state_dn_conv, state_dn, cache_sb_k, cache_sb_v, state_ffn_conv,
           mix_norm, ffn_norm, final_norm, pool_w, pool_scale, dn_w_in, dn_conv_w, dn_a_log, dn_dt_bias,
           dn_norm, dn_w_out, sb_w_qkv, sb_w_out, ffn_w_up, ffn_conv_w, ffn_conv_b, ffn_w_down,
           _n_tiles=8, _do_sample=True, _layers=None):
    f = lambda a: np.asarray(a, np.float32)
    w = dict(mix_norm=f(mix_norm), ffn_norm=f(ffn_norm), final_norm=f(final_norm), pool_w=f(pool_w),
             pool_scale=f(pool_scale), dn_w_in=f(dn_w_in), dn_conv_w=f(dn_conv_w), dn_a_log=f(dn_a_log),
             dn_dt_bias=f(dn_dt_bias), dn_norm=f(dn_norm), dn_w_out=f(dn_w_out), sb_w_qkv=f(sb_w_qkv),
             sb_w_out=f(sb_w_out), ffn_w_up=f(ffn_w_up), ffn_conv_w=f(ffn_conv_w), ffn_conv_b=f(ffn_conv_b),
             ffn_w_down=f(ffn_w_down))
    wpack = pack_weights(w)
    cvec = pack_cvec(w)
    x_prompt = f(x_prompt); x_sample = f(x_sample); state_pool = f(state_pool); state_dn_conv = f(state_dn_conv)
    state_dn = f(state_dn); cache_sb_k = f(cache_sb_k); cache_sb_v = f(cache_sb_v); state_ffn_conv = f(state_ffn_conv)
    B = 8
    in_maps = []
    for b in range(B):
        m = {
            "xTp": np.ascontiguousarray(x_prompt[b].T),
            "xTs": np.ascontiguousarray(x_sample[b].T),
            "wpack": wpack,
            "cvec": cvec,
            "st_pool": np.ascontiguousarray(state_pool[:, b].reshape(2, 15, 8, 128).transpose(0, 3, 2, 1)),
            "st_dnc": np.ascontiguousarray(state_dn_conv[0, b].reshape(3, 24, 128).transpose(2, 1, 0)),
            "st_dn": np.ascontiguousarray(state_dn[0, b].transpose(1, 0, 2)),
            "st_ffn": np.ascontiguousarray(state_ffn_conv[:, b].reshape(4, 2, NFC, 128).transpose(0, 3, 2, 1)),
            "cacheKT": np.ascontiguousarray(cache_sb_k[0, b].reshape(PAST, D).T),
            "cacheV": np.ascontiguousarray(cache_sb_v[0, b].reshape(16, 128, 8, 128).transpose(2, 1, 0, 3)),
        }
        in_maps.append(m)
    key = (_n_tiles, _do_sample, None if _layers is None else tuple(sorted(_layers)))
    if key not in _CACHE:
        _CACHE[key] = build_program(_n_tiles, _do_sample, _layers)[0]
    nc = _CACHE[key]
    res = run_bass_kernel_spmd(nc, in_maps, core_ids=list(range(B)))
    R = res.results
    global _LAST
    _LAST = R

    def st(name, fn):
        return np.stack([fn(np.asarray(R[b][name])) for b in range(B)], axis=0)
    y_p = st("yTp", lambda a: a.T)
    y_s = st("yTs", lambda a: a.T)
    pool_p = st("o_poolp", lambda a: a.transpose(0, 3, 2, 1).reshape(2, 15, D)).transpose(1, 0, 2, 3)
    pool_s = st("o_pools", lambda a: a.transpose(0, 3, 2, 1).reshape(2, 15, D)).transpose(1, 0, 2, 3)
    dnc_p = st("o_dncp", lambda a: a.transpose(2, 1, 0).reshape(3, 3072))[None]
    dnc_s = st("o_dncs", lambda a: a.transpose(2, 1, 0).reshape(3, 3072))[None]
    dn_p = st("o_dnp", lambda a: a.transpose(1, 0, 2))[None]
    dn_s = st("o_dns", lambda a: a.transpose(1, 0, 2))[None]
    k_p = st("kTp", lambda a: a.T.reshape(SEQ, 16, 64))[None]
    k_s = st("kTs", lambda a: a.T.reshape(DSEQ, 16, 64))[None]
    v_p = st("vp", lambda a: a.reshape(SEQ, 16, 64))[None]
    v_s = st("vs", lambda a: a.reshape(DSEQ, 16, 64))[None]
    ffn_p = st("o_ffnp", lambda a: a.transpose(0, 3, 2, 1).reshape(4, 2, DFF)).transpose(1, 0, 2, 3)
    ffn_s = st("o_ffns", lambda a: a.transpose(0, 3, 2, 1).reshape(4, 2, DFF)).transpose(1, 0, 2, 3)
    outs = (y_p, y_s, pool_p, pool_s, dnc_p, dnc_s, dn_p, dn_s, k_p, k_s, v_p, v_s, ffn_p, ffn_s)
    return tuple(np.ascontiguousarray(o, dtype=np.float32) for o in outs)
```

```python
import bisect
import contextlib
import numpy as np
import concourse.bass as bass
import concourse.mybir as mybir
from concourse.bass_utils import run_bass_kernel_spmd

F32 = mybir.dt.float32
BF16 = mybir.dt.bfloat16
AF = mybir.ActivationFunctionType
ALU = mybir.AluOpType

D = 1024
SEQ = 4096
DSEQ = 64
PAST = 2048
DFF = 2816
NFC = 22
EPS = 1e-6
SLOT = 4096
NSLOT = 5
SAME_ENGINE_SYNC = True


class Sched:
    ENGS = ['pe', 'act', 'dve', 'pool', 'sp']

    def __init__(self):
        self.streams = {e: [] for e in self.ENGS}
        self.cnt = {e: 0 for e in self.ENGS}
        self.lane_cnt = {}
        self.seen = {e: {} for e in self.ENGS}
        self.snaps = {}
        self.res = {}
        self.nops = 0
        self.phase = 'setup'
        self.tags = {e: [] for e in self.ENGS}

    def _snap_put(self, key, val, d):
        vals, dicts = self.snaps.setdefault(key, ([], []))
        if dicts and dicts[-1] is d:
            return
        vals.append(val)
        dicts.append(d)

    def _snap_get(self, key, val):
        if key not in self.snaps:
            return None
        vals, dicts = self.snaps[key]
        i = bisect.bisect_right(vals, val) - 1
        return dicts[i] if i >= 0 else None

    def _wait(self, eng, key, val):
        seen = self.seen[eng]
        if seen.get(key, 0) >= val:
            return
        if key == ('eng', eng) and (eng == 'pe' or not SAME_ENGINE_SYNC):
            return
        self.streams[eng].append(('wait', key, val))
        new = dict(seen)
        new[key] = val
        sn = self._snap_get(key, val)
        if sn:
            for k, v in sn.items():
                if new.get(k, 0) < v:
                    new[k] = v
        self.seen[eng] = new

    def emit(self, eng, fn, reads=(), writes=(), lane=None, same_gen=False):
        deps = {}
        for r in reads:
            ent = self.res.get(r)
            if ent and ent[0]:
                k, v = ent[0]
                if deps.get(k, 0) < v:
                    deps[k] = v
            if ent and isinstance(r, tuple) and r[0] == 'ps':
                for k, v in ent[1].items():
                    if k != ('eng', eng) and deps.get(k, 0) < v:
                        deps[k] = v
        for w in writes:
            ent = self.res.get(w)
            if ent:
                if ent[0]:
                    k, v = ent[0]
                    if deps.get(k, 0) < v:
                        deps[k] = v
                for k, v in ent[1].items():
                    if deps.get(k, 0) < v:
                        deps[k] = v
        for k, v in deps.items():
            self._wait(eng, k, v)
        if lane is not None:
            key = ('lane', lane)
            cur = self.lane_cnt.get(lane, 0)
            if cur and not same_gen:
                self._wait(eng, key, cur)
            cur += 16
            self.lane_cnt[lane] = cur
            ref = (key, cur)
            self.streams[eng].append(('dma', fn, lane))
        else:
            key = ('eng', eng)
            self.cnt[eng] += 1
            ref = (key, self.cnt[eng])
            self.streams[eng].append(('op', fn))
            self.tags[eng].append(self.phase)
        self._snap_put(key, ref[1], self.seen[eng])
        for r in reads:
            ent = self.res.setdefault(r, [None, {}])
            if ent[1].get(key, 0) < ref[1]:
                ent[1][key] = ref[1]
        for w in writes:
            self.res[w] = [ref, {}]
        self.nops += 1
        return ref

    def barrier(self):
        for e in self.ENGS:
            for f in self.ENGS:
                if self.cnt[f]:
                    self._wait(e, ('eng', f), self.cnt[f])
            for l, c in self.lane_cnt.items():
                self._wait(e, ('lane', l), c)

    def finish(self):
        for l, c in self.lane_cnt.items():
            self._wait('sp', ('lane', l), c)
        for f in self.ENGS:
            if self.cnt[f]:
                self._wait('sp', ('eng', f), self.cnt[f])

    def replay(self, nc):
        with contextlib.ExitStack() as es:
            sems = {}
            for e in self.ENGS:
                if self.cnt[e]:
                    sems[('eng', e)] = es.enter_context(nc.semaphore("s_" + e))
            for i, l in enumerate(self.lane_cnt):
                sems[('lane', l)] = es.enter_context(nc.semaphore("l_%d" % i))
            block = es.enter_context(nc.Block())

            def run(engname):
                def body(eng):
                    for it in self.streams[engname]:
                        if it[0] == 'wait':
                            eng.wait_ge(sems[it[1]], it[2])
                        elif it[0] == 'op':
                            it[1](eng).then_inc(sems[('eng', engname)], 1)
                        else:
                            it[1](eng).then_inc(sems[('lane', it[2])], 16)
                return body
            block.tensor(run('pe'))
            block.scalar(run('act'))
            block.vector(run('dve'))
            block.gpsimd(run('pool'))
            block.sync(run('sp'))


def _unit(W, col0, ncol=128):
    K = W.shape[0]
    return np.ascontiguousarray(
        W[:, col0:col0 + ncol].reshape(K // 128, 128, ncol).transpose(1, 0, 2)).reshape(128, -1)


def chunk_plan():
    plan = []

    def ffn(l):
        for cc in range(NFC // 2):
            plan.append([('up', l, 2 * cc, 0), ('up', l, 2 * cc, 1), ('up', l, 2 * cc + 1, 0), ('up', l, 2 * cc + 1, 1)])
        for j in range(8):
            plan.append([('down', l, j)])
    plan.append([('pool', 0)])
    ffn(0)
    plan.append([('dn_ab',)])
    for h in range(8):
        plan.append([('dn_u', h * 128), ('dn_u', 1024 + h * 128), ('dn_u', 2048 + h * 128), ('dn_u', 3072 + h * 128)])
    for n in range(2):
        plan.append([('dn_out', 4 * n + i) for i in range(4)])
    ffn(1)
    for n in range(4):
        plan.append([('sb_u', (4 * n + i) * 128) for i in range(4)])
    plan.append([('sb_v', 0)])
    plan.append([('sb_v', 1)])
    for n in range(2):
        plan.append([('sb_out', 4 * n + i) for i in range(4)])
    ffn(2)
    plan.append([('pool', 1)])
    ffn(3)
    return plan


def piece_size(p):
    k = p[0]
    if k == 'pool':
        return 2048
    if k == 'down':
        return DFF
    if k == 'dn_ab':
        return 128
    if k == 'sb_v':
        return 4096
    return 1024


def chunk_offsets():
    plan = chunk_plan()
    offs = []
    o = 0
    for ch in plan:
        used = sum(piece_size(p) for p in ch)
        offs.append((o, used))
        o += used
    tot = ((o + 1023) // 1024) * 1024
    return plan, offs, tot


def piece_data(p, w):
    k = p[0]
    if k == 'pool':
        a = w['pool_w'][p[1]].reshape(4, 2, 128, 2, 128)
        return np.ascontiguousarray(a.transpose(2, 0, 3, 1, 4)).reshape(128, 2048)
    if k == 'up':
        return _unit(w['ffn_w_up'][p[1]], p[3] * DFF + p[2] * 128)
    if k == 'down':
        return _unit(w['ffn_w_down'][p[1]], p[2] * 128)
    if k == 'dn_ab':
        return _unit(w['dn_w_in'][0], 4096, 16)
    if k == 'dn_u':
        return _unit(w['dn_w_in'][0], p[1])
    if k == 'dn_out':
        return _unit(w['dn_w_out'][0], p[1] * 128)
    if k == 'sb_u':
        return _unit(w['sb_w_qkv'][0], p[1])
    if k == 'sb_v':
        return _unit(w['sb_w_qkv'][0], 2048 + p[1] * 512, 512)
    if k == 'sb_out':
        return _unit(w['sb_w_out'][0], p[1] * 128)
    raise KeyError(k)


def pack_weights(w):
    plan, offs, tot = chunk_offsets()
    out = np.zeros((128, tot), np.float32)
    for ch, (o, used) in zip(plan, offs):
        for p in ch:
            n = piece_size(p)
            out[:, o:o + n] = piece_data(p, w)
            o += n
    return out


CV = {}
_o = 0
for _name, _n in [('mixn', 32), ('ffnn', 32), ('finn', 8), ('pscale', 16), ('fcw', 4 * NFC * 3), ('fcb', 4 * NFC),
                  ('dcw', 96), ('dnorm', 1), ('alog', 8), ('dtb', 8)]:
    CV[_name] = _o
    _o += _n
NCV = _o


def pack_cvec(w):
    cv = np.zeros((128, NCV), np.float32)

    def fm(v):
        return np.asarray(v, np.float32).reshape(-1, 128).T
    for l in range(4):
        cv[:, CV['mixn'] + 8 * l: CV['mixn'] + 8 * l + 8] = fm(w['mix_norm'][l])
        cv[:, CV['ffnn'] + 8 * l: CV['ffnn'] + 8 * l + 8] = fm(w['ffn_norm'][l])
        a = np.asarray(w['ffn_conv_w'][l], np.float32)
        a = a.reshape(3, NFC, 128).transpose(2, 1, 0)
        cv[:, CV['fcw'] + l * NFC * 3: CV['fcw'] + (l + 1) * NFC * 3] = a.reshape(128, NFC * 3)
        cv[:, CV['fcb'] + l * NFC: CV['fcb'] + (l + 1) * NFC] = fm(w['ffn_conv_b'][l])
    cv[:, CV['finn']:CV['finn'] + 8] = fm(w['final_norm'])
    for j in range(2):
        cv[:, CV['pscale'] + 8 * j: CV['pscale'] + 8 * j + 8] = fm(w['pool_scale'][j])
    a = np.asarray(w['dn_conv_w'][0], np.float32).reshape(4, 24, 128).transpose(2, 1, 0)
    cv[:, CV['dcw']:CV['dcw'] + 96] = a.reshape(128, 96)
    cv[:, CV['dnorm']] = np.asarray(w['dn_norm'][0], np.float32)
    cv[:, CV['alog']:CV['alog'] + 8] = np.asarray(w['dn_a_log'][0], np.float32)[None, :]
    cv[:, CV['dtb']:CV['dtb'] + 8] = np.asarray(w['dn_dt_bias'][0], np.float32)[None, :]
    return cv


def build_program(n_tiles=8, do_sample=True, layers=None):
    nc = bass.Bass("TRN2", target_bir_lowering=False)
    S = Sched()
    plan, offs, WTOT = chunk_offsets()
    NCH = len(plan)

    def din(name, shape):
        return nc.dram_tensor(name, shape, F32, kind="ExternalInput").ap()

    def dout(name, shape):
        return nc.dram_tensor(name, shape, F32, kind="ExternalOutput").ap()

    xTp = din("xTp", [D, SEQ])
    xTs = din("xTs", [D, DSEQ])
    wpack = din("wpack", [128, WTOT])
    cvec_d = din("cvec", [128, NCV])
    st_pool = din("st_pool", [2, 128, 8, 15])
    st_dnc = din("st_dnc", [128, 24, 3])
    st_dn = din("st_dn", [128, 8, 128])
    st_ffn = din("st_ffn", [4, 128, NFC, 2])
    cacheKT = din("cacheKT", [D, PAST])
    cacheV = din("cacheV", [8, 128, 16, 128])

    yT = {'p': dout("yTp", [D, SEQ]), 's': dout("yTs", [D, DSEQ])}
    o_pool = {'p': dout("o_poolp", [2, 128, 8, 15]), 's': dout("o_pools", [2, 128, 8, 15])}
    o_dnc = {'p': dout("o_dncp", [128, 24, 3]), 's': dout("o_dncs", [128, 24, 3])}
    o_dn = {'p': dout("o_dnp", [128, 8, 128]), 's': dout("o_dns", [128, 8, 128])}
    kTo = {'p': dout("kTp", [D, SEQ]), 's': dout("kTs", [D, DSEQ])}
    vo = {'p': dout("vp", [SEQ, D]), 's': dout("vs", [DSEQ, D])}
    o_ffn = {'p': dout("o_ffnp", [4, 128, NFC, 2]), 's': dout("o_ffns", [4, 128, NFC, 2])}

    wbf = nc.dram_tensor("wbf", [128, WTOT], BF16).ap()
    KTs = nc.dram_tensor("KTs", [8, 128, SEQ], BF16).ap()
    Vs = nc.dram_tensor("Vs", [8, 128, 32, 128], BF16).ap()

    def sb(name, shape, dt):
        return nc.alloc_sbuf_tensor(name, shape, dt).ap()
    xT = sb("xT", [128, 8, 512], F32)
    hT = sb("hT", [128, 8, 512], BF16)
    wring = sb("wring", [128, NSLOT, SLOT], BF16)
    cv = sb("cv", [128, NCV], F32)
    ident = sb("ident", [128, 128], F32)
    ones_f = sb("ones_f", [128, 128], F32)
    minclT = sb("minclT", [128, 128], F32)
    nminclT = sb("nminclT", [128, 128], F32)
    mstrict = sb("mstrict", [128, 128], F32)
    blockones = sb("blockones", [128, 128], F32)
    ones_b = sb("ones_b", [128, 128], BF16)
    nones_b = sb("nones_b", [128, 128], BF16)
    negUI = sb("negUI", [128, 128], BF16)
    mask_lt = sb("mask_lt", [128, 128], BF16)
    invc = sb("invc", [128, 4, 15], F32)
    nexpA = sb("nexpA", [128, 8], F32)
    poolhist = sb("poolhist", [128, 2, 8, 15], F32)
    ghist = sb("ghist", [128, 4, NFC, 2], F32)
    chist = sb("chist", [128, 24, 3], F32)
    Sdn = sb("Sdn", [128, 8, 128], F32)
    Sdb = sb("Sdb", [128, 8, 128], BF16)
    nsq = sb("nsq", [128, 2, 512], BF16)
    nrt = sb("nrt", [128, 512], F32)
    nrstd = sb("nrstd", [128, 512], F32)
    ARENA = 30 * 1024
    arena = sb("arena", [128, ARENA], F32)
    psb = [nc.alloc_psum_tensor("ps%d" % i, [128, 512], F32).ap() for i in range(8)]

    def MM(out, lhsT, rhs, start=True, stop=True, rd=(), wr=()):
        S.emit('pe', lambda e: e.matmul(out, lhsT=lhsT, rhs=rhs, start=start, stop=stop), rd, wr)
        if lhsT.dtype == F32:
            S.tags['pe'].append(S.phase)

    def TR(out, in_, idn, rd=(), wr=()):
        S.emit('pe', lambda e: e.transpose(out, in_, idn), rd, wr)

    def ACT(out, in_, func, rd=(), wr=(), bias=None, scale=None):
        kw = {}
        if bias is not None:
            kw['bias'] = bias
        if scale is not None:
            kw['scale'] = scale
        S.emit('act', lambda e: e.activation(out=out, in_=in_, func=func, **kw), rd, wr)

    def TT(eng, out, in0, in1, op, rd=(), wr=()):
        S.emit(eng, lambda e: e.tensor_tensor(out=out, in0=in0, in1=in1, op=op), rd, wr)

    def TS(eng, out, in0, s1, s2, op0, op1=None, rd=(), wr=()):
        if op1 is None:
            S.emit(eng, lambda e: e.tensor_scalar(out=out, in0=in0, scalar1=s1, scalar2=None, op0=op0), rd, wr)
        else:
            S.emit(eng, lambda e: e.tensor_scalar(out=out, in0=in0, scalar1=s1, scalar2=s2, op0=op0, op1=op1), rd, wr)

    def STT(out, in0, sc, in1, op0, op1, rd=(), wr=()):
        S.emit('dve', lambda e: e.scalar_tensor_tensor(out=out, in0=in0, scalar=sc, in1=in1, op0=op0, op1=op1), rd, wr)

    def CP(eng, out, in_, rd=(), wr=()):
        if eng == 'act':
            S.emit('act', lambda e: e.copy(out=out, in_=in_), rd, wr)
        else:
            S.emit(eng, lambda e: e.tensor_copy(out=out, in_=in_), rd, wr)

    def MS(eng, ap, val, wr=()):
        S.emit(eng, lambda e: e.memset(ap, val), (), wr)

    def RCP(out, in_, rd=(), wr=()):
        S.emit('dve', lambda e: e.reciprocal(out=out, in_=in_), rd, wr)

    def DMA(eng, out, in_, rd=(), wr=(), lane=None, same_gen=False):
        S.emit(eng, lambda e: e.dma_start(out=out, in_=in_), rd, wr, lane=lane, same_gen=same_gen)

    def ASEL(out, in_, pattern, cmp, fill, base, cm, rd=(), wr=()):
        S.emit('pool', lambda e: e.affine_select(out=out, in_=in_, pattern=pattern, compare_op=cmp, fill=fill,
                                                 base=base, channel_multiplier=cm), rd, wr)

    class Arena:
        def __init__(self):
            self.off = 0

        def reset(self):
            self.off = 0

        def alloc(self, shape, dt):
            n = int(np.prod(shape))
            n4 = n if dt == F32 else (n + 1) // 2
            n4 = (n4 + 7) // 8 * 8
            assert self.off + n4 <= ARENA, ("arena overflow", self.off, n4)
            v = arena[:, self.off:self.off + n4]
            self.off += n4
            if dt != F32:
                v = v.bitcast(dt)
            v = v[:, 0:n]
            if len(shape) == 2:
                return v.rearrange("p (a b) -> p a b", a=shape[0])
            if len(shape) == 3:
                return v.rearrange("p (a b c) -> p a b c", a=shape[0], b=shape[1])
            return v
    AR = Arena()

    class Rot:
        cnt = [0]

        def __init__(self, n, shape, dt):
            Rot.cnt[0] += 1
            self.id = Rot.cnt[0]
            self.bufs = [AR.alloc(shape, dt) for _ in range(n)]
            self.i = 0

        def next(self):
            k = self.i % len(self.bufs)
            self.i += 1
            return self.bufs[k], ('rot', self.id, k)

    class PSRot:
        def __init__(self, banks):
            self.banks = banks
            self.i = 0

        def next(self):
            b = self.banks[self.i % len(self.banks)]
            self.i += 1
            return psb[b], ('ps', b)
    PSA = PSRot([0, 1])
    PSB = PSRot([2, 3, 4, 5])

    class PSQ:
        def __init__(self):
            self.i = 0
            self.banks = [6, 7]

        def next(self):
            b = self.banks[self.i % len(self.banks)]
            self.i += 1
            return psb[b][:, 0:128], ('ps', b)
    PQ = PSQ()

    passes = n_tiles + (1 if do_sample else 0)
    CASTW = 32 * 1024
    ncast = (WTOT + CASTW - 1) // CASTW

    class WStream:
        def __init__(self):
            self.seq = [i % NCH for i in range(passes * NCH)]
            self.nl = 0
            self.na = 0
            self.ncast_done = 0

        def ensure_cast(self, upto):
            while self.ncast_done <= upto and self.ncast_done < ncast:
                kk = self.ncast_done
                c0 = kk * CASTW
                c1 = min(WTOT, c0 + CASTW)
                DMA('pool', wbf[:, c0:c1].rearrange("p (a b) -> p a b", b=1024),
                    wpack[:, c0:c1].rearrange("p (a b) -> p a b", b=1024), wr=[('wbf', kk)], lane='cast')
                self.ncast_done += 1

        def _load(self):
            if self.nl >= len(self.seq):
                return
            ci = self.seq[self.nl]
            s = self.nl % NSLOT
            off, used = offs[ci]
            self.ensure_cast((off + used - 1) // CASTW)
            rd = [('wbf', k) for k in range(off // CASTW, (off + used - 1) // CASTW + 1)]
            DMA('sp', wring[:, s, 0:used], wbf[:, off:off + used], rd=rd, wr=[('w', s)], lane=('w', s))
            self.nl += 1

        def prime(self):
            for _ in range(NSLOT):
                self._load()

        def acquire(self, expect):
            ci = self.seq[self.na]
            assert expect is None or plan[ci][0][0] == expect, (plan[ci], expect)
            s = self.na % NSLOT
            self.na += 1
            return wring[:, s, :], ('w', s)

        def release(self):
            self._load()
    W = WStream()

    DMA('sp', cv, cvec_d, wr=['cv'], lane='cv')
    MS('dve', ident, 0.0, ['ident'])
    ASEL(ident, ident, [[-1, 128]], ALU.not_equal, 1.0, 0, 1, ['ident'], ['ident'])
    MS('dve', ones_f, 1.0, ['ones_f'])
    MS('dve', ones_b, 1.0, ['ones_b'])
    MS('dve', nones_b, -1.0, ['nones_b'])
    MS('dve', mask_lt, 1.0, ['mask_lt'])
    ASEL(mask_lt, mask_lt, [[1, 128]], ALU.is_gt, 0.0, 0, -1, ['mask_lt'], ['mask_lt'])
    MS('dve', negUI, -1.0, ['negUI'])
    ASEL(negUI, negUI, [[-1, 128]], ALU.is_ge, 0.0, 0, 1, ['negUI'], ['negUI'])
    MS('dve', minclT, 1.0, ['minclT'])
    ASEL(minclT, minclT, [[1, 128]], ALU.is_ge, 0.0, 0, -1, ['minclT'], ['minclT'])
    MS('pool', minclT[0:64, 64:128], 0.0, ['minclT'])
    MS('dve', nminclT, -1.0, ['nminclT'])
    ASEL(nminclT, nminclT, [[1, 128]], ALU.is_ge, 0.0, 0, -1, ['nminclT'], ['nminclT'])
    MS('pool', nminclT[0:64, 64:128], 0.0, ['nminclT'])
    MS('dve', mstrict, 1.0, ['mstrict'])
    ASEL(mstrict, mstrict, [[-1, 128]], ALU.is_gt, 0.0, 0, 1, ['mstrict'], ['mstrict'])
    MS('pool', mstrict[64:128, 0:64], 0.0, ['mstrict'])
    MS('dve', blockones, 1.0, ['blockones'])
    MS('pool', blockones[0:64, 64:128], 0.0, ['blockones'])
    MS('pool', blockones[64:128, 0:64], 0.0, ['blockones'])
    for g in range(4):
        w_ = 2 ** (g + 1)
        MS('dve', invc[:, g, :], 1.0 / w_, ['invc'])
        for t in range(w_ - 1):
            MS('dve', invc[:, g, t:t + 1], 1.0 / (t + 1), ['invc'])
    ACT(nexpA, cv[:, CV['alog']:CV['alog'] + 8], AF.Exp, ['cv'], ['nexpA'])
    TS('dve', nexpA, nexpA, -1.0, None, ALU.mult, None, ['nexpA'], ['nexpA'])
    W.prime()
    S.barrier()

    def rr_gen(gens):
        gens = list(gens)
        while gens:
            for g in list(gens):
                try:
                    next(g)
                except StopIteration:
                    gens.remove(g)
            yield

    def run_rr(gens):
        gens = list(gens)
        while gens:
            for g in list(gens):
                try:
                    next(g)
                except StopIteration:
                    gens.remove(g)

    def cvcol(name, idx):
        o = CV[name] + idx
        return cv[:, o:o + 1]

    def rmsnorm(T, gname, gbase, out_fn, out_keys, out_dt_is_f32=False):
        ps, pk = PSA.next()
        for c in range(8):
            b = c % 2
            ACT(nsq[:, b, 0:T], xT[:, c, 0:T], AF.Square, [('xT', c)], [('nsq', b)])
            MM(ps[:, 0:T], ones_b, nsq[:, b, 0:T], c == 0, c == 7, [('nsq', b)], [pk])
        ACT(nrt[:, 0:T], ps[:, 0:T], AF.Ln, [pk], ['nrt'], bias=EPS, scale=1.0 / D)
        ACT(nrstd[:, 0:T], nrt[:, 0:T], AF.Exp, ['nrt'], ['nrstd'], scale=-0.5)
        for c in range(8):
            STT(out_fn(c), xT[:, c, 0:T], cvcol(gname, gbase + c), nrstd[:, 0:T], ALU.mult, ALU.mult,
                [('xT', c), 'nrstd'], [out_keys(c)])

    def norm_to_hT(T, gname, gbase):
        rmsnorm(T, gname, gbase, lambda c: hT[:, c, 0:T], lambda c: ('hT', c))

    HT_ALL = [('hT', c) for c in range(8)]

    def pool_layer(T, j, l, first):
        S.phase = 'pool%d' % l
        S.barrier()
        AR.reset()
        L = 15 + T
        ext = AR.alloc([8, L], F32)
        dT = AR.alloc([8, T], BF16)
        tmps = [[AR.alloc([2, L], F32) for _ in range(2)] for _ in range(4)]
        fix = AR.alloc([8, 15], F32)
        CP('pool', ext[:, :, 0:15], poolhist[:, j, :, :], [('phist', j)], ['ext_h'])
        rmsnorm(T, 'mixn', 8 * l, lambda c: ext[:, c, 15:L], lambda c: ('ext', c))
        CP('pool', poolhist[:, j, :, :], ext[:, :, T:L], ['ext_h'] + [('ext', c) for c in range(8)], [('phist', j)])
        slot, sk = W.acquire('pool')
        for g in range(4):
            eng = 'dve' if g % 2 == 0 else 'pool'
            cs = [2 * g, 2 * g + 1]
            ekeys = ['ext_h', ('ext', cs[0]), ('ext', cs[1])]
            src = ext[:, 2 * g:2 * g + 2, :]
            lo = 0
            rkeys = ekeys
            for st in range(g + 1):
                sh = 2 ** st
                dst = tmps[g][st % 2]
                nlo = lo + sh
                TT(eng, dst[:, :, nlo:L], src[:, :, nlo:L], src[:, :, lo:L - sh], ALU.add, rkeys, [('ptmp', g, st % 2)])
                src = dst
                lo = nlo
                rkeys = [('ptmp', g, st % 2)]
            wdw = 2 ** (g + 1)
            for ci, c in enumerate(cs):
                STT(dT[:, c, :], src[:, ci, 15:L], 1.0 / wdw, ext[:, c, 15:L], ALU.mult, ALU.subtract,
                    rkeys + ekeys, [('dT', c)])
                if first:
                    TT('dve', fix[:, c, :], src[:, ci, 15:30], invc[:, g, :], ALU.mult, rkeys, [('fix', c)])
                    TT('dve', dT[:, c, 0:15], fix[:, c, :], ext[:, c, 15:30], ALU.subtract,
                       [('fix', c)] + ekeys, [('dT', c)])
        for g in range(4):
            for ec in range(2):
                ps, pk = PSA.next()
                for kc in range(2):
                    o = ((g * 2 + ec) * 2 + kc) * 128
                    MM(ps[:, 0:T], slot[:, o:o + 128], dT[:, 2 * g + kc, :], kc == 0, kc == 1, [sk, ('dT', 2 * g + kc)], [pk])
                c = 2 * g + ec
                STT(xT[:, c, 0:T], ps[:, 0:T], cvcol('pscale', 8 * j + c), xT[:, c, 0:T], ALU.mult, ALU.add,
                    [pk, ('xT', c)], [('xT', c)])
        W.release()

    def ffn_layer(T, l):
        S.phase = 'ffn%d' % l
        S.barrier()
        AR.reset()
        PSB.banks = [2, 3, 4, 5, 6, 7]
        actT = AR.alloc([NFC, T], BF16)
        gext = Rot(3, [T + 2], F32)
        tb = Rot(3, [T], F32)
        sb_ = Rot(2, [T], F32)
        norm_to_hT(T, 'ffnn', 8 * l)
        for cc in range(NFC // 2):
            slot, sk = W.acquire('up')
            for ci in range(2):
                c = 2 * cc + ci
                psv, kv = PSB.next()
                psg, kg = PSB.next()
                for kc in range(8):
                    o = (ci * 2) * 1024 + kc * 128
                    MM(psv[:, 0:T], slot[:, o:o + 128], hT[:, kc, 0:T], kc == 0, kc == 7, [sk, ('hT', kc)], [kv])
                for kc in range(8):
                    o = (ci * 2 + 1) * 1024 + kc * 128
                    MM(psg[:, 0:T], slot[:, o:o + 128], hT[:, kc, 0:T], kc == 0, kc == 7, [sk, ('hT', kc)], [kg])
                ge, gk = gext.next()
                CP('act', ge[:, 0:2], ghist[:, l, c, :], [('ghist', l)], [gk])
                CP('act', ge[:, 2:T + 2], psg[:, 0:T], [kg], [gk])
                CP('act', ghist[:, l, c, :], ge[:, T:T + 2], [gk], [('ghist', l)])
                t, tk = tb.next()
                wb = CV['fcw'] + (l * NFC + c) * 3
                TS('dve', t, ge[:, 0:T], cv[:, wb:wb + 1], None, ALU.mult, None, [gk], [tk])
                STT(t, ge[:, 1:T + 1], cv[:, wb + 1:wb + 2], t, ALU.mult, ALU.add, [gk, tk], [tk])
                STT(t, ge[:, 2:T + 2], cv[:, wb + 2:wb + 3], t, ALU.mult, ALU.add, [gk, tk], [tk])
                s_, sk2 = sb_.next()
                ACT(s_, t, AF.Silu, [tk], [sk2], bias=cvcol('fcb', l * NFC + c))
                TT('dve', actT[:, c, :], s_, psv[:, 0:T], ALU.mult, [sk2, kv], [('actT', c)])
            W.release()
        for j in range(8):
            slot, sk = W.acquire('down')
            ps, pk = PSA.next()
            for c in range(NFC):
                MM(ps[:, 0:T], slot[:, c * 128:(c + 1) * 128], actT[:, c, :], c == 0, c == NFC - 1,
                   [sk, ('actT', c)], [pk])
            TT('dve', xT[:, j, 0:T], ps[:, 0:T], xT[:, j, 0:T], ALU.add, [pk, ('xT', j)], [('xT', j)])
            W.release()

    def dn_layer(T):
        S.phase = 'dn'
        S.barrier()
        AR.reset()
        NB = (T + 127) // 128
        BS = min(T, 128)
        NCHK = T // 64
        PSB.banks = [2, 3]
        PQ.banks = [4, 5, 6, 7]
        norm_to_hT(T, 'mixn', 8)
        ogT = AR.alloc([8, T], BF16)
        ab = AR.alloc([NB, 16], F32)
        gx = AR.alloc([NB, 8], F32)
        g_tm = AR.alloc([NB, 8], F32)
        nbeta = AR.alloc([NB, 8], F32)
        beta = AR.alloc([NB, 8], F32)
        gc_tm = AR.alloc([NB, 8], F32)
        ngc_tm = AR.alloc([NB, 8], F32)
        gl_tm = AR.alloc([NB, 8], F32)
        kgs = AR.alloc([NB, 8], F32)
        kbs = AR.alloc([NB, 8], F32)
        slot, sk = W.acquire('dn_ab')
        for nb in range(NB):
            ps, pk = PQ.next()
            for kc in range(8):
                MM(ps[0:BS, 0:16], hT[:, kc, nb * 128:nb * 128 + BS], slot[:, kc * 16:(kc + 1) * 16], kc == 0, kc == 7,
                   [sk, ('hT', kc)], [pk])
            CP('dve', ab[0:BS, nb, :], ps[0:BS, 0:16], [pk], ['ab'])
        W.release()
        A = slice(0, BS)
        for nb in range(NB):
            TT('dve', gx[A, nb, :], ab[A, nb, 0:8], cv[A, CV['dtb']:CV['dtb'] + 8], ALU.add, ['ab'], ['gx'])
        ACT(gx[A], gx[A], AF.Exp, ['gx'], ['gx'])
        ACT(gx[A], gx[A], AF.Ln, ['gx'], ['gx'], bias=1.0)
        for nb in range(NB):
            TT('dve', g_tm[A, nb, :], gx[A, nb, :], nexpA[A], ALU.mult, ['gx'], ['g_tm'])
        ACT(beta[A], ab[A, :, 8:16], AF.Exp, ['ab'], ['beta'], scale=-1.0)
        TS('dve', beta[A], beta[A], 1.0, None, ALU.add, None, ['beta'], ['beta'])
        RCP(beta[A], beta[A], ['beta'], ['beta'])
        TS('dve', nbeta[A], beta[A], -1.0, None, ALU.mult, None, ['beta'], ['nbeta'])
        for nb in range(NB):
            ps, pk = PQ.next()
            MM(ps[0:BS, 0:8], minclT[0:BS, 0:BS], g_tm[0:BS, nb, :], True, True, ['g_tm'], [pk])
            CP('dve', gc_tm[0:BS, nb, :], ps[0:BS, 0:8], [pk], ['gc_tm'])
            ps2, pk2 = PQ.next()
            MM(ps2[0:BS, 0:8], blockones[0:BS, 0:BS], g_tm[0:BS, nb, :], True, True, ['g_tm'], [pk2])
            CP('dve', gl_tm[0:BS, nb, :], ps2[0:BS, 0:8], [pk2], ['gl_tm'])
        TS('dve', ngc_tm[A], gc_tm[A], -1.0, None, ALU.mult, None, ['gc_tm'], ['ngc_tm'])
        TT('dve', kgs[A], gl_tm[A], gc_tm[A], ALU.subtract, ['gl_tm', 'gc_tm'], ['kgs'])
        ACT(kgs[A], kgs[A], AF.Exp, ['kgs'], ['kgs'])
        ACT(kbs[A], gc_tm[A], AF.Exp, ['gc_tm'], ['kbs'])
        TT('dve', kbs[A], kbs[A], beta[A], ALU.mult, ['kbs', 'beta'], ['kbs'])

        cext = Rot(2, [T + 3], F32)
        tbuf = Rot(2, [T], F32)
        qkvs = [AR.alloc([T], F32) for _ in range(3)]
        sqb = Rot(2, [T], BF16)
        rr = Rot(2, [T], F32)
        kbg = AR.alloc([NB, 128], BF16)
        vb = AR.alloc([NB, 128], BF16)
        vnew = AR.alloc([NB, 128], BF16)
        attnFs = [Rot(2, [128], F32) for _ in range(NB)]
        qnb = AR.alloc([T], BF16)
        knb = AR.alloc([T], BF16)
        Ybs = [Rot(2, [128], BF16) for _ in range(NB)]
        m128s = [Rot(12, [128], F32) for _ in range(NB)]
        mbs = [Rot(12, [128], BF16) for _ in range(NB)]
        og = AR.alloc([T], F32)
        PBUF = [(AR.alloc([T], F32), AR.alloc([NB, 128], BF16), AR.alloc([NB, 128], F32), AR.alloc([T], BF16), AR.alloc([T], BF16),
                 AR.alloc([NB, 128], BF16), AR.alloc([T], F32)) for _ in range(2)]
        PSA.banks = [2, 3]
        kn2 = AR.alloc([T], BF16)

        def front(h):
            p = h % 2
            zs, kgm, u_, wT, qgT, attnT, EG = PBUF[p]
            slot, sk = W.acquire('dn_u')
            for idx in range(3):
                ps, pk = PSB.next()
                for kc in range(8):
                    o = idx * 1024 + kc * 128
                    MM(ps[:, 0:T], slot[:, o:o + 128], hT[:, kc, 0:T], kc == 0, kc == 7, [sk, ('hT', kc)], [pk])
                ce, ck = cext.next()
                ch = idx * 8 + h
                CP('act', ce[:, 0:3], chist[:, ch, :], ['chist'], [ck])
                CP('act', ce[:, 3:T + 3], ps[:, 0:T], [pk], [ck])
                CP('act', chist[:, ch, :], ce[:, T:T + 3], [ck], ['chist'])
                t, tk = tbuf.next()
                wb = CV['dcw'] + ch * 4
                TS('dve', t, ce[:, 0:T], cv[:, wb:wb + 1], None, ALU.mult, None, [ck], [tk])
                for tap in range(1, 4):
                    STT(t, ce[:, tap:T + tap], cv[:, wb + tap:wb + tap + 1], t, ALU.mult, ALU.add, [ck, tk], [tk])
                ACT(qkvs[idx], t, AF.Silu, [tk], [('qkv', idx)])
                yield
            ps, pk = PSB.next()
            for kc in range(8):
                o = 3 * 1024 + kc * 128
                MM(ps[:, 0:T], slot[:, o:o + 128], hT[:, kc, 0:T], kc == 0, kc == 7, [sk, ('hT', kc)], [pk])
            ACT(zs, ps[:, 0:T], AF.Silu, [pk], [('zs', p)])
            W.release()
            yield
            for idx in range(2):
                sq, sqk = sqb.next()
                ACT(sq, qkvs[idx], AF.Square, [('qkv', idx)], [sqk])
                ps, pk = PSA.next()
                MM(ps[:, 0:T], ones_b, sq, True, True, [sqk], [pk])
                r, rk = rr.next()
                ACT(r, ps[:, 0:T], AF.Ln, [pk], [rk], bias=EPS)
                ACT(r, r, AF.Exp, [rk], [rk], scale=-0.5)
                STT(qkvs[idx], qkvs[idx], (128.0 ** -0.5) if idx == 0 else 1.0, r, ALU.mult, ALU.mult,
                    [('qkv', idx), rk], [('qkv', idx)])
            yield
            qn, kn, vs_ = qkvs
            CP('pool', kn2, kn, [('qkv', 1)], ['kn2'])
            CP('act', knb, kn, [('qkv', 1)], ['knb'])
            CP('pool', qnb, qn, [('qkv', 0)], ['qnb'])
            def blk_gen(nb, h=h, kn=kn, qn=qn, vs_=vs_):
                m128 = m128s[nb]
                mb = mbs[nb]
                attnF = attnFs[nb]
                Yb = Ybs[nb]
                cs = slice(nb * 128, nb * 128 + BS)
                R = slice(0, BS)
                bcol = lambda tl: tl[0:BS, nb, h:h + 1]
                ps, pk = PQ.next()
                TR(ps[0:BS, :], kn[:, cs], ident, [('qkv', 1), 'ident'], [pk])
                TS('dve', kbg[R, nb, :], ps[0:BS, :], bcol(kbs), None, ALU.mult, None, [pk, 'kbs'], [('kbg', nb)])
                TS('dve', kgm[R, nb, :], ps[0:BS, :], bcol(kgs), None, ALU.mult, None, [pk, 'kgs'], [('kgm', p, nb)])
                ps, pk = PQ.next()
                TR(ps[0:BS, :], vs_[:, cs], ident, [('qkv', 2), 'ident'], [pk])
                TS('dve', vb[R, nb, :], ps[0:BS, :], bcol(beta), None, ALU.mult, None, [pk, 'beta'], [('vb', nb)])
                yield
                gb2, gb2k = m128.next()
                TS('dve', gb2[R, :], ones_f[R, :], bcol(g_tm), None, ALU.mult, None, ['g_tm'], [gb2k])
                pn, pnk = PQ.next()
                MM(pn[R, 0:BS], gb2[R, 0:BS], nminclT[R, 0:BS], True, True, [gb2k], [pnk])
                pg, pgk = PQ.next()
                MM(pg[:, 0:BS], gb2[R, :], minclT[R, 0:BS], True, True, [gb2k], [pgk])
                E, Ek = m128.next()
                TS('dve', E[R, 0:BS], pn[R, 0:BS], bcol(gc_tm), 0.0, ALU.add, ALU.min, [pnk, 'gc_tm'], [Ek])
                ACT(E[R, 0:BS], E[R, 0:BS], AF.Exp, [Ek], [Ek])
                ET, ETk = m128.next()
                TS('dve', ET[R, 0:BS], pg[R, 0:BS], bcol(ngc_tm), 0.0, ALU.add, ALU.min, [pgk, 'ngc_tm'], [ETk])
                ACT(ET[R, 0:BS], ET[R, 0:BS], AF.Exp, [ETk], [ETk])
                ACT(EG[:, cs], pg[:, 0:BS], AF.Exp, [pgk], [('EG', p, nb)])
                yield
                pa, pak = PQ.next()
                MM(pa[R, 0:BS], knb[:, cs], qnb[:, cs], True, True, ['qnb', 'knb'], [pak])
                af, afk = attnF.next()
                TT('dve', af[R, 0:BS], pa[R, 0:BS], ET[R, 0:BS], ALU.mult, [pak, ETk], [afk])
                TT('dve', attnT[R, nb, 0:BS], af[R, 0:BS], minclT[R, 0:BS], ALU.mult, [afk], [('attnT', p, nb)])
                TT('dve', qgT[:, cs], qn[:, cs], EG[:, cs], ALU.mult, [('qkv', 0), ('EG', p, nb)], [('qgT', p, nb)])
                yield
                pgm, pgmk = PQ.next()
                MM(pgm[R, 0:BS], knb[:, cs], kn2[:, cs], True, True, ['knb', 'kn2'], [pgmk])
                Am, Ak = m128.next()
                STT(Am[R, 0:BS], pgm[R, 0:BS], bcol(nbeta), E[R, 0:BS], ALU.mult, ALU.mult, [pgmk, 'nbeta', Ek], [Ak])
                TT('dve', Am[R, 0:BS], Am[R, 0:BS], mstrict[R, 0:BS], ALU.mult, [Ak], [Ak])
                pt, ptk = PQ.next()
                TR(pt[R, 0:BS], Am[R, 0:BS], ident[R, 0:BS], [Ak], [ptk])
                Bm, Bk = m128.next()
                CP('act', Bm[R, 0:BS], pt[R, 0:BS], [ptk], [Bk])
                yield
                QA, QAk = mb.next()
                CP('dve', QA[R, 0:BS], Am[R, 0:BS], [Ak], [QAk])
                QB, QBk = mb.next()
                CP('act', QB[R, 0:BS], Bm[R, 0:BS], [Bk], [QBk])
                Y, Yk = mb.next()
                TT('dve', Y[R, 0:BS], Bm[R, 0:BS], ident[R, 0:BS], ALU.add, [Bk], [Yk])
                for k in range(1, 6):
                    yield
                    if k < 5:
                        p1, p1k = PQ.next()
                        MM(p1[R, 0:BS], QA[R, 0:BS], QB[R, 0:BS], True, True, [QAk, QBk], [p1k])
                        QBn, QBnk = mb.next()
                        CP('act', QBn[R, 0:BS], p1[R, 0:BS], [p1k], [QBnk])
                    p2, p2k = PQ.next()
                    MM(p2[R, 0:BS], QB[R, 0:BS], QA[R, 0:BS], True, True, [QAk, QBk], [p2k])
                    QAn, QAnk = mb.next()
                    CP('act', QAn[R, 0:BS], p2[R, 0:BS], [p2k], [QAnk])
                    p3, p3k = PQ.next()
                    MM(p3[R, 0:BS], QAn[R, 0:BS], Y[R, 0:BS], True, True, [QAnk, Yk], [p3k])
                    Yn, Ynk = mb.next()
                    TT('dve', Yn[R, 0:BS], p3[R, 0:BS], Y[R, 0:BS], ALU.add, [p3k, Yk], [Ynk])
                    Y, Yk = Yn, Ynk
                    QA, QAk = QAn, QAnk
                    if k < 5:
                        QB, QBk = QBn, QBnk
                yield
                WT, WTk = m128.next()
                TT('dve', WT[R, 0:BS], ident[R, 0:BS], Am[R, 0:BS], ALU.subtract, [Ak], [WTk])
                Y0f, Y0k = m128.next()
                CP('dve', Y0f[R, 0:BS], Y[R, 0:BS], [Yk], [Y0k])
                px, pxk = PQ.next()
                TR(px[R, 0:BS], Y0f[R, 0:BS], ident[R, 0:BS], [Y0k], [pxk])
                Xm, Xk = m128.next()
                CP('act', Xm[R, 0:BS], px[R, 0:BS], [pxk], [Xk])
                yield
                pn1, pn1k = PQ.next()
                MM(pn1[R, 0:BS], WT[R, 0:BS], Y0f[R, 0:BS], True, True, [WTk, Y0k], [pn1k])
                Rm, Rmk = m128.next()
                TT('dve', Rm[R, 0:BS], ident[R, 0:BS], pn1[R, 0:BS], ALU.subtract, [pn1k], [Rmk])
                yield
                pn2, pn2k = PQ.next()
                MM(pn2[R, 0:BS], Xm[R, 0:BS], Rm[R, 0:BS], True, True, [Xk, Rmk], [pn2k])
                Y, Yk = m128.next()
                TT('dve', Y[R, 0:BS], pn2[R, 0:BS], Y0f[R, 0:BS], ALU.add, [pn2k, Y0k], [Yk])
                yield
                pu, puk = PQ.next()
                yb, ybk = Yb.next()
                CP('dve', yb[R, 0:BS], Y[R, 0:BS], [Yk], [ybk])
                MM(pu[R, :], yb[R, 0:BS], vb[R, nb, :], True, True, [ybk, ('vb', nb)], [puk])
                CP('act', u_[R, nb, :], pu[R, :], [puk], [('u', p, nb)])
                pw, pwk = PQ.next()
                MM(pw[:, 0:BS], kbg[R, nb, :], yb[R, 0:BS], True, True, [ybk, ('kbg', nb)], [pwk])
                CP('act', wT[:, cs], pw[:, 0:BS], [pwk], [('wT', p, nb)])
            yield
            for _ in rr_gen([blk_gen(nb) for nb in range(NB)]):
                yield

        def rec(h):
            p = h % 2
            zs, kgm, u_, wT, qgT, attnT, EG = PBUF[p]
            po, pok = psb[p], ('ps', p)
            Sh = Sdn[:, h, :]
            Sk = ('Sdn', h)
            Sb = Sdb[:, h, :]
            Sbk = ('Sdb', h)
            CP('act', Sb, Sh, [Sk], [Sbk])
            for ci in range(NCHK):
                nb = ci // 2
                r0 = (ci % 2) * 64
                c0 = ci * 64
                RR = slice(r0, r0 + 64)
                p1, p1k = PQ.next()
                MM(p1[RR, :], wT[:, c0:c0 + 64], Sb, True, True, [('wT', p, nb), Sbk], [p1k])
                TT('dve', vnew[RR, nb, :], u_[RR, nb, :], p1[RR, :], ALU.subtract, [('u', p, nb), p1k], [('vnew', ci)])
                yield
                MM(po[:, c0:c0 + 64], Sb, qgT[:, c0:c0 + 64], True, False, [Sbk, ('qgT', p, nb)], [pok])
                MM(po[:, c0:c0 + 64], vnew[RR, nb, :], attnT[RR, nb, r0:r0 + 64], False, True,
                   [('vnew', ci), ('attnT', p, nb)], [pok])
                p2, p2k = PQ.next()
                MM(p2, kgm[RR, nb, :], vnew[RR, nb, :], True, True, [('kgm', p, nb), ('vnew', ci)], [p2k])
                STT(Sh, Sh, EG[:, c0 + 63:c0 + 64], p2, ALU.mult, ALU.add, [Sk, ('EG', p, nb), p2k], [Sk])
                if ci < NCHK - 1:
                    CP('act', Sb, Sh, [Sk], [Sbk])
                yield
            sq, sqk = sqb.next()
            ACT(sq, po[:, 0:T], AF.Square, [pok], [sqk])
            ps, pk = PSA.next()
            MM(ps[:, 0:T], ones_b, sq, True, True, [sqk], [pk])
            r, rk = rr.next()
            ACT(r, ps[:, 0:T], AF.Ln, [pk], [rk], bias=EPS, scale=1.0 / 128)
            ACT(r, r, AF.Exp, [rk], [rk], scale=-0.5)
            STT(og, po[:, 0:T], cvcol('dnorm', 0), r, ALU.mult, ALU.mult, [pok, rk], ['og'])
            TT('dve', ogT[:, h, :], og, zs, ALU.mult, ['og', ('zs', p)], [('ogT', h)])
        def exhaust(g):
            for _ in g:
                pass
        exhaust(front(0))
        for h in range(8):
            run_rr([rec(h)] + ([front(h + 1)] if h + 1 < 8 else []))
        PSA.banks = [0, 1]
        for n2 in range(2):
            slot, sk = W.acquire('dn_out')
            for i in range(4):
                n = 4 * n2 + i
                ps, pk = PSA.next()
                for hh in range(8):
                    o = i * 1024 + hh * 128
                    MM(ps[:, 0:T], slot[:, o:o + 128], ogT[:, hh, :], hh == 0, hh == 7, [sk, ('ogT', hh)], [pk])
                TT('dve', xT[:, n, 0:T], ps[:, 0:T], xT[:, n, 0:T], ALU.add, [pk, ('xT', n)], [('xT', n)])
            W.release()

    def sb_layer(T, grp, ti):
        S.phase = 'sb'
        S.barrier()
        AR.reset()
        NB = (T + 127) // 128
        BS = min(T, 128)
        t0 = ti * T
        PSB.banks = [2, 3, 4, 5]
        norm_to_hT(T, 'mixn', 16)
        qT = AR.alloc([8, T], BF16)
        kTb = AR.alloc([8, T], BF16)
        oT = AR.alloc([8, T], BF16)
        kst = Rot(2, [T], F32)
        vst = Rot(2, [512], F32)
        vbb = AR.alloc([NB, 1024], BF16)
        if grp == 'p':
            Stot = t0 + T
        else:
            Stot = PAST + T
        NBLK = (Stot + 127) // 128
        KTc = Rot(2, [Stot], BF16)
        Vc = Rot(2, [NBLK, 128], BF16)
        ebuf = Rot(8, [T], F32)
        spbuf = Rot(12, [T], BF16)
        abuf = Rot(8, [T], BF16)
        Sruns = [AR.alloc([T], F32) for _ in range(4)]
        Sbfs = [AR.alloc([T], BF16) for _ in range(4)]
        for n4 in range(4):
            slot, sk = W.acquire('sb_u')
            for i in range(4):
                cidx = 4 * n4 + i
                ps, pk = PSB.next()
                for kc in range(8):
                    o = i * 1024 + kc * 128
                    MM(ps[:, 0:T], slot[:, o:o + 128], hT[:, kc, 0:T], kc == 0, kc == 7, [sk, ('hT', kc)], [pk])
                if cidx < 8:
                    S.emit('act', (lambda o_, i_: (lambda e: e.mul(out=o_, in_=i_, mul=0.125)))(qT[:, cidx, :], ps[:, 0:T]),
                           [pk], [('qT', cidx)])
                else:
                    c = cidx - 8
                    st, stk = kst.next()
                    CP('act', st, ps[:, 0:T], [pk], [stk])
                    DMA('sp', kTo[grp][c * 128:(c + 1) * 128, t0:t0 + T], st, [stk], [], lane=('kst', stk[2]))
                    CP('dve', kTb[:, c, :], st, [stk], [('kTb', c)])
            W.release()
        if grp == 'p':
            DMA('sp', KTs[:, :, t0:t0 + T].rearrange("c p t -> p c t"), kTb, [('kTb', c) for c in range(8)], ['KTs'], lane='kts')
        for half in range(2):
            slot, sk = W.acquire('sb_v')
            for nb in range(NB):
                ps, pk = PSB.next()
                for kc in range(8):
                    MM(ps[0:BS, :], hT[:, kc, nb * 128:nb * 128 + BS], slot[:, kc * 512:(kc + 1) * 512], kc == 0, kc == 7,
                       [sk, ('hT', kc)], [pk])
                st, stk = vst.next()
                CP('act', st[0:BS, :], ps[0:BS, :], [pk], [stk])
                DMA('sp', vo[grp][t0 + nb * 128:t0 + nb * 128 + BS, half * 512:(half + 1) * 512], st[0:BS, :], [stk], [],
                    lane=('vst', stk[2]))
                CP('dve', vbb[0:BS, nb, half * 512:(half + 1) * 512], st[0:BS, :], [stk], [('vbb', nb, half)])
            W.release()
        VBB = [('vbb', nb, half) for nb in range(NB) for half in range(2)]
        if grp == 'p':
            for nb in range(NB):
                DMA('sp', Vs[:, :, t0 // 128 + nb, :].rearrange("c p n -> p c n"),
                    vbb[:, nb, :].rearrange("p (c n) -> p c n", c=8), VBB, ['Vs'], lane='vs', same_gen=(nb > 0))
        PO_BANKS = [0, 0, 1, 1]
        PSB.banks = [2, 3, 4, 5, 6, 7]
        for c0 in range(0, 8, 2):
          gens = []
          for c in (c0, c0 + 1):
            kt, ktk = KTc.next()
            vc, vck = Vc.next()
            bi = ktk[2]
            if grp == 'p':
                DMA('sp', kt[:, 0:Stot], KTs[c, :, 0:Stot], ['KTs'], [ktk], lane=('ktc', bi))
                DMA('sp', vc[:, 0:NBLK, :], Vs[c, :, 0:NBLK, :], ['Vs'], [vck], lane=('vc', bi))
            else:
                DMA('pool', kt[:, 0:PAST].rearrange("p (a b) -> p a b", b=1024), cacheKT[c * 128:(c + 1) * 128, :].rearrange("p (a b) -> p a b", b=1024), [], [ktk], lane=('ktcs', bi))
                CP('dve', kt[:, PAST:PAST + T], kTb[:, c, :], [('kTb', c)], [ktk])
                DMA('pool', vc[:, 0:16, :], cacheV[c], [], [vck], lane=('vcs', bi))
                CP('dve', vc[0:BS, 16, :], vbb[0:BS, 0, c * 128:(c + 1) * 128], VBB, [vck])
            def head_gen(hh, gi, kt=kt, ktk=ktk, vc=vc, vck=vck, c=c):
                P0 = hh * 64
                PR = slice(P0, P0 + 64)
                po, pok = psb[PO_BANKS[gi]], ('ps', PO_BANKS[gi])
                Srun, Sbf = Sruns[gi], Sbfs[gi]
                srk, sbk = ('Srun', gi), ('Sbf', gi)
                MS('pool', Srun, 0.0, [srk])
                MS('pool', Sbf, 0.0, [sbk])
                def stage1(jb):
                    bs = min(128, Stot - jb * 128)
                    if grp == 'p':
                        diag = jb >= t0 // 128
                        qlo = (jb - t0 // 128) * 128 if diag else 0
                    else:
                        diag = jb == NBLK - 1
                        qlo = 0
                    dcols = min(128, T - qlo)
                    QS = slice(qlo, T)
                    B_ = slice(0, bs)
                    pz, pzk = PSB.next()
                    MM(pz[B_, QS], kt[PR, jb * 128:jb * 128 + bs], qT[PR, c, QS], True, True, [ktk, ('qT', c)], [pzk])
                    e_, ek = ebuf.next()
                    ACT(e_[B_, QS], pz[B_, QS], AF.Exp, [pzk], [ek])
                    sp, spk = spbuf.next()
                    ACT(sp[B_, QS], e_[B_, QS], AF.Ln, [ek], [spk], bias=1.0)
                    if diag:
                        TT('dve', sp[B_, qlo:qlo + dcols], sp[B_, qlo:qlo + dcols], mask_lt[B_, 0:dcols], ALU.mult, [spk], [spk])
                    return dict(jb=jb, bs=bs, diag=diag, qlo=qlo, dcols=dcols, QS=QS, B_=B_, sp=sp, spk=spk)

                units = list(range(NBLK - 1, -1, -1))
                ctx = stage1(units[0])
                yield
                for ui in range(len(units)):
                    first = ui == 0
                    nxt = stage1(units[ui + 1]) if ui + 1 < len(units) else None
                    yield
                    jb, bs, diag, qlo, dcols, QS, B_, sp, spk = (ctx[n] for n in ('jb', 'bs', 'diag', 'qlo', 'dcols', 'QS', 'B_', 'sp', 'spk'))
                    pz, pzk = PSB.next()
                    MM(pz[B_, QS], kt[PR, jb * 128:jb * 128 + bs], qT[PR, c, QS], True, False, [ktk, ('qT', c)], [pzk])
                    yield
                    MM(pz[B_, QS], negUI[B_, B_], sp[B_, QS], False, first, [spk], [pzk])
                    if not first:
                        MM(pz[B_, QS], nones_b[:, B_], Sbf[:, QS], False, True, [sbk], [pzk])
                    a_, ak = abuf.next()
                    if diag and qlo > 0:
                        MS('pool', a_[B_, 0:qlo], 0.0, [ak])
                    ACT(a_[B_, QS], pz[B_, QS], AF.Exp, [pzk], [ak])
                    if diag:
                        TT('dve', a_[B_, qlo:qlo + dcols], a_[B_, qlo:qlo + dcols], mask_lt[B_, 0:dcols], ALU.mult, [ak], [ak])
                    if jb > 0:
                        TT('dve', Srun[B_, QS], Srun[B_, QS], sp[B_, QS], ALU.add, [srk, spk], [srk])
                        CP('dve', Sbf[B_, QS], Srun[B_, QS], [srk], [sbk])
                    yield
                    MM(po[PR, 0:T], vc[B_, jb, hh * 64:(hh + 1) * 64], a_[B_, 0:T], first, jb == 0, [vck, ak], [pok])
                    yield
                    ctx = nxt
                CP('act', oT[PR, c, :], po[PR, 0:T], [pok], [('oT', c)])
            gens.append(head_gen(0, len(gens)))
            gens.append(head_gen(1, len(gens)))
          run_rr(gens)
        for n2 in range(2):
            slot, sk = W.acquire('sb_out')
            for i in range(4):
                n = 4 * n2 + i
                ps, pk = PSA.next()
                for cc in range(8):
                    o = i * 1024 + cc * 128
                    MM(ps[:, 0:T], slot[:, o:o + 128], oT[:, cc, :], cc == 0, cc == 7, [sk, ('oT', cc)], [pk])
                TT('dve', xT[:, n, 0:T], ps[:, 0:T], xT[:, n, 0:T], ALU.add, [pk, ('xT', n)], [('xT', n)])
            W.release()

    def final_layer(T, grp, ti):
        S.phase = 'final'
        S.barrier()
        AR.reset()
        t0 = ti * T
        yst = AR.alloc([8, T], F32)
        rmsnorm(T, 'finn', 0, lambda c: yst[:, c, :], lambda c: ('yst', c))
        DMA('sp', yT[grp].rearrange("(c p) t -> p c t", p=128)[:, :, t0:t0 + T], yst, [('yst', c) for c in range(8)], [],
            lane='yst')

    XT_ALL = [('xT', c) for c in range(8)]
    HIST = [('phist', 0), ('phist', 1), 'chist'] + [('Sdn', h) for h in range(8)] + [('ghist', l) for l in range(4)]

    def run_tile(grp, ti, T):
        t0 = ti * T
        src = xTp if grp == 'p' else xTs
        DMA('sp', xT[:, :, 0:T], src.rearrange("(c p) t -> p c t", p=128)[:, :, t0:t0 + T], [], XT_ALL, lane='xin')
        def skip(n):
            for _ in range(n):
                W.acquire(None)
                W.release()
        LY = layers if layers is not None else {'p0', 'f0', 'dn', 'f1', 'sb', 'f2', 'p3', 'f3'}
        fst = grp == 'p' and ti == 0
        pool_layer(T, 0, 0, fst) if 'p0' in LY else skip(1)
        ffn_layer(T, 0) if 'f0' in LY else skip(19)
        dn_layer(T) if 'dn' in LY else skip(11)
        ffn_layer(T, 1) if 'f1' in LY else skip(19)
        sb_layer(T, grp, ti) if 'sb' in LY else skip(8)
        ffn_layer(T, 2) if 'f2' in LY else skip(19)
        pool_layer(T, 1, 3, fst) if 'p3' in LY else skip(1)
        ffn_layer(T, 3) if 'f3' in LY else skip(19)
        final_layer(T, grp, ti)

    def store_states(grp):
        S.barrier()
        DMA('sp', o_pool[grp].rearrange("l p c t -> p l c t"), poolhist, [('phist', 0), ('phist', 1)], [], lane='st0')
        DMA('sp', o_dnc[grp], chist, ['chist'], [], lane='st1')
        DMA('sp', o_dn[grp], Sdn, [('Sdn', h) for h in range(8)], [], lane='st2')
        DMA('sp', o_ffn[grp].rearrange("l p c t -> p l c t"), ghist, [('ghist', l) for l in range(4)], [], lane='st3')

    MS('pool', poolhist, 0.0, [('phist', 0), ('phist', 1)])
    MS('pool', ghist, 0.0, [('ghist', l) for l in range(4)])
    MS('pool', chist, 0.0, ['chist'])
    MS('pool', Sdn, 0.0, [('Sdn', h) for h in range(8)])
    for ti in range(n_tiles):
        run_tile('p', ti, 512)
    store_states('p')
    if do_sample:
        S.barrier()
        DMA('sp', poolhist, st_pool.rearrange("l p c t -> p l c t"), [], [('phist', 0), ('phist', 1)], lane='ld0')
        DMA('sp', chist, st_dnc, [], ['chist'], lane='ld1')
        DMA('sp', Sdn, st_dn, [], [('Sdn', h) for h in range(8)], lane='ld2')
        DMA('sp', ghist, st_ffn.rearrange("l p c t -> p l c t"), [], [('ghist', l) for l in range(4)], lane='ld3')
        run_tile('s', 0, 64)
        store_states('s')
    S.finish()
    S.replay(nc)
    return nc, S


_CACHE = {}
_LAST = None
_DBG = None


def kernel(x_prompt, x_sample, state_pool, state_dn_conv, state_dn, cache_sb_k, cache_sb_v, state_ffn_conv,
           mix_norm, ffn_norm, final_norm, pool_w, pool_scale, dn_w_in, dn_conv_w, dn_a_log, dn_dt_bias,
           dn_norm, dn_w_out, sb_w_qkv, sb_w_out, ffn_w_up, ffn_conv_w, ffn_conv_b, ffn_w_down,
           _n_tiles=8, _do_sample=True, _layers=None):
    f = lambda a: np.asarray(a, np.float32)
    w = dict(mix_norm=f(mix_norm), ffn_norm=f(ffn_norm), final_norm=f(final_norm), pool_w=f(pool_w),
             pool_scale=f(pool_scale), dn_w_in=f(dn_w_in), dn_conv_w=f(dn_conv_w), dn_a_log=f(dn_a_log),
             dn_dt_bias=f(dn_dt_bias), dn_norm=f(dn_norm), dn_w_out=f(dn_w_out), sb_w_qkv=f(sb_w_qkv),
             sb_w_out=f(sb_w_out), ffn_w_up=f(ffn_w_up), ffn_conv_w=f(ffn_conv_w), ffn_conv_b=f(ffn_conv_b),
             ffn_w_down=f(ffn_w_down))
    wpack = pack_weights(w)
    cvec = pack_cvec(w)
    x_prompt = f(x_prompt); x_sample = f(x_sample); state_pool = f(state_pool); state_dn_conv = f(state_dn_conv)
    state_dn = f(state_dn); cache_sb_k = f(cache_sb_k); cache_sb_v = f(cache_sb_v); state_ffn_conv = f(state_ffn_conv)
    B = 8
    in_maps = []
    for b in range(B):
        m = {
            "xTp": np.ascontiguousarray(x_prompt[b].T),
            "xTs": np.ascontiguousarray(x_sample[b].T),
            "wpack": wpack,
            "cvec": cvec,
            "st_pool": np.ascontiguousarray(state_pool[:, b].reshape(2, 15, 8, 128).transpose(0, 3, 2, 1)),
            "st_dnc": np.ascontiguousarray(state_dn_conv[0, b].reshape(3, 24, 128).transpose(2, 1, 0)),
            "st_dn": np.ascontiguousarray(state_dn[0, b].transpose(1, 0, 2)),
            "st_ffn": np.ascontiguousarray(state_ffn_conv[:, b].reshape(4, 2, NFC, 128).transpose(0, 3, 2, 1)),
            "cacheKT": np.ascontiguousarray(cache_sb_k[0, b].reshape(PAST, D).T),
            "cacheV": np.ascontiguousarray(cache_sb_v[0, b].reshape(16, 128, 8, 128).transpose(2, 1, 0, 3)),
        }
        in_maps.append(m)
    key = (_n_tiles, _do_sample, None if _layers is None else tuple(sorted(_layers)))
    if key not in _CACHE:
        _CACHE[key] = build_program(_n_tiles, _do_sample, _layers)[0]
    nc = _CACHE[key]
    res = run_bass_kernel_spmd(nc, in_maps, core_ids=list(range(B)))
    R = res.results
    global _LAST
    _LAST = R

    def st(name, fn):
        return np.stack([fn(np.asarray(R[b][name])) for b in range(B)], axis=0)
    y_p = st("yTp", lambda a: a.T)
    y_s = st("yTs", lambda a: a.T)
    pool_p = st("o_poolp", lambda a: a.transpose(0, 3, 2, 1).reshape(2, 15, D)).transpose(1, 0, 2, 3)
    pool_s = st("o_pools", lambda a: a.transpose(0, 3, 2, 1).reshape(2, 15, D)).transpose(1, 0, 2, 3)
    dnc_p = st("o_dncp", lambda a: a.transpose(2, 1, 0).reshape(3, 3072))[None]
    dnc_s = st("o_dncs", lambda a: a.transpose(2, 1, 0).reshape(3, 3072))[None]
    dn_p = st("o_dnp", lambda a: a.transpose(1, 0, 2))[None]
    dn_s = st("o_dns", lambda a: a.transpose(1, 0, 2))[None]
    k_p = st("kTp", lambda a: a.T.reshape(SEQ, 16, 64))[None]
    k_s = st("kTs", lambda a: a.T.reshape(DSEQ, 16, 64))[None]
    v_p = st("vp", lambda a: a.reshape(SEQ, 16, 64))[None]
    v_s = st("vs", lambda a: a.reshape(DSEQ, 16, 64))[None]
    ffn_p = st("o_ffnp", lambda a: a.transpose(0, 3, 2, 1).reshape(4, 2, DFF)).transpose(1, 0, 2, 3)
    ffn_s = st("o_ffns", lambda a: a.transpose(0, 3, 2, 1).reshape(4, 2, DFF)).transpose(1, 0, 2, 3)
    outs = (y_p, y_s, pool_p, pool_s, dnc_p, dnc_s, dn_p, dn_s, k_p, k_s, v_p, v_s, ffn_p, ffn_s)
    return tuple(np.ascontiguousarray(o, dtype=np.float32) for o in outs)
```
